# Optimizing a Trainium2 kernel written in Bass

```python
import math
import jax, jax.numpy as jnp
from jax import lax
import numpy as np


D_MODEL = 1024
BATCH = 16
SEQ = 256
DEPTH = 2
DEC_BATCH = 4
DEC_SEQ = 2048
PAST_LEN = 256

GRID_W = 64
EPS = 1e-6
RET_HEADS = 4
RET_HEAD_DIM = 128
RET_WIDTH = RET_HEADS * RET_HEAD_DIM
RET_CHUNK = 128
ROPE_BASE = 10000.0
HY_WIDTH = 512
HY_ORDER = 2
HY_SHORT = 3
HY_EMB = 33
HY_FFN = 64
HY_SHORT_DECAY_PCT = 0.3
HY_LONG_DECAY_PCT = 1.5
HY_TARGET = 1e-2
LRU_WIDTH = 512
LRU_BLOCKS = 8
LRU_BLOCK_DIM = LRU_WIDTH // LRU_BLOCKS
LRU_CONV = 4
LRU_C = 8.0
D_MIX = RET_WIDTH + HY_WIDTH + LRU_WIDTH
D_IN = 4 * RET_WIDTH + 4 * HY_WIDTH + 2 * LRU_WIDTH

kernel_name = "hybrid_ret_hyena_rglru_diffusion_step"

F32 = jnp.float32


def rmsnorm(x, g):
    xf = x.astype(F32)
    y = xf * lax.rsqrt(jnp.mean(xf * xf, axis=-1, keepdims=True) + EPS)
    return (y * g.astype(F32)).astype(x.dtype)


def grid_rope(rows):
    row = jnp.repeat(jnp.arange(rows, dtype=F32), GRID_W)
    col = jnp.broadcast_to(jnp.arange(GRID_W, dtype=F32)[None, :], (rows, GRID_W)).reshape(-1)
    n_f = RET_HEAD_DIM // 4
    inv = ROPE_BASE ** (-jnp.arange(n_f, dtype=F32) / n_f)
    ang = jnp.concatenate([row[:, None] * inv[None], col[:, None] * inv[None]], axis=-1)
    return jnp.cos(ang), jnp.sin(ang)


def apply_rope(x, cos, sin):
    half = RET_HEAD_DIM // 2
    xa, xb = x[..., :half], x[..., half:]
    c = cos[None, :, None, :]
    s = sin[None, :, None, :]
    return jnp.concatenate([xa * c - xb * s, xa * s + xb * c], axis=-1)


def depthwise_conv(u, w, b, left):
    K, C = w.shape
    y = lax.conv_general_dilated(u, w[:, None, :].astype(u.dtype), (1,), [(left, K - 1 - left)],
                                 dimension_numbers=('NWC', 'WIO', 'NWC'), feature_group_count=C)
    return y + b.astype(u.dtype)


def retention_dir(q, k, v, log_gamma, state0):
    B, L, H, dk = q.shape
    C = RET_CHUNK
    n = L // C
    qc = q.reshape(B, n, C, H, dk)
    kc = k.reshape(B, n, C, H, dk)
    vc = v.reshape(B, n, C, H, -1)
    idx = jnp.arange(C, dtype=F32)
    diff = idx[:, None] - idx[None, :]
    dmat = jnp.where(diff >= 0, jnp.exp(log_gamma[:, None, None] * jnp.maximum(diff, 0.0)[None]), 0.0)
    scores = jnp.einsum('bnihd,bnjhd->bnhij', qc, kc) * dmat[None, None]
    inner = jnp.einsum('bnhij,bnjhe->bnihe', scores, vc)
    zeta = jnp.exp(log_gamma[:, None] * (C - 1 - idx)[None, :])
    kv = jnp.einsum('bnjhd,hj,bnjhe->bnhde', kc, zeta, vc)
    chunk_decay = jnp.exp(log_gamma * C)[None, :, None, None]

    def step(R, kv_i):
        return chunk_decay * R + kv_i, R

    R_final, R_prev = lax.scan(step, state0, jnp.moveaxis(kv, 1, 0))
    R_prev = jnp.moveaxis(R_prev, 0, 1)
    xi = jnp.exp(log_gamma[:, None] * (idx + 1.0)[None, :])
    cross = jnp.einsum('bnihd,hi,bnhde->bnihe', qc, xi, R_prev)
    return (inner + cross).reshape(B, L, H, -1), R_final


def retention_mixer(q, k, v, decay_logit, state0):
    B, L, H, _ = q.shape
    log_g = jax.nn.log_sigmoid(decay_logit.astype(F32))
    s0 = state0.astype(F32)
    of, sf = retention_dir(q, k, v, log_g[0], s0[:, 0])
    ob, sb = retention_dir(q[:, ::-1], k[:, ::-1], v[:, ::-1], log_g[1], s0[:, 1])
    out = of + ob[:, ::-1]
    out = out * lax.rsqrt(jnp.mean(out * out, axis=-1, keepdims=True) + EPS)
    return out.reshape(B, L, RET_WIDTH), jnp.stack([sf, sb], axis=1)


def hyena_filters(L, w1, b1, w2, b2, w3, freq):
    t = jnp.linspace(0.0, 1.0, L, dtype=F32)[:, None]
    n_bands = (HY_EMB - 1) // 2
    f = jnp.linspace(1e-4, n_bands - 1, n_bands, dtype=F32)
    ang = (2.0 * math.pi / L) * jnp.arange(L, dtype=F32)[:, None] * f[None, :]
    z = jnp.concatenate([t, jnp.cos(ang), -jnp.sin(ang)], axis=-1)
    fr = freq.astype(F32)
    hid = jnp.sin(fr[0] * (z @ w1.astype(F32) + b1.astype(F32)))
    hid = jnp.sin(fr[1] * (hid @ w2.astype(F32) + b2.astype(F32)))
    h = (hid @ w3.astype(F32)).reshape(L, HY_ORDER, 2, HY_WIDTH)
    min_decay = math.log(HY_TARGET) / HY_LONG_DECAY_PCT
    max_decay = math.log(HY_TARGET) / HY_SHORT_DECAY_PCT
    deltas = jnp.abs(jnp.linspace(min_decay, max_decay, HY_WIDTH, dtype=F32))
    h = h * jnp.exp(-t * deltas[None, :])[:, None, None, :]
    return h / (jnp.sum(jnp.abs(h), axis=0, keepdims=True) + EPS)


def long_conv_bidir(u, hf, hb, bias):
    L = u.shape[1]
    n = 2 * L
    U = jnp.fft.rfft(u, n=n, axis=1)
    Hs = jnp.fft.rfft(hf, n=n, axis=0) + jnp.conj(jnp.fft.rfft(hb, n=n, axis=0))
    y = jnp.fft.irfft(U * Hs[None], n=n, axis=1)[:, :L]
    return y + u * bias


def hyena_mixer(u3, conv_w, conv_b, filt, bias):
    u = depthwise_conv(u3, conv_w, conv_b, (HY_SHORT - 1) // 2).astype(F32)
    v, x1, x2 = jnp.split(u, 3, axis=-1)
    bias = bias.astype(F32)
    z = x1 * long_conv_bidir(v, filt[:, 0, 0], filt[:, 0, 1], bias[0])
    z = x2 * long_conv_bidir(z, filt[:, 1, 0], filt[:, 1, 1], bias[1])
    return z


def _lru_combine(left, right):
    a1, b1 = left
    a2, b2 = right
    return a1 * a2, a2 * b1 + b2


def rglru_dir(u, gw, gb, lam, h0):
    B, L, W = u.shape
    ub = u.reshape(B, L, LRU_BLOCKS, LRU_BLOCK_DIM)
    g = jnp.einsum('blnd,knde->kblne', ub, gw).reshape(2, B, L, W) + gb[:, None, None, :]
    r = jax.nn.sigmoid(g[0])
    i = jax.nn.sigmoid(g[1])
    log_a = -LRU_C * r * jax.nn.softplus(-lam)[None, None, :]
    a = jnp.exp(log_a)
    b = jnp.sqrt(-jnp.expm1(2.0 * log_a)) * (i * u)
    a_cum, b_cum = lax.associative_scan(_lru_combine, (a, b), axis=1)
    h = b_cum + a_cum * h0[:, None, :]
    return h, h[:, -1]


def rglru_mixer(x_in, conv_w, conv_b, gw, gb, lam, h0):
    u = depthwise_conv(x_in, conv_w, conv_b, LRU_CONV // 2).astype(F32)
    gw = gw.astype(F32)
    gb = gb.astype(F32)
    lam = lam.astype(F32)
    h0 = h0.astype(F32)
    hf, sf = rglru_dir(u, gw[0], gb[0], lam[0], h0[:, 0])
    hb, sb = rglru_dir(u[:, ::-1], gw[1], gb[1], lam[1], h0[:, 1])
    return hf + hb[:, ::-1], jnp.stack([sf, sb], axis=1)


def mixer_layer(x, mod, ret_s0, lru_s0, rope, filt, norm_g, w_in, ret_logit, hy_conv_w, hy_conv_b,
                hy_bias, lru_conv_w, lru_conv_b, lru_gw, lru_gb, lru_lam, w_out):
    B, L, _ = x.shape
    shift, scale, gate = jnp.split(mod, 3, axis=-1)
    h = rmsnorm(x, norm_g) * (1 + scale) + shift
    z = h @ w_in
    o1 = 4 * RET_WIDTH
    o2 = o1 + 4 * HY_WIDTH
    zr, zh, zl = z[..., :o1], z[..., o1:o2], z[..., o2:]
    q, k, v, g_r = jnp.split(zr.astype(F32), 4, axis=-1)
    q = q.reshape(B, L, RET_HEADS, RET_HEAD_DIM)
    k = k.reshape(B, L, RET_HEADS, RET_HEAD_DIM) * (RET_HEAD_DIM ** -0.5)
    v = v.reshape(B, L, RET_HEADS, RET_HEAD_DIM)
    if rope is not None:
        q = apply_rope(q, rope[0], rope[1])
        k = apply_rope(k, rope[0], rope[1])
    ret_out, ret_s = retention_mixer(q, k, v, ret_logit, ret_s0)
    hy_out = hyena_mixer(zh[..., :3 * HY_WIDTH], hy_conv_w, hy_conv_b, filt, hy_bias)
    g_h = zh[..., 3 * HY_WIDTH:].astype(F32)
    lru_out, lru_s = rglru_mixer(zl[..., :LRU_WIDTH], lru_conv_w, lru_conv_b, lru_gw, lru_gb, lru_lam, lru_s0)
    g_l = zl[..., LRU_WIDTH:].astype(F32)
    y = jnp.concatenate([ret_out * jax.nn.silu(g_r), hy_out * jax.nn.silu(g_h),
                         lru_out * jax.nn.silu(g_l)], axis=-1).astype(x.dtype)
    return x + gate * (y @ w_out), ret_s, lru_s


def setup_inputs(seed: int = 0) -> dict:
    key = jax.random.key(seed)
    ks = jax.random.split(key, 32)
    nrm = lambda i, shape: jax.random.normal(ks[i], shape, dtype=F32)
    base_logit = jnp.log(2.0 ** (5.0 + jnp.arange(RET_HEADS, dtype=F32)) - 1.0)
    ret_decay_logit = base_logit[None, None, :] + 0.1 * nrm(10, (DEPTH, 2, RET_HEADS))
    a_target = jax.random.uniform(ks[11], (DEPTH, 2, LRU_WIDTH), dtype=F32, minval=0.9, maxval=0.999)
    sp = -jnp.log(a_target) / LRU_C
    lru_lambda = -jnp.log(jnp.expm1(sp))
    return {
        "x_prompt": nrm(0, (BATCH, SEQ, D_MODEL)),
        "x_sample": nrm(1, (DEC_BATCH, DEC_SEQ, D_MODEL)),
        "state_ret": 0.5 * nrm(2, (DEC_BATCH, DEPTH, 2, RET_HEADS, RET_HEAD_DIM, RET_HEAD_DIM)),
        "state_lru": 0.5 * nrm(3, (DEC_BATCH, DEPTH, 2, LRU_WIDTH)),
        "c": nrm(4, (DEC_BATCH, D_MODEL)),
        "c_ctx": nrm(5, (D_MODEL,)),
        "norm_g": 1.0 + 0.1 * nrm(6, (DEPTH, D_MODEL)),
        "ada_w": 0.5 * D_MODEL ** -0.5 * nrm(7, (DEPTH, D_MODEL, 3 * D_MODEL)),
        "ada_b": 0.02 * nrm(8, (DEPTH, 3 * D_MODEL)),
        "w_in": D_MODEL ** -0.5 * nrm(9, (DEPTH, D_MODEL, D_IN)),
        "ret_decay_logit": ret_decay_logit,
        "hy_conv_w": HY_SHORT ** -0.5 * nrm(12, (DEPTH, HY_SHORT, 3 * HY_WIDTH)),
        "hy_conv_b": 0.02 * nrm(13, (DEPTH, 3 * HY_WIDTH)),
        "hy_ffn_w1": HY_EMB ** -0.5 * nrm(14, (DEPTH, HY_EMB, HY_FFN)),
        "hy_ffn_b1": 0.02 * nrm(15, (DEPTH, HY_FFN)),
        "hy_ffn_w2": HY_FFN ** -0.5 * nrm(16, (DEPTH, HY_FFN, HY_FFN)),
        "hy_ffn_b2": 0.02 * nrm(17, (DEPTH, HY_FFN)),
        "hy_ffn_w3": HY_FFN ** -0.5 * nrm(18, (DEPTH, HY_FFN, HY_ORDER * 2 * HY_WIDTH)),
        "hy_freq": 1.0 + 0.1 * nrm(19, (DEPTH, 2, HY_FFN)),
        "hy_bias": 0.1 * nrm(20, (DEPTH, HY_ORDER, HY_WIDTH)),
        "lru_conv_w": 0.5 * nrm(21, (DEPTH, LRU_CONV, LRU_WIDTH)),
        "lru_conv_b": 0.02 * nrm(22, (DEPTH, LRU_WIDTH)),
        "lru_gate_w": LRU_BLOCK_DIM ** -0.5 * nrm(23, (DEPTH, 2, 2, LRU_BLOCKS, LRU_BLOCK_DIM, LRU_BLOCK_DIM)),
        "lru_gate_b": 0.02 * nrm(24, (DEPTH, 2, 2, LRU_WIDTH)),
        "lru_lambda": lru_lambda,
        "w_out": D_MIX ** -0.5 * nrm(25, (DEPTH, D_MIX, D_MODEL)),
        "final_g": 1.0 + 0.1 * nrm(26, (D_MODEL,)),
    }


def reference(x_prompt, x_sample, state_ret, state_lru, c, c_ctx, norm_g, ada_w, ada_b, w_in,
              ret_decay_logit, hy_conv_w, hy_conv_b, hy_ffn_w1, hy_ffn_b1, hy_ffn_w2, hy_ffn_b2,
              hy_ffn_w3, hy_freq, hy_bias, lru_conv_w, lru_conv_b, lru_gate_w, lru_gate_b,
              lru_lambda, w_out, final_g):
    Bp, Lp, _ = x_prompt.shape
    Bs, Ls, _ = x_sample.shape
    rows = Ls // GRID_W
    rope = grid_rope(rows)
    zero_ret = jnp.zeros((Bp, 2, RET_HEADS, RET_HEAD_DIM, RET_HEAD_DIM), F32)
    zero_lru = jnp.zeros((Bp, 2, LRU_WIDTH), F32)
    xc, xl = x_prompt, x_sample
    new_ret, new_lru = [], []
    for l in range(DEPTH):
        p = (norm_g[l], w_in[l], ret_decay_logit[l], hy_conv_w[l], hy_conv_b[l], hy_bias[l],
             lru_conv_w[l], lru_conv_b[l], lru_gate_w[l], lru_gate_b[l], lru_lambda[l], w_out[l])
        filt_c = hyena_filters(Lp, hy_ffn_w1[l], hy_ffn_b1[l], hy_ffn_w2[l], hy_ffn_b2[l], hy_ffn_w3[l], hy_freq[l])
        filt_l = hyena_filters(Ls, hy_ffn_w1[l], hy_ffn_b1[l], hy_ffn_w2[l], hy_ffn_b2[l], hy_ffn_w3[l], hy_freq[l])
        mod_c = (jax.nn.silu(c_ctx) @ ada_w[l] + ada_b[l])[None, None, :]
        mod_l = (jax.nn.silu(c) @ ada_w[l] + ada_b[l])[:, None, :]
        xc, rs, ls = mixer_layer(xc, mod_c, zero_ret, zero_lru, None, filt_c, *p)
        new_ret.append(rs)
        new_lru.append(ls)
        xl, _, _ = mixer_layer(xl, mod_l, state_ret[:, l], state_lru[:, l], rope, filt_l, *p)
    y_prompt = rmsnorm(xc, final_g)
    y_sample = rmsnorm(xl, final_g)
    new_state_ret = jnp.stack(new_ret, axis=1).astype(x_prompt.dtype)
    new_state_lru = jnp.stack(new_lru, axis=1).astype(x_prompt.dtype)
    return (y_prompt, y_sample, new_state_ret, new_state_lru)
```

```python
import contextlib
import math
import os
import numpy as np
import ml_dtypes
import concourse.bass as bass
import concourse.mybir as mybir
from concourse.bass_utils import run_bass_kernel_spmd
from concourse.ap import AP

F32 = mybir.dt.float32
BF16 = mybir.dt.bfloat16
ALU = mybir.AluOpType
AF = mybir.ActivationFunctionType

EPOCH = 16000
SAME_ENGINE_SYNC = True
EPS = 1e-6
T = 2048
NT = 16
NFS = 4
DEPTH = 2
STAGE = os.environ.get("KSTAGE", "all")


class Res:
    __slots__ = ("name", "w", "r", "dsem", "dcnt", "dkind")

    def __init__(self, name):
        self.name = name
        self.w = None
        self.r = {}
        self.dsem = None
        self.dcnt = 0
        self.dkind = None


class Prog:
    ENG = ("pe", "act", "dve", "pool", "sp")

    def __init__(self, nc, es):
        self.nc = nc
        self.es = es
        self.ops = {e: [] for e in self.ENG}
        self.sem = {}
        self.cnt = {e: 0 for e in self.ENG}
        self.seen = {e: {} for e in self.ENG}
        self.nsem = 0
        self.dres = []
        self.sempool = {"sw": [], "hw": []}
        self.allsems = []
        for e in self.ENG:
            self._new_epoch(e)

    def new_sem(self, name):
        self.nsem += 1
        return self.es.enter_context(self.nc.semaphore("%s_%d" % (name, self.nsem)))

    def _new_epoch(self, e):
        self.sem[e] = self.new_sem("e_" + e)
        self.cnt[e] = 0
        self.allsems.append((e, self.sem[e]))

    def _waits(self, eng, reads, writes):
        need = {}

        def add(ev):
            if ev is None:
                return
            sem, val, e = ev
            if e == eng and (eng == "pe" or not SAME_ENGINE_SYNC):
                return
            k = id(sem)
            if self.seen[eng].get(k, 0) >= val:
                return
            if k not in need or need[k][1] < val:
                need[k] = (sem, val)

        for r in reads:
            add(r.w)
        for w in writes:
            add(w.w)
            for ev in w.r.values():
                add(ev)
        out = list(need.values())
        for sem, val in out:
            self.seen[eng][id(sem)] = val
        return out

    def op(self, eng, fn, reads=(), writes=()):
        waits = self._waits(eng, reads, writes)
        if self.cnt[eng] >= EPOCH:
            self._new_epoch(eng)
        self.cnt[eng] += 1
        ev = (self.sem[eng], self.cnt[eng], eng)
        self.ops[eng].append((waits, fn, (self.sem[eng], 1)))
        for r in reads:
            r.r[id(ev[0])] = ev
        for w in writes:
            w.w = ev
            w.r = {}
        return ev

    def dma(self, eng, fn, reads=(), writes=(), sres=None):
        waits = self._waits(eng, reads, writes)
        kind = "sw" if eng == "pool" else "hw"
        if sres.dsem is None:
            if self.sempool[kind]:
                sres.dsem, sres.dcnt = self.sempool[kind].pop()
            else:
                sres.dsem = self.new_sem("d" + kind)
            sres.dkind = kind
            self.dres.append(sres)
        assert sres.dkind == kind, "semaphore shared between SW and HW DGE: %s" % sres.name
        sres.dcnt += 16
        ev = (sres.dsem, sres.dcnt, "dma")
        self.ops[eng].append((waits, fn, (sres.dsem, 16)))
        for r in reads:
            r.r[id(ev[0])] = ev
        for w in writes:
            w.w = ev
            w.r = {}
        return ev

    def release(self, res_list):
        for r in res_list:
            if r.dsem is not None:
                self.sempool[r.dkind].append((r.dsem, r.dcnt))
                if r in self.dres:
                    self.dres.remove(r)
                r.dsem = None

    def wait_event(self, eng, ev):
        sem, val, e = ev
        if self.seen[eng].get(id(sem), 0) >= val:
            return
        self.seen[eng][id(sem)] = val
        self.ops[eng].append(([(sem, val)], None, None))

    def barrier(self):
        evs = []
        for e in self.ENG:
            if self.cnt[e] > 0:
                evs.append((self.sem[e], self.cnt[e], e))
        for r in self.dres:
            evs.append((r.dsem, r.dcnt, "dma"))
        for e in self.ENG:
            for ev in evs:
                if ev[2] == e:
                    continue
                self.wait_event(e, ev)

    def emit(self):
        nc = self.nc
        with nc.Block() as block:
            def mk(e):
                def body(engh):
                    for waits, fn, inc in self.ops[e]:
                        for sem, val in waits:
                            engh.wait_ge(sem, val)
                        if fn is not None:
                            ins = fn(engh)
                            ins.then_inc(inc[0], inc[1])
                return body
            block.tensor(mk("pe"))
            block.scalar(mk("act"))
            block.vector(mk("dve"))
            block.gpsimd(mk("pool"))
            block.sync(mk("sp"))


def _pcol_layout():
    off = {}
    n = 0

    def add(name, w):
        nonlocal n
        off[name] = n
        n += w
    add("cvec", 8)
    add("normg", 16)
    add("ada_b", 48)
    add("final_g", 8)
    add("hy_cw", 72)
    add("hy_cb", 24)
    add("hy_bias", 16)
    add("lru_cw", 32)
    add("lru_cb", 8)
    add("lru_gb", 32)
    add("lru_lam", 16)
    add("s0lru", 16)
    add("hy_b1", 2)
    add("hy_b2", 2)
    add("hy_fr", 4)
    add("tnorm", 16)
    add("col127", 1)
    add("colp", 1)
    return off, n


PC, NPC = _pcol_layout()
CT = {"pos": 0, "neg": 128, "eyes": 256, "iota1": 384, "rev": 512, "delta": 640, "logit": 1152, "cmask": 1168,
      "normw": 1200, "nf8": 1328, "nl8": 1336}
NCT = 1344


def build_program():
    nc = bass.Bass("TRN2", target_bir_lowering=False)
    dI = lambda name, shape, dt=F32: nc.dram_tensor(name, list(shape), dt, kind="ExternalInput").ap()
    dO = lambda name, shape, dt=F32: nc.dram_tensor(name, list(shape), dt, kind="ExternalOutput").ap()
    xT_d = dI("xT", [1024, T])
    pcol_d = dI("pcol", [128, NPC])
    ctab_d = dI("ctab", [128, NCT])
    ropec_d = dI("ropec", [128, NT * 64])
    ropes_d = dI("ropes", [128, NT * 64])
    identb_d = dI("identb", [128, 128], BF16)
    s0ret_d = dI("s0ret", [DEPTH, 2, 4, 128, 128])
    dftF_d = dI("dftF", [32, 128, NT * 128], BF16)
    dftI_d = dI("dftI", [32, 128, T], BF16)
    zT_d = dI("zT", [33, T])
    ada_w_d = dI("ada_w", [DEPTH, 1024, 3072])
    w_in_d = dI("w_in", [DEPTH, 1024, 5120])
    w_out_d = dI("w_out", [DEPTH, 1536, 1024])
    hw1_d = dI("hy_ffn_w1", [DEPTH, 33, 64])
    hw2_d = dI("hy_ffn_w2", [DEPTH, 64, 64])
    hw3_d = dI("hy_ffn_w3", [DEPTH, 64, 2048])
    lgw_d = dI("lru_gate_w", [DEPTH, 2, 2, 8, 64, 64])
    yT_d = dO("yT", [1024, T])
    stret_d = dO("st_ret", [DEPTH, 2, 8, 4, 128, 128])
    stlru_d = dO("st_lru", [DEPTH, 2, 4, 128, 8])

    with contextlib.ExitStack() as es:
        P = Prog(nc, es)
        cnt = [0]

        def SB(shape, dt, name=None, stack=None):
            cnt[0] += 1
            nm = "%s_%d" % (name or "t", cnt[0])
            t = (stack or es).enter_context(nc.sbuf_tensor(nm, list(shape), dt))
            r = Res(nm)
            if stack is not None:
                if not hasattr(stack, "_res"):
                    stack._res = []
                stack._res.append(r)
            return t, r

        @contextlib.contextmanager
        def scope():
            st = contextlib.ExitStack()
            try:
                yield st
                P.barrier()
                P.release(getattr(st, "_res", []))
            finally:
                st.close()

        def mm(out, lhsT, rhs, start, stop, reads, writes):
            P.op("pe", lambda e, o=out, l=lhsT, r=rhs, s=start, t=stop: e.matmul(o, lhsT=l, rhs=r, start=s, stop=t),
                 reads, writes)

        def tr(out, in_, reads, writes):
            P.op("pe", lambda e, o=out, i=in_: e.transpose(out=o, in_=i, identity=identb[:]), list(reads) + [r_const], writes)

        def act(out, in_, func, reads, writes, scale=1.0, bias=None):
            if bias is None:
                P.op("act", lambda e, o=out, i=in_, f=func, s=scale: e.activation(out=o, in_=i, func=f, scale=s), reads, writes)
            else:
                P.op("act", lambda e, o=out, i=in_, f=func, s=scale, b=bias: e.activation(out=o, in_=i, func=f, scale=s, bias=b), reads, writes)

        def tt(eng, out, in0, in1, op, reads, writes):
            P.op(eng, lambda e, o=out, a=in0, b=in1, p=op: e.tensor_tensor(out=o, in0=a, in1=b, op=p), reads, writes)

        def ts(eng, out, in0, s1, op0, reads, writes, s2=None, op1=None):
            if op1 is None:
                P.op(eng, lambda e, o=out, a=in0, s=s1, p=op0: e.tensor_scalar(out=o, in0=a, scalar1=s, scalar2=None, op0=p), reads, writes)
            else:
                P.op(eng, lambda e, o=out, a=in0, s=s1, p=op0, s_2=s2, p1=op1: e.tensor_scalar(out=o, in0=a, scalar1=s, scalar2=s_2, op0=p, op1=p1), reads, writes)

        def stt(out, in0, scalar, in1, op0, op1, reads, writes):
            P.op("dve", lambda e, o=out, a=in0, s=scalar, b=in1, p0=op0, p1=op1: e.scalar_tensor_tensor(out=o, in0=a, scalar=s, in1=b, op0=p0, op1=p1), reads, writes)

        def cp(eng, out, in_, reads, writes):
            if eng == "act":
                P.op("act", lambda e, o=out, i=in_: e.copy(out=o, in_=i), reads, writes)
            else:
                P.op(eng, lambda e, o=out, i=in_: e.tensor_copy(out=o, in_=i), reads, writes)

        def dma(eng, out, in_, reads, writes, sres):
            P.dma(eng, lambda e, o=out, i=in_: e.dma_start(out=o, in_=i), reads, writes, sres)

        x, _ = SB([128, 8, T], F32, "x")
        xr = [[Res("x%d_%d" % (k, g)) for g in range(4)] for k in range(8)]
        hT, _ = SB([128, 8, T], BF16, "hT")
        hr = [[Res("h%d_%d" % (k, g)) for g in range(4)] for k in range(8)]
        pcol, r_const = SB([128, NPC], F32, "pcol")
        ctab, _ = SB([128, NCT], F32, "ctab")
        identb, _ = SB([128, 128], BF16, "identb")
        ones32, _ = SB([128, 128], F32, "ones32")
        onesb, _ = SB([128, 128], BF16, "onesb")
        normwb, _ = SB([128, 128], BF16, "normwb")
        modt, r_mod = SB([128, 24], F32, "modt")
        Avec, r_A = SB([128, 8], F32, "Avec")
        sc, r_sc = SB([128, 8], F32, "sc")
        LG, r_LG = SB([128, 16], F32, "LG")
        NW = 2
        wslots = [SB([128, 4096], BF16, "w") for _ in range(NW)]
        wnext = [0]
        PSF = []
        for i in range(8):
            t_ = es.enter_context(nc.psum_tensor("psf%d" % i, [128, 512], F32))
            PSF.append((t_, Res("psf%d" % i)))

        class _BV:
            def __init__(self, t_):
                self.v = t_[:].bitcast(BF16)

            def __getitem__(self, k):
                return self.v[k]
        PSB = [(_BV(PSF[6][0]), PSF[6][1]), (_BV(PSF[7][0]), PSF[7][1])]
        bank_i = [0]

        def bank():
            b = PSF[bank_i[0] % 6]
            bank_i[0] += 1
            return b

        def pc(name, i=0, n=1, parts=128):
            o = PC[name] + i
            return pcol[0:parts, o:o + n]

        dma("sp", pcol[:], pcol_d, [], [r_const], r_const)
        dma("sp", ctab[:], ctab_d, [], [r_const], r_const)
        dma("sp", identb[:], identb_d, [], [r_const], r_const)
        P.op("pool", lambda e: e.memset(ones32[:], 1.0), [], [r_const])
        P.op("pool", lambda e: e.memset(onesb[:], 1.0), [], [r_const])
        act(LG[:], ctab[:, CT["logit"]:CT["logit"] + 16], AF.Exp, [r_const], [r_LG], scale=-1.0)
        ts("dve", LG[:], LG[:], 1.0, ALU.add, [r_LG], [r_LG])
        act(LG[:], LG[:], AF.Ln, [r_LG], [r_LG])
        ts("dve", LG[:], LG[:], -1.0, ALU.mult, [r_LG], [r_LG])

        P.barrier()
        cp("dve", normwb[:], ctab[:, CT["normw"]:CT["normw"] + 128], [r_const], [r_const])
        P.barrier()
        for k in range(8):
            dma("sp", x[:, k, :], xT_d[k * 128:(k + 1) * 128, :], [], xr[k], xr[k][0])

        def wslot(view_k, ncols):
            i = wnext[0] % NW
            wnext[0] += 1
            wt, wr = wslots[i]
            v = wt[:, 0:view_k * ncols].rearrange("p (k c) -> p k c", k=view_k)
            return v, wr

        def w_in_src(l, c0, ncols):
            return w_in_d[l].rearrange("(k p) c -> p k c", p=128)[:, :, c0:c0 + ncols]

        def w_in_cols(l, c0, ncols=512):
            v, wr = wslot(8, ncols)
            dma("pool", v, w_in_src(l, c0, ncols), [], [wr], wr)
            return v, wr

        def bcol(t_, off, name):
            return t_.rearrange("p (s t) -> p s t", t=256)[:, :, off]

        def mod_and_norm(l):
            act(sc[:], pc("cvec", 0, 8), AF.Silu, [r_const], [r_sc])
            with scope() as ls:
                scb, r_scb = SB([128, 8], BF16, "scb", ls)
                cp("dve", scb[:], sc[:], [r_sc], [r_scb])
                aws = [SB([128, 8, 512], BF16, "aw", ls) for _ in range(3)]
                pst, psr = bank()
                for pi in range(6):
                    aw, awr = aws[pi % 3]
                    dma("pool", aw[:], ada_w_d[l].rearrange("(k p) c -> p k c", p=128)[:, :, pi * 512:(pi + 1) * 512], [], [awr], awr)
                    for jj in range(4):
                        j = pi * 4 + jj
                        for k in range(8):
                            mm(pst[:, j:j + 1], aw[:, k, jj * 128:(jj + 1) * 128], scb[:, k:k + 1], k == 0, k == 7, [awr, r_scb], [psr])
                tt("dve", modt[:], pst[:, 0:24], pc("ada_b", l * 24, 24), ALU.add, [psr, r_const], [r_mod])
                ts("dve", Avec[:], modt[:, 8:16], 1.0, ALU.add, [r_mod], [r_A])
                tt("dve", Avec[:], Avec[:], pc("normg", l * 8, 8), ALU.mult, [r_A, r_const], [r_A])
                sqs = [SB([128, 512], BF16, "sq", ls) for _ in range(2)]
                rstds = [SB([128, 512], F32, "rstd", ls) for _ in range(2)]
                tmps = [SB([128, 512], F32, "tmp", ls) for _ in range(2)]
                for g in range(4):
                    rstd, r_rstd = rstds[g % 2]
                    pst, psr = bank()
                    for k in range(8):
                        sq, sqr = sqs[k % 2]
                        act(sq[:], x[:, k, g * 512:(g + 1) * 512], AF.Square, [xr[k][g]], [sqr])
                        mm(pst[:], onesb[:], sq[:], k == 0, k == 7, [sqr, r_const], [psr])
                    ts("dve", rstd[:], pst[:], 1.0 / 1024, ALU.mult, [psr], [r_rstd], s2=EPS, op1=ALU.add)
                    act(rstd[:], rstd[:], AF.Ln, [r_rstd], [r_rstd])
                    act(rstd[:], rstd[:], AF.Exp, [r_rstd], [r_rstd], scale=-0.5)
                    for k in range(8):
                        tmp, tmr = tmps[k % 2]
                        tt("dve" if k % 2 == 0 else "pool", tmp[:], x[:, k, g * 512:(g + 1) * 512], rstd[:], ALU.mult, [xr[k][g], r_rstd], [tmr])
                        act(hT[:, k, g * 512:(g + 1) * 512], tmp[:], AF.Identity, [tmr, r_A, r_mod], [hr[k][g]], scale=Avec[:, k:k + 1], bias=modt[:, k:k + 1])
                P.barrier()

        def out_proj(l, row0, nch, ymix, r_y):
            wo, wor = wslot(nch, 1024)
            dma("pool", wo, w_out_d[l, row0:row0 + nch * 128, :].rearrange("(k p) c -> p k c", p=128), [], [wor], wor)
            for kc in range(8):
                for g in range(4):
                    pst, psr = bank()
                    for c in range(nch):
                        mm(pst[:], wo[:, c, kc * 128:(kc + 1) * 128], ymix[:, c, g * 512:(g + 1) * 512], c == 0, c == nch - 1, [wor, r_y], [psr])
                    stt(x[:, kc, g * 512:(g + 1) * 512], pst[:], modt[:, 16 + kc:17 + kc], x[:, kc, g * 512:(g + 1) * 512], ALU.mult, ALU.add,
                        [psr, r_mod, xr[kc][g]], [xr[kc][g]])

        def retention(l):
            s = 128.0 ** -0.5
            with scope() as ls:
                ropec, r_rope = SB([128, NT, 64], F32, "ropec", ls)
                ropes, _ = SB([128, NT, 64], F32, "ropes", ls)
                dma("sp", ropec[:].rearrange("p n f -> p (n f)"), ropec_d, [], [r_rope], r_rope)
                dma("sp", ropes[:].rearrange("p n f -> p (n f)"), ropes_d, [], [r_rope], r_rope)
                DT, r_DT = SB([128, 4, 128], F32, "DT", ls)
                XF, r_XF = SB([128, 4, 128], F32, "XF", ls)
                XB, _ = SB([128, 4, 128], F32, "XB", ls)
                ZF, r_Z = SB([128, 512], F32, "ZF", ls)
                ZB, _ = SB([128, 512], F32, "ZB", ls)
                zc, r_zc = SB([128, 8], F32, "zc", ls)
                g128, r_g128 = SB([128, 8], F32, "g128", ls)
                DEC, r_DEC = SB([128, 2, NT, 4], F32, "DEC", ls)
                tA, r_tA = SB([128, 128], F32, "tA", ls)
                for h in range(4):
                    lf = LG[:, l * 8 + h:l * 8 + h + 1]
                    lb = LG[:, l * 8 + 4 + h:l * 8 + 4 + h + 1]
                    ts("dve", tA[:], ctab[:, CT["pos"]:CT["pos"] + 128], lf, ALU.mult, [r_const, r_LG], [r_tA])
                    stt(tA[:], ctab[:, CT["neg"]:CT["neg"] + 128], lb, tA[:], ALU.mult, ALU.add, [r_const, r_LG, r_tA], [r_tA])
                    act(tA[:], tA[:], AF.Exp, [r_tA], [r_tA])
                    stt(DT[:, h, :], tA[:], s, ctab[:, CT["eyes"]:CT["eyes"] + 128], ALU.mult, ALU.add, [r_tA, r_const], [r_DT])
                    act(XF[:, h, :], ctab[:, CT["iota1"]:CT["iota1"] + 128], AF.Exp, [r_const, r_LG], [r_XF], scale=lf)
                    act(XB[:, h, :], ctab[:, CT["rev"]:CT["rev"] + 128], AF.Exp, [r_const, r_LG], [r_XF], scale=lb)
                    act(zc[:, h:h + 1], lf, AF.Exp, [r_LG, r_const], [r_zc], scale=pc("col127"))
                    act(zc[:, 4 + h:5 + h], lb, AF.Exp, [r_LG, r_const], [r_zc], scale=pc("colp"))
                ts("dve", zc[:], zc[:], s, ALU.mult, [r_zc], [r_zc])
                for h in range(4):
                    ts("dve", ZF[:, h * 128:(h + 1) * 128], ones32[:], zc[:, h:h + 1], ALU.mult, [r_const, r_zc], [r_Z])
                    ts("dve", ZB[:, h * 128:(h + 1) * 128], ones32[:], zc[:, 4 + h:5 + h], ALU.mult, [r_const, r_zc], [r_Z])
                act(g128[:], LG[:, l * 8:l * 8 + 8], AF.Exp, [r_LG], [r_g128], scale=128.0)
                for d in range(2):
                    for n in range(NT):
                        cm = ctab[:, CT["cmask"] + d * 16 + n:CT["cmask"] + d * 16 + n + 1]
                        ts("dve", DEC[:, d, n, :], g128[:, d * 4:d * 4 + 4], cm, ALU.mult, [r_g128, r_const], [r_DEC])
                s1, r_s1 = SB([128, 4, 64], F32, "s1", ls)
                s2, r_s2 = SB([128, 4, 64], F32, "s2", ls)
                s3, r_s3 = SB([128, 4, 64], F32, "s3", ls)
                s4, r_s4 = SB([128, 4, 64], F32, "s4", ls)
                P.barrier()
                for hp in range(2):
                    with scope() as hs:
                        qk, _ = SB([128, NT, 512], BF16, "qk", hs)
                        vt, _ = SB([128, NT, 256], BF16, "vt", hs)
                        ymix, r_y = SB([128, 2, T], BF16, "ymix", hs)
                        RB, _ = SB([128, NT, 256], BF16, "RB", hs)
                        qk_r = [Res("qk%d" % n) for n in range(NT)]
                        vn_r = [Res("vn%d" % n) for n in range(NT)]
                        rbn_r = [Res("rb%d" % n) for n in range(NT)]
                        WA, rWA = wslot(8, 512)
                        dma("pool", WA[:, :, 0:256], w_in_src(l, hp * 256, 256), [], [rWA], rWA)
                        dma("pool", WA[:, :, 256:512], w_in_src(l, 512 + hp * 256, 256), [], [rWA], rWA)
                        WB, rWB = wslot(8, 512)
                        dma("pool", WB[:, :, 0:256], w_in_src(l, 1024 + hp * 256, 256), [], [rWB], rWB)
                        dma("pool", WB[:, :, 256:512], w_in_src(l, 1536 + hp * 256, 256), [], [rWB], rWB)
                        for n in range(NT):
                            g = n // 4
                            pq, pqr = bank()
                            pv_, pvr = bank()
                            for k in range(8):
                                mm(pq[:], hT[:, k, n * 128:(n + 1) * 128], WA[:, k, :], k == 0, k == 7, [hr[k][g], rWA], [pqr])
                            for k in range(8):
                                mm(pv_[:, 0:256], hT[:, k, n * 128:(n + 1) * 128], WB[:, k, 0:256], k == 0, k == 7, [hr[k][g], rWB], [pvr])
                            pv = pq[:].rearrange("p (h two f) -> p h two f", h=4, two=2)
                            dv = qk[:, n, :].rearrange("p (h two f) -> p h two f", h=4, two=2)
                            cb = ropec[:, n, :].unsqueeze(1).to_broadcast([128, 4, 64])
                            sb_ = ropes[:, n, :].unsqueeze(1).to_broadcast([128, 4, 64])
                            tt("dve", s1[:], pv[:, :, 0, :], cb, ALU.mult, [pqr, r_rope], [r_s1])
                            tt("dve", s2[:], pv[:, :, 1, :], sb_, ALU.mult, [pqr, r_rope], [r_s2])
                            tt("pool", dv[:, :, 0, :], s1[:], s2[:], ALU.subtract, [r_s1, r_s2], [qk_r[n]])
                            tt("dve", s3[:], pv[:, :, 0, :], sb_, ALU.mult, [pqr, r_rope], [r_s3])
                            tt("dve", s4[:], pv[:, :, 1, :], cb, ALU.mult, [pqr, r_rope], [r_s4])
                            tt("pool", dv[:, :, 1, :], s3[:], s4[:], ALU.add, [r_s3, r_s4], [qk_r[n]])
                            cp("act", vt[:, n, :], pv_[:, 0:256], [pvr], [vn_r[n]])
                        Sf, r_Sf = SB([128, 2, 128], F32, "Sf", hs)
                        Sb, r_Sb = SB([128, 2, 128], F32, "Sb", hs)
                        RF, r_RF = SB([128, 256], BF16, "RF", hs)
                        vz, r_vz = SB([128, 256], BF16, "vz", hs)
                        dma("sp", Sf[:], s0ret_d[l, 0, hp * 2:hp * 2 + 2].rearrange("h d e -> d h e"), [], [r_Sf], r_Sf)
                        dma("sp", Sb[:], s0ret_d[l, 1, hp * 2:hp * 2 + 2].rearrange("h d e -> d h e"), [], [r_Sb], r_Sb)
                        H0 = hp * 2
                        for n in range(NT - 1, -1, -1):
                            cmb = ctab[:, CT["cmask"] + 16 + n:CT["cmask"] + 17 + n]
                            ts("dve", RB[:, n, :], Sb[:].rearrange("p h e -> p (h e)"), cmb, ALU.mult, [r_Sb, r_const], [rbn_r[n]])
                            tt("pool", vz[:], vt[:, n, :], ZB[:, H0 * 128:H0 * 128 + 256], ALU.mult, [vn_r[n], r_Z], [r_vz])
                            pst, psr = bank()
                            for hh in range(2):
                                mm(pst[:, hh * 128:(hh + 1) * 128], qk[:, n, 256 + hh * 128:256 + (hh + 1) * 128], vz[:, hh * 128:(hh + 1) * 128], True, True, [qk_r[n], r_vz], [psr])
                            for hh in range(2):
                                stt(Sb[:, hh, :], Sb[:, hh, :], DEC[:, 1, n, H0 + hh:H0 + hh + 1], pst[:, hh * 128:(hh + 1) * 128], ALU.mult, ALU.add, [r_Sb, r_DEC, psr], [r_Sb])
                            if n % 2 == 0:
                                dma("sp", stret_d[l, 1, n // 2, H0:H0 + 2].rearrange("h d e -> d h e"), Sb[:], [r_Sb], [], r_Sb)
                        qT, r_qT = SB([128, 2, 128], BF16, "qT", hs)
                        qxf, r_qxf = SB([128, 2, 128], BF16, "qxf", hs)
                        qxb, r_qxb = SB([128, 2, 128], BF16, "qxb", hs)
                        kT, r_kT = SB([128, 2, 128], BF16, "kT", hs)
                        PT, r_PT = SB([128, 2, 128], BF16, "PT", hs)
                        sqn, r_sqn = SB([128, 512], BF16, "sqn", hs)
                        rn_, r_rn = SB([128, 512], F32, "rn", hs)
                        y1, r_y1 = SB([128, 512], F32, "y1", hs)
                        sgt, r_sgt = SB([128, 512], F32, "sgt", hs)
                        obanks = PSF[0:2]
                        sbank = PSF[4]
                        kbank = PSF[5]
                        for n in range(NT):
                            cmf = ctab[:, CT["cmask"] + n:CT["cmask"] + n + 1]
                            ts("dve", RF[:], Sf[:].rearrange("p h e -> p (h e)"), cmf, ALU.mult, [r_Sf, r_const], [r_RF])
                            tt("pool", vz[:], vt[:, n, :], ZF[:, H0 * 128:H0 * 128 + 256], ALU.mult, [vn_r[n], r_Z], [r_vz])
                            pb, pbr = PSB[n % 2]
                            for i in range(4):
                                tr(pb[:, i * 128:(i + 1) * 128], qk[:, n, i * 128:(i + 1) * 128], [qk_r[n]], [pbr])
                            pq3 = pb[:, 0:256].rearrange("p (h i) -> p h i", h=2)
                            cp("act", qT[:], pq3, [pbr], [r_qT])
                            tt("dve", qxf[:], pq3, XF[:, H0:H0 + 2, :], ALU.mult, [pbr, r_XF], [r_qxf])
                            tt("dve", qxb[:], pq3, XB[:, H0:H0 + 2, :], ALU.mult, [pbr, r_XF], [r_qxb])
                            cp("act", kT[:], pb[:, 256:512].rearrange("p (h i) -> p h i", h=2), [pbr], [r_kT])
                            pss, pssr = sbank
                            for hh in range(2):
                                mm(pss[:, hh * 128:(hh + 1) * 128], kT[:, hh, :], qT[:, hh, :], True, True, [r_kT, r_qT], [pssr])
                            tt("dve", PT[:], pss[:, 0:256].rearrange("p (h i) -> p h i", h=2), DT[:, H0:H0 + 2, :], ALU.mult, [pssr, r_DT], [r_PT])
                            c0 = (n % 4) * 128
                            for hh in range(2):
                                po, por = obanks[hh]
                                mm(po[:, c0:c0 + 128], vt[:, n, hh * 128:(hh + 1) * 128], PT[:, hh, :], True, False, [vn_r[n], r_PT], [por])
                                mm(po[:, c0:c0 + 128], RF[:, hh * 128:(hh + 1) * 128], qxf[:, hh, :], False, False, [r_RF, r_qxf], [por])
                                mm(po[:, c0:c0 + 128], RB[:, n, hh * 128:(hh + 1) * 128], qxb[:, hh, :], False, True, [rbn_r[n], r_qxb], [por])
                            pkv, pkvr = kbank
                            for hh in range(2):
                                mm(pkv[:, hh * 128:(hh + 1) * 128], qk[:, n, 256 + hh * 128:256 + (hh + 1) * 128], vz[:, hh * 128:(hh + 1) * 128], True, True, [qk_r[n], r_vz], [pkvr])
                            for hh in range(2):
                                stt(Sf[:, hh, :], Sf[:, hh, :], DEC[:, 0, n, H0 + hh:H0 + hh + 1], pkv[:, hh * 128:(hh + 1) * 128], ALU.mult, ALU.add, [r_Sf, r_DEC, pkvr], [r_Sf])
                            if n % 2 == 1:
                                dma("sp", stret_d[l, 0, n // 2, H0:H0 + 2].rearrange("h d e -> d h e"), Sf[:], [r_Sf], [], r_Sf)
                            if n % 4 == 3:
                                g = n // 4
                                for hh in range(2):
                                    po, por = obanks[hh]
                                    act(sqn[:], po[:], AF.Square, [por], [r_sqn])
                                    pss, pssr = sbank
                                    mm(pss[:], onesb[:], sqn[:], True, True, [r_sqn, r_const], [pssr])
                                    ts("dve", rn_[:], pss[:], 1.0 / 128, ALU.mult, [pssr], [r_rn], s2=EPS, op1=ALU.add)
                                    act(rn_[:], rn_[:], AF.Ln, [r_rn], [r_rn])
                                    act(rn_[:], rn_[:], AF.Exp, [r_rn], [r_rn], scale=-0.5)
                                    tt("dve", y1[:], po[:], rn_[:], ALU.mult, [por, r_rn], [r_y1])
                                    pg, pgr = PSF[2 + hh]
                                    for k in range(8):
                                        mm(pg[:], WB[:, k, 256 + hh * 128:256 + (hh + 1) * 128], hT[:, k, g * 512:(g + 1) * 512], k == 0, k == 7, [rWB, hr[k][g]], [pgr])
                                    act(sgt[:], pg[:], AF.Silu, [pgr], [r_sgt])
                                    tt("pool", ymix[:, hh, g * 512:(g + 1) * 512], y1[:], sgt[:], ALU.mult, [r_y1, r_sgt], [r_y])
                        out_proj(l, hp * 256, 2, ymix, r_y)
                        P.barrier()

        sv8, r_sv8 = SB([128, 16], F32, "sv8")

        def proj_conv(W, rW, wc, taps, b_ap, dst, dst_r, zfull, r_z):
            for g in range(4):
                pst, psr = bank()
                for k in range(8):
                    mm(pst[:], W[:, k, wc * 128:(wc + 1) * 128], hT[:, k, g * 512:(g + 1) * 512], k == 0, k == 7, [rW, hr[k][g]], [psr])
                cp("act", zfull[:, g * 512:(g + 1) * 512], pst[:], [psr], [r_z])
            for off, wap in taps:
                if off == 0:
                    act(dst, zfull[:], AF.Identity, [r_z, r_const], [dst_r], scale=wap, bias=b_ap)
            nf8 = ctab[:, CT["nf8"]:CT["nf8"] + 8]
            nl8 = ctab[:, CT["nl8"]:CT["nl8"] + 8]
            for off, wap in taps:
                if off == 0:
                    continue
                if off < 0:
                    o = -off
                    cols = [bcol(zfull[:], 255 - i, "z") for i in range(o)]
                    mk = nl8
                else:
                    cols = [bcol(zfull[:], 0, "z")]
                    mk = nf8
                for i, cl in enumerate(cols):
                    cp("dve", sv8[:, i * 8:(i + 1) * 8], cl, [r_z], [r_sv8])
                    tt("dve", cl, sv8[:, i * 8:(i + 1) * 8], mk, ALU.mult, [r_sv8, r_const], [r_z])
                if off < 0:
                    stt(dst[:, o:T], zfull[:, 0:T - o], wap, dst[:, o:T], ALU.mult, ALU.add, [r_z, r_const, dst_r], [dst_r])
                else:
                    stt(dst[:, 0:T - off], zfull[:, off:T], wap, dst[:, 0:T - off], ALU.mult, ALU.add, [r_z, r_const, dst_r], [dst_r])
                for i, cl in enumerate(cols):
                    cp("dve", cl, sv8[:, i * 8:(i + 1) * 8], [r_sv8, dst_r], [r_z])

        def rglru(l):
            with scope() as ls:
                ymix, r_y = SB([128, 4, T], BF16, "ymixl", ls)
                S = [SB([128, T], F32, "S%d" % i, ls) for i in range(6)]
                ub, r_ub = SB([128, T], BF16, "ub", ls)
                gwt = [SB([128, 128], BF16, "gw", ls) for _ in range(4)]
                scv, r_scv = SB([128, 16], F32, "scv", ls)
                stc, r_stc = SB([128, 8], F32, "stc", ls)
                sgl, r_sgl = SB([128, 512], F32, "sgl", ls)
                act(scv[:, 0:8], pc("lru_lam", l * 8, 8), AF.Exp, [r_const], [r_scv], scale=-1.0)
                ts("dve", scv[:, 0:8], scv[:, 0:8], 1.0, ALU.add, [r_scv], [r_scv])
                act(scv[:, 0:8], scv[:, 0:8], AF.Ln, [r_scv], [r_scv])
                ts("dve", scv[:, 8:16], scv[:, 0:8], -16.0, ALU.mult, [r_scv], [r_scv])
                ts("dve", scv[:, 0:8], scv[:, 0:8], -8.0, ALU.mult, [r_scv], [r_scv])
                Wx, rWx = w_in_cols(l, 4096)
                Wgl, rWgl = w_in_cols(l, 4608)
                for gt, gr in gwt:
                    P.op("pool", lambda e, o=gt: e.memset(o[:], 0.0), [], [gr])
                nf8 = ctab[:, CT["nf8"]:CT["nf8"] + 8]
                nl8 = ctab[:, CT["nl8"]:CT["nl8"] + 8]
                for c in range(4):
                    (zfull, r_z), (u, r_u), (scr, r_scr), (h1, r_h1), (h2, r_h2), (rg1, r_rg1) = S
                    taps = [(j - 2, pc("lru_cw", (l * 4 + j) * 4 + c)) for j in range(4)]
                    proj_conv(Wx, rWx, c, taps, pc("lru_cb", l * 4 + c), u[:], r_u, zfull, r_z)
                    cp("pool", ub[:], u[:], [r_u], [r_ub])
                    for d in range(2):
                        gws = []
                        for gi in range(2):
                            gt, gr = gwt[d * 2 + gi]
                            for bb in range(2):
                                dma("pool", gt[bb * 64:(bb + 1) * 64, bb * 64:(bb + 1) * 64], lgw_d[l, d, gi, c * 2 + bb], [], [gr], gr)
                            gws.append((gt, gr))
                        rg, r_rg = (zfull, r_z) if d == 0 else (rg1, r_rg1)
                        ig, r_ig = (h2, r_h2)
                        for gi, (dstt, dstr) in enumerate(((rg, r_rg), (ig, r_ig))):
                            for g in range(4):
                                pst, psr = bank()
                                mm(pst[:], gws[gi][0][:], ub[:, g * 512:(g + 1) * 512], True, True, [gws[gi][1], r_ub], [psr])
                                act(dstt[:, g * 512:(g + 1) * 512], pst[:], AF.Sigmoid, [psr, r_const], [dstr],
                                    bias=pc("lru_gb", ((l * 2 + d) * 2 + gi) * 4 + c))
                        act(scr[:], rg[:], AF.Exp, [r_rg, r_scv], [r_scr], scale=scv[:, 8 + d * 4 + c:9 + d * 4 + c])
                        act(rg[:], rg[:], AF.Exp, [r_rg, r_scv], [r_rg], scale=scv[:, d * 4 + c:d * 4 + c + 1])
                        ts("dve", scr[:], scr[:], 1.0, ALU.min, [r_scr], [r_scr], s2=-1.0, op1=ALU.mult)
                        act(scr[:], scr[:], AF.Sqrt, [r_scr, r_const], [r_scr], bias=ones32[:, 0:1])
                        tt("dve", scr[:], scr[:], ig[:], ALU.mult, [r_scr, r_ig], [r_scr])
                        tt("pool", scr[:], scr[:], u[:], ALU.mult, [r_scr, r_u], [r_scr])
                        h0 = pc("s0lru", (l * 2 + d) * 4 + c)
                        if d == 0:
                            cl = bcol(rg[:], 0, "a")
                            tt("dve", cl, cl, nf8, ALU.mult, [r_rg, r_const], [r_rg])
                            P.op("dve", lambda e, o=h1, a=rg, b=scr, i0=h0: e.tensor_tensor_scan(out=o[:], data0=a[:], data1=b[:], initial=i0, op0=ALU.mult, op1=ALU.add),
                                 [r_rg, r_scr, r_const], [r_h1])
                            cp("dve", stc[:], bcol(h1[:], 255, "h"), [r_h1], [r_stc])
                        else:
                            cl = bcol(rg[:], 255, "a")
                            tt("dve", cl, cl, nl8, ALU.mult, [r_rg, r_const], [r_rg])

                            def rev(t_):
                                xx = t_[:, :]
                                return AP(xx.tensor, xx.offset + T - 1, [list(xx.ap[0]), [-1, T]])
                            P.op("dve", lambda e, o=rev(h2), a=rev(rg), b=rev(scr), i0=h0: e.tensor_tensor_scan(out=o, data0=a, data1=b, initial=i0, op0=ALU.mult, op1=ALU.add),
                                 [r_rg, r_scr, r_const, r_ig], [r_h2])
                            cp("dve", stc[:], bcol(h2[:], 0, "h"), [r_h2], [r_stc])
                            tt("pool", h1[:], h1[:], h2[:], ALU.add, [r_h1, r_h2], [r_h1])
                        dma("sp", stlru_d[l, d, c], stc[:], [r_stc], [], r_stc)
                    for g in range(4):
                        pst, psr = bank()
                        for k in range(8):
                            mm(pst[:], Wgl[:, k, c * 128:(c + 1) * 128], hT[:, k, g * 512:(g + 1) * 512], k == 0, k == 7, [rWgl, hr[k][g]], [psr])
                        act(sgl[:], pst[:], AF.Silu, [psr], [r_sgl])
                        tt("dve", ymix[:, c, g * 512:(g + 1) * 512], h1[:, g * 512:(g + 1) * 512], sgl[:], ALU.mult, [r_h1, r_sgl], [r_y])
                out_proj(l, 1024, 4, ymix, r_y)
                P.barrier()

        def hyena(l):
            with scope() as ls:
                hidb, r_hid = SB([64, T], BF16, "hidb", ls)
                w3, r_w3 = SB([64, 2048], BF16, "w3", ls)
                dma("pool", w3[:], hw3_d[l], [], [r_w3], r_w3)
                with scope() as fs_:
                    zTt, r_zT = SB([33, T], F32, "zT", fs_)
                    w1, r_w1 = SB([33, 64], F32, "w1", fs_)
                    w2, r_w2 = SB([64, 64], F32, "w2", fs_)
                    fb, r_fb = SB([64, 4], F32, "fb", fs_)
                    frh, r_frh = SB([64, 4], F32, "frh", fs_)
                    sa, r_sa = SB([64, 512], F32, "sa", fs_)
                    sb4, r_sb4 = SB([64, 512], F32, "sb4", fs_)
                    hid1, r_hid1 = SB([64, T], F32, "hid1", fs_)
                    dma("sp", zTt[:], zT_d, [], [r_zT], r_zT)
                    dma("sp", w1[:], hw1_d[l], [], [r_w1], r_w1)
                    dma("sp", w2[:], hw2_d[l], [], [r_w2], r_w2)
                    for i, bn in enumerate(("hy_b1", "hy_b2")):
                        tt("dve", fb[:, i:i + 1], pc(bn, l, 1, 64), pc("hy_fr", l * 2 + i, 1, 64), ALU.mult, [r_const], [r_fb])
                        ts("dve", fb[:, 2 + i:3 + i], fb[:, i:i + 1], 0.25, ALU.mult, [r_fb], [r_fb])
                        ts("dve", fb[:, i:i + 1], fb[:, i:i + 1], 0.5, ALU.mult, [r_fb], [r_fb])
                        ts("dve", frh[:, i:i + 1], pc("hy_fr", l * 2 + i, 1, 64), 0.5, ALU.mult, [r_const], [r_frh])
                        ts("dve", frh[:, 2 + i:3 + i], pc("hy_fr", l * 2 + i, 1, 64), 0.25, ALU.mult, [r_const], [r_frh])

                    def sin_layer(i, lhsT, lr, rhs_t, rr, K, out_t, out_r):
                        for g in range(4):
                            pst, psr = bank()
                            mm(pst[0:64, :], lhsT, rhs_t[0:K, g * 512:(g + 1) * 512], True, True, [lr, rr], [psr])
                            act(sa[:], pst[0:64, :], AF.Sin, [psr, r_frh, r_fb], [r_sa], scale=frh[:, i:i + 1], bias=fb[:, i:i + 1])
                            act(sb4[:], pst[0:64, :], AF.Sin, [psr, r_frh, r_fb], [r_sb4], scale=frh[:, 2 + i:3 + i], bias=fb[:, 2 + i:3 + i])
                            tt("dve", sb4[:], sb4[:], sb4[:], ALU.mult, [r_sb4], [r_sb4])
                            ts("dve", sb4[:], sb4[:], -4.0, ALU.mult, [r_sb4], [r_sb4], s2=2.0, op1=ALU.add)
                            tt("dve", out_t[:, g * 512:(g + 1) * 512], sa[:], sb4[:], ALU.mult, [r_sa, r_sb4], [out_r])
                    sin_layer(0, w1[:], r_w1, zTt, r_zT, 33, hid1, r_hid1)
                    sin_layer(1, w2[:], r_w2, hid1, r_hid1, 64, hidb, r_hid)
                    P.barrier()

                Wv_, Wx1, Wx2, Wg_ = 2048, 2560, 3072, 3584

                for hh in range(2):
                    with scope() as hs:
                        Y, r_Y = SB([128, 32, 256], BF16, "Y", hs)
                        ufm, r_ufm = SB([128, 2, T], BF16, "ufm", hs)
                        ntn, r_ntn = SB([128, NT], F32, "ntn", hs)
                        ts("dve", ntn[:], pc("tnorm", 0, NT), -1.0, ALU.mult, [r_const], [r_ntn])

                        def conv_one(W, rW, cc, tapbase, cacc, r_cacc, zfull, r_z):
                            c = hh * 2 + cc
                            ch12 = tapbase * 4 + c
                            taps = [(j - 1, pc("hy_cw", (l * 3 + j) * 12 + ch12)) for j in range(3)]
                            proj_conv(W, rW, c, taps, pc("hy_cb", l * 12 + ch12), cacc[:], r_cacc, zfull, r_z)

                        def spectral(o):
                            with scope() as s2:
                                FU, r_FU = SB([128, NT, 768], BF16, "FU", s2)
                                for n in range(NT):
                                    pb, pbr = PSB[n % 2]
                                    for cc in range(2):
                                        tr(pb[:, cc * 128:(cc + 1) * 128], ufm[:, cc, n * 128:(n + 1) * 128], [r_ufm], [pbr])
                                    cp("act", FU[:, n, 512:768], pb[:, 0:256], [pbr], [r_FU])
                                with scope() as s3:
                                    decs = [SB([128, 256], F32, "dec", s3) for _ in range(2)]
                                    abs_ = [SB([128, 512], BF16, "ab", s3) for _ in range(3)]
                                    tas = [SB([128, 256], F32, "ta", s3) for _ in range(2)]
                                    tbs = [SB([128, 256], F32, "tb", s3) for _ in range(2)]
                                    rn_, r_rn = SB([128, 512], F32, "rnh", s3)
                                    fun_r = [Res("fun%d" % n) for n in range(NT)]
                                    cf = o * 1024 + hh * 256
                                    cb_ = o * 1024 + 512 + hh * 256
                                    pn, pnr = PSF[5]
                                    for n in range(NT + 2):
                                        if n < NT:
                                            pst, psr = PSF[n % 4]
                                            dec, r_dec = decs[n % 2]
                                            ab, r_ab = abs_[n % 3]
                                            mm(pst[:, 0:256], hidb[0:64, n * 128:(n + 1) * 128], w3[0:64, cf:cf + 256], True, True, [r_hid, r_w3], [psr])
                                            mm(pst[:, 256:512], hidb[0:64, n * 128:(n + 1) * 128], w3[0:64, cb_:cb_ + 256], True, True, [r_hid, r_w3], [psr])
                                            act(dec[:], ctab[:, CT["delta"] + hh * 256:CT["delta"] + hh * 256 + 256], AF.Exp, [r_const, r_ntn], [r_dec], scale=ntn[:, n:n + 1])
                                            tt("dve", FU[:, n, 0:512].rearrange("p (d c) -> p d c", d=2), pst[:].rearrange("p (d c) -> p d c", d=2),
                                               dec[:].unsqueeze(1).to_broadcast([128, 2, 256]), ALU.mult, [psr, r_dec], [fun_r[n], r_FU])
                                            act(ab[:], FU[:, n, 0:512], AF.Abs, [fun_r[n]], [r_ab])
                                        m = n - 2
                                        if m >= 0:
                                            ab, r_ab = abs_[m % 3]
                                            mm(pn[:], normwb[:], ab[:], m == 0, m == NT - 1, [r_const, r_ab], [pnr])
                                    ts("dve", rn_[:], pn[:], EPS, ALU.add, [pnr], [r_rn])
                                    P.op("dve", lambda e, r=rn_: e.reciprocal(out=r[:], in_=r[:]), [r_rn], [r_rn])
                                    for n in range(NT):
                                        ta, r_ta = tas[n % 2]
                                        tb, r_tb = tbs[n % 2]
                                        tt("dve", ta[:], FU[:, n, 0:256], rn_[:, 0:256], ALU.mult, [fun_r[n], r_rn], [r_ta])
                                        tt("dve", tb[:], FU[:, n, 256:512], rn_[:, 256:512], ALU.mult, [fun_r[n], r_rn], [r_tb])
                                        tt("dve", FU[:, n, 0:256], ta[:], tb[:], ALU.add, [r_ta, r_tb], [fun_r[n], r_FU])
                                        tt("pool", FU[:, n, 256:512], ta[:], tb[:], ALU.subtract, [r_ta, r_tb], [fun_r[n], r_FU])
                                with scope() as s4:
                                    fstr = [SB([128, NT, 128], BF16, "fstr", s4) for _ in range(NFS)]
                                    U0, r_U0 = SB([128, 512], F32, "U0", s4)
                                    U1, r_U1 = SB([128, 512], F32, "U1", s4)
                                    t1, r_t1 = SB([128, 256], F32, "t1", s4)
                                    t2, r_t2 = SB([128, 256], F32, "t2", s4)
                                    for j in range(16):
                                        fr_, frr = fstr[(2 * j) % NFS]
                                        fi_, fir = fstr[(2 * j + 1) % NFS]
                                        dma("sp", fr_[:].rearrange("p n r -> p (n r)"), dftF_d[j], [], [frr], frr)
                                        dma("sp", fi_[:].rearrange("p n r -> p (n r)"), dftF_d[16 + j], [], [fir], fir)
                                        pre, prer = PSF[(2 * j) % 6]
                                        pim, pimr = PSF[(2 * j + 1) % 6]
                                        for n in range(NT):
                                            mm(pre[:, 0:256], fr_[:, n, :], FU[:, n, 0:256], n == 0, n == NT - 1, [frr, r_FU], [prer])
                                        for n in range(NT):
                                            mm(pre[:, 256:512], fr_[:, n, :], FU[:, n, 512:768], n == 0, n == NT - 1, [frr, r_FU], [prer])
                                        for n in range(NT):
                                            mm(pim[:], fi_[:, n, :], FU[:, n, 256:768], n == 0, n == NT - 1, [fir, r_FU], [pimr])
                                        cp("act", U0[:], pre[:], [prer], [r_U0])
                                        cp("act", U1[:], pim[:], [pimr], [r_U1])
                                        tt("dve", t1[:], U0[:, 256:512], U0[:, 0:256], ALU.mult, [r_U0], [r_t1])
                                        tt("pool", t2[:], U1[:, 256:512], U1[:, 0:256], ALU.mult, [r_U1], [r_t2])
                                        tt("dve", Y[:, j, :], t1[:], t2[:], ALU.subtract, [r_t1, r_t2], [r_Y])
                                        tt("dve", t1[:], U0[:, 256:512], U1[:, 0:256], ALU.mult, [r_U0, r_U1], [r_t1])
                                        tt("pool", t2[:], U1[:, 256:512], U0[:, 0:256], ALU.mult, [r_U0, r_U1], [r_t2])
                                        tt("dve", Y[:, 16 + j, :], t1[:], t2[:], ALU.add, [r_t1, r_t2], [r_Y])

                        def inverse(Wpre, tapbase, o, gate):
                            with scope() as s5:
                                istr = [SB([128, T], BF16, "istr", s5) for _ in range(4)]
                                zfull, r_z = SB([128, T], F32, "zf", s5)
                                cacc, r_cacc = SB([128, T], F32, "cacc", s5)
                                xc, r_xc = SB([128, 2, T], BF16, "xc", s5)
                                ev1, r_ev1 = SB([128, 512], F32, "ev1", s5)
                                sgh, r_sgh = SB([128, 512], F32, "sgh", s5)
                                (W, rW), gpre = Wpre
                                if gate:
                                    Wg, rWg = gpre
                                for cc in range(2):
                                    c = hh * 2 + cc
                                    conv_one(W, rW, cc, tapbase, cacc, r_cacc, zfull, r_z)
                                    if gate:
                                        for g in range(4):
                                            sl = slice(g * 512, (g + 1) * 512)
                                            pg, pgr = bank()
                                            for k in range(8):
                                                mm(pg[:], Wg[:, k, c * 128:(c + 1) * 128], hT[:, k, sl], k == 0, k == 7, [rWg, hr[k][g]], [pgr])
                                            act(sgh[:], pg[:], AF.Silu, [pgr], [r_sgh])
                                            tt("dve", xc[:, cc, sl], cacc[:, sl], sgh[:], ALU.mult, [r_cacc, r_sgh], [r_xc])
                                    else:
                                        cp("pool", xc[:, cc, :], cacc[:], [r_cacc], [r_xc])
                                for j in range(32):
                                    it, itr = istr[j % 4]
                                    dma("sp", it[:], dftI_d[j], [], [itr], itr)
                                    for cc in range(2):
                                        for g in range(4):
                                            pst, psr = PSF[cc * 4 + g]
                                            mm(pst[:], Y[:, j, cc * 128:(cc + 1) * 128], it[:, g * 512:(g + 1) * 512], j == 0, j == 31, [r_Y, itr], [psr])
                                for cc in range(2):
                                    for g in range(4):
                                        sl = slice(g * 512, (g + 1) * 512)
                                        pst, psr = PSF[cc * 4 + g]
                                        stt(ev1[:], ufm[:, cc, sl], pc("hy_bias", (l * 2 + o) * 4 + hh * 2 + cc), pst[:], ALU.mult, ALU.add, [r_ufm, r_const, psr], [r_ev1])
                                        tt("pool", ufm[:, cc, sl], ev1[:], xc[:, cc, sl], ALU.mult, [r_ev1, r_xc], [r_ufm])

                        with scope() as s1:
                            zfull, r_z = SB([128, T], F32, "zf", s1)
                            cacc, r_cacc = SB([128, T], F32, "cacc", s1)
                            W, rW = w_in_cols(l, Wv_)
                            for cc in range(2):
                                conv_one(W, rW, cc, 0, cacc, r_cacc, zfull, r_z)
                                cp("pool", ufm[:, cc, :], cacc[:], [r_cacc], [r_ufm])
                        pre1 = (w_in_cols(l, Wx1), None)
                        spectral(0)
                        inverse(pre1, 1, 0, False)
                        pre2 = (w_in_cols(l, Wx2), w_in_cols(l, Wg_))
                        spectral(1)
                        inverse(pre2, 2, 1, True)
                        out_proj(l, 512 + hh * 256, 2, ufm, r_ufm)

        for l in range(DEPTH):
            mod_and_norm(l)
            if STAGE in ("all", "ret"):
                retention(l)
            if STAGE in ("all", "lru"):
                rglru(l)
            if STAGE in ("all", "hy"):
                hyena(l)
        with scope() as ls:
            outs = [SB([128, 512], F32, "ob", ls) for _ in range(4)]
            oi = [0]
            sqs = [SB([128, 512], BF16, "sq", ls) for _ in range(2)]
            rstds = [SB([128, 512], F32, "rstd", ls) for _ in range(2)]
            for g in range(4):
                rstd, r_rstd = rstds[g % 2]
                pst, psr = bank()
                for k in range(8):
                    sq, sqr = sqs[k % 2]
                    act(sq[:], x[:, k, g * 512:(g + 1) * 512], AF.Square, [xr[k][g]], [sqr])
                    mm(pst[:], onesb[:], sq[:], k == 0, k == 7, [sqr, r_const], [psr])
                ts("dve", rstd[:], pst[:], 1.0 / 1024, ALU.mult, [psr], [r_rstd], s2=EPS, op1=ALU.add)
                act(rstd[:], rstd[:], AF.Ln, [r_rstd], [r_rstd])
                act(rstd[:], rstd[:], AF.Exp, [r_rstd], [r_rstd], scale=-0.5)
                for k in range(8):
                    ob, obr = outs[oi[0] % 4]
                    oi[0] += 1
                    stt(ob[:], x[:, k, g * 512:(g + 1) * 512], pc("final_g", k), rstd[:], ALU.mult, ALU.mult, [xr[k][g], r_const, r_rstd], [obr])
                    dma("sp", yT_d[k * 128:(k + 1) * 128, g * 512:(g + 1) * 512], ob[:], [obr], [], obr)
            P.barrier()
        P.emit()
    return nc


def _bf16(a):
    return np.asarray(a, dtype=np.float32).astype(ml_dtypes.bfloat16)


def _dft_mats(L, nseq):
    N = 2 * L
    t = np.arange(L, dtype=np.int64)
    k = np.arange(L, dtype=np.int64)
    ph = ((2 * k[None, :] + 1) * t[:, None]) % (2 * N)
    ang = ph.astype(np.float64) * (math.pi / N)
    C = np.cos(ang)
    S = -np.sin(ang)
    Fwd = np.zeros((T, 2 * T), np.float32)
    for b in range(nseq):
        Fwd[b * L:(b + 1) * L, b * L:(b + 1) * L] = C
        Fwd[b * L:(b + 1) * L, T + b * L:T + (b + 1) * L] = S
    dftF = np.ascontiguousarray(Fwd.reshape(NT, 128, 32, 128).transpose(2, 1, 0, 3)).reshape(32, 128, NT * 128)
    Inv = (Fwd.T * (2.0 / N)).astype(np.float32)
    dftI = np.ascontiguousarray(Inv.reshape(32, 128, T))
    return _bf16(dftF), _bf16(dftI)


def _core_consts(L, nseq, rope_on):
    f32 = np.float32
    pos_in = np.arange(T) % L
    m_int = 1.0 if L == T else 0.0
    nf8 = np.array([1.0] + [m_int] * 7, f32)
    nl8 = np.array([m_int] * 7 + [1.0], f32)
    masks = (nf8, nl8)
    if rope_on:
        rows = T // 64
        row = np.repeat(np.arange(rows, dtype=f32), 64)
        col = np.tile(np.arange(64, dtype=f32), rows)
        inv = (f32(10000.0) ** (-np.arange(32, dtype=f32) / f32(32))).astype(f32)
        ang = np.concatenate([row[:, None] * inv[None], col[:, None] * inv[None]], axis=-1).astype(f32)
        c, s = np.cos(ang).astype(f32), np.sin(ang).astype(f32)
    else:
        c, s = np.ones((T, 64), f32), np.zeros((T, 64), f32)
    ropec = np.ascontiguousarray(c.reshape(NT, 128, 64).transpose(1, 0, 2)).reshape(128, NT * 64)
    ropes = np.ascontiguousarray(s.reshape(NT, 128, 64).transpose(1, 0, 2)).reshape(128, NT * 64)
    tl = np.linspace(0.0, 1.0, L, dtype=f32)
    f = np.linspace(1e-4, 15.0, 16, dtype=f32)
    ang = (f32(2.0 * math.pi / L) * np.arange(L, dtype=f32)[:, None] * f[None, :]).astype(f32)
    z = np.concatenate([tl[:, None], np.cos(ang), -np.sin(ang)], axis=-1).astype(f32)
    zT = np.ascontiguousarray(np.tile(z, (nseq, 1)).T)
    tnorm = np.tile(tl, nseq).reshape(NT, 128).T
    cm = np.ones((2, NT), f32)
    cpl = L // 128
    for n in range(NT):
        if n % cpl == 0 and n != 0:
            cm[0, n] = 0.0
        if n % cpl == cpl - 1 and n != NT - 1:
            cm[1, n] = 0.0
    return masks, ropec, ropes, zT, np.ascontiguousarray(tnorm), cm, 1.0 / nseq


def _cols(a):
    a = np.asarray(a, np.float32)
    lead = int(np.prod(a.shape[:-1])) if a.ndim > 1 else 1
    n = a.shape[-1] // 128
    return np.ascontiguousarray(a.reshape(lead, n, 128).transpose(2, 0, 1)).reshape(128, lead * n)


_PROG = {}


def kernel(x_prompt, x_sample, state_ret, state_lru, c, c_ctx, norm_g, ada_w, ada_b, w_in,
           ret_decay_logit, hy_conv_w, hy_conv_b, hy_ffn_w1, hy_ffn_b1, hy_ffn_w2, hy_ffn_b2,
           hy_ffn_w3, hy_freq, hy_bias, lru_conv_w, lru_conv_b, lru_gate_w, lru_gate_b,
           lru_lambda, w_out, final_g):
    f32 = np.float32
    A = lambda a: np.ascontiguousarray(np.asarray(a, f32))
    x_prompt, x_sample = A(x_prompt), A(x_sample)
    ncores = 8
    if "nc" not in _PROG:
        _PROG["nc"] = build_program()
    nc = _PROG["nc"]
    dft_s = _dft_mats(2048, 1)
    dft_p = _dft_mats(256, 8)
    cc_s = _core_consts(2048, 1, True)
    cc_p = _core_consts(256, 8, False)
    identb = _bf16(np.eye(128))
    idx = np.arange(128, dtype=f32)
    diff = idx[None, :] - idx[:, None]
    s = f32(128.0 ** -0.5)
    hcw = np.asarray(hy_conv_w, f32)
    deltas = np.abs(np.linspace(math.log(1e-2) / 1.5, math.log(1e-2) / 0.3, 512, dtype=f32)).astype(f32)

    def pcol_for(cv, s0l):
        pc_ = np.zeros((128, NPC), f32)

        def put(name, arr):
            arr = np.asarray(arr, f32)
            pc_[:arr.shape[0], PC[name]:PC[name] + arr.shape[1]] = arr
        put("cvec", _cols(cv))
        put("normg", _cols(norm_g))
        put("ada_b", _cols(ada_b))
        put("final_g", _cols(final_g))
        put("hy_cw", _cols(hcw))
        put("hy_cb", _cols(hy_conv_b))
        put("hy_bias", _cols(hy_bias))
        put("lru_cw", _cols(lru_conv_w))
        put("lru_cb", _cols(lru_conv_b))
        put("lru_gb", _cols(lru_gate_b))
        put("lru_lam", _cols(lru_lambda))
        put("s0lru", _cols(s0l))
        put("hy_b1", np.asarray(hy_ffn_b1, f32).T)
        put("hy_b2", np.asarray(hy_ffn_b2, f32).T)
        put("hy_fr", np.asarray(hy_freq, f32).reshape(4, 64).T)
        put("col127", (127.0 - idx)[:, None])
        put("colp", idx[:, None])
        return pc_

    def ctab_for(cm, nw, bm):
        ct = np.zeros((128, NCT), f32)
        ct[:, CT["pos"]:CT["pos"] + 128] = np.maximum(diff, 0)
        ct[:, CT["neg"]:CT["neg"] + 128] = np.maximum(-diff, 0)
        ct[:, CT["eyes"]:CT["eyes"] + 128] = np.eye(128, dtype=f32) * s
        ct[:, CT["iota1"]:CT["iota1"] + 128] = (idx + 1)[None, :]
        ct[:, CT["rev"]:CT["rev"] + 128] = (128 - idx)[None, :]
        ct[:, CT["delta"]:CT["delta"] + 512] = deltas[None, :]
        ct[:, CT["logit"]:CT["logit"] + 16] = np.asarray(ret_decay_logit, f32).reshape(1, 16)
        ct[:, CT["cmask"]:CT["cmask"] + 32] = cm.reshape(1, 32)
        ct[:, CT["normw"]:CT["normw"] + 128] = nw
        ct[:, CT["nf8"]:CT["nf8"] + 8] = bm[0][None, :]
        ct[:, CT["nl8"]:CT["nl8"] + 8] = bm[1][None, :]
        return ct

    shared = dict(ada_w=A(ada_w), w_in=A(w_in), w_out=A(w_out), hy_ffn_w1=A(hy_ffn_w1), hy_ffn_w2=A(hy_ffn_w2),
                  hy_ffn_w3=A(hy_ffn_w3), lru_gate_w=A(lru_gate_w), identb=identb)
    in_maps = []
    for core in range(ncores):
        if core < 4:
            xs = x_sample[core]
            cv = np.asarray(c, f32)[core]
            s0r = A(state_ret)[core]
            s0l = np.asarray(state_lru, f32)[core]
            cc, dft = cc_s, dft_s
        else:
            pb = (core - 4) % 2
            xs = x_prompt[pb * 8:(pb + 1) * 8].reshape(T, 1024)
            cv = np.asarray(c_ctx, f32)
            s0r = np.zeros((DEPTH, 2, 4, 128, 128), f32)
            s0l = np.zeros((DEPTH, 2, 512), f32)
            cc, dft = cc_p, dft_p
        masks, ropec, ropes, zT, tnorm, cm, nw = cc
        pc_ = pcol_for(cv, s0l)
        pc_[:, PC["tnorm"]:PC["tnorm"] + 16] = tnorm
        m = dict(shared)
        m.update(xT=np.ascontiguousarray(xs.T), pcol=pc_, ctab=ctab_for(cm, nw, masks), ropec=ropec, ropes=ropes,
                 s0ret=np.ascontiguousarray(s0r), dftF=dft[0], dftI=dft[1], zT=zT)
        in_maps.append(m)
    res = run_bass_kernel_spmd(nc, in_maps, core_ids=list(range(ncores)))
    R = res.results
    y_sample = np.stack([np.ascontiguousarray(R[j]["yT"].T) for j in range(4)]).astype(f32)
    y_prompt = np.concatenate([np.ascontiguousarray(R[4 + pb]["yT"].T).reshape(8, 256, 1024) for pb in range(2)]).astype(f32)
    nsr = np.concatenate([R[4 + pb]["st_ret"].transpose(2, 0, 1, 3, 4, 5) for pb in range(2)]).astype(f32)
    nsl = np.concatenate([R[4 + pb]["st_lru"].transpose(4, 0, 1, 2, 3).reshape(8, DEPTH, 2, 512) for pb in range(2)]).astype(f32)
    return (y_prompt, y_sample, np.ascontiguousarray(nsr), np.ascontiguousarray(nsl))
```

```python
import contextlib
import math
import os
import numpy as np
import ml_dtypes
import concourse.bass as bass
import concourse.mybir as mybir
from concourse.bass_utils import run_bass_kernel_spmd
from concourse.ap import AP

F32 = mybir.dt.float32
BF16 = mybir.dt.bfloat16
ALU = mybir.AluOpType
AF = mybir.ActivationFunctionType

EPOCH = 16000
SAME_ENGINE_SYNC = True
EPS = 1e-6
T = 2048
NT = 16
NFS = 4
DEPTH = 2
STAGE = os.environ.get("KSTAGE", "all")


class Res:
    __slots__ = ("name", "w", "r", "dsem", "dcnt", "dkind")

    def __init__(self, name):
        self.name = name
        self.w = None
        self.r = {}
        self.dsem = None
        self.dcnt = 0
        self.dkind = None


class Prog:
    ENG = ("pe", "act", "dve", "pool", "sp")

    def __init__(self, nc, es):
        self.nc = nc
        self.es = es
        self.ops = {e: [] for e in self.ENG}
        self.sem = {}
        self.cnt = {e: 0 for e in self.ENG}
        self.seen = {e: {} for e in self.ENG}
        self.nsem = 0
        self.dres = []
        self.sempool = {"sw": [], "hw": []}
        self.allsems = []
        for e in self.ENG:
            self._new_epoch(e)

    def new_sem(self, name):
        self.nsem += 1
        return self.es.enter_context(self.nc.semaphore("%s_%d" % (name, self.nsem)))

    def _new_epoch(self, e):
        self.sem[e] = self.new_sem("e_" + e)
        self.cnt[e] = 0
        self.allsems.append((e, self.sem[e]))

    def _waits(self, eng, reads, writes):
        need = {}

        def add(ev):
            if ev is None:
                return
            sem, val, e = ev
            if e == eng and (eng == "pe" or not SAME_ENGINE_SYNC):
                return
            k = id(sem)
            if self.seen[eng].get(k, 0) >= val:
                return
            if k not in need or need[k][1] < val:
                need[k] = (sem, val)

        for r in reads:
            add(r.w)
        for w in writes:
            add(w.w)
            for ev in w.r.values():
                add(ev)
        out = list(need.values())
        for sem, val in out:
            self.seen[eng][id(sem)] = val
        return out

    def op(self, eng, fn, reads=(), writes=()):
        waits = self._waits(eng, reads, writes)
        if self.cnt[eng] >= EPOCH:
            self._new_epoch(eng)
        self.cnt[eng] += 1
        ev = (self.sem[eng], self.cnt[eng], eng)
        self.ops[eng].append((waits, fn, (self.sem[eng], 1)))
        for r in reads:
            r.r[id(ev[0])] = ev
        for w in writes:
            w.w = ev
            w.r = {}
        return ev

    def dma(self, eng, fn, reads=(), writes=(), sres=None):
        waits = self._waits(eng, reads, writes)
        kind = "sw" if eng == "pool" else "hw"
        if sres.dsem is None:
            if self.sempool[kind]:
                sres.dsem, sres.dcnt = self.sempool[kind].pop()
            else:
                sres.dsem = self.new_sem("d" + kind)
            sres.dkind = kind
            self.dres.append(sres)
        assert sres.dkind == kind, "semaphore shared between SW and HW DGE: %s" % sres.name
        sres.dcnt += 16
        ev = (sres.dsem, sres.dcnt, "dma")
        self.ops[eng].append((waits, fn, (sres.dsem, 16)))
        for r in reads:
            r.r[id(ev[0])] = ev
        for w in writes:
            w.w = ev
            w.r = {}
        return ev

    def release(self, res_list):
        for r in res_list:
            if r.dsem is not None:
                self.sempool[r.dkind].append((r.dsem, r.dcnt))
                if r in self.dres:
                    self.dres.remove(r)
                r.dsem = None

    def wait_event(self, eng, ev):
        sem, val, e = ev
        if self.seen[eng].get(id(sem), 0) >= val:
            return
        self.seen[eng][id(sem)] = val
        self.ops[eng].append(([(sem, val)], None, None))

    def barrier(self):
        evs = []
        for e in self.ENG:
            if self.cnt[e] > 0:
                evs.append((self.sem[e], self.cnt[e], e))
        for r in self.dres:
            evs.append((r.dsem, r.dcnt, "dma"))
        for e in self.ENG:
            for ev in evs:
                if ev[2] == e:
                    continue
                self.wait_event(e, ev)

    def emit(self):
        nc = self.nc
        with nc.Block() as block:
            def mk(e):
                def body(engh):
                    for waits, fn, inc in self.ops[e]:
                        for sem, val in waits:
                            engh.wait_ge(sem, val)
                        if fn is not None:
                            ins = fn(engh)
                            ins.then_inc(inc[0], inc[1])
                return body
            block.tensor(mk("pe"))
            block.scalar(mk("act"))
            block.vector(mk("dve"))
            block.gpsimd(mk("pool"))
            block.sync(mk("sp"))


def _pcol_layout():
    off = {}
    n = 0

    def add(name, w):
        nonlocal n
        off[name] = n
        n += w
    add("cvec", 8)
    add("normg", 16)
    add("ada_b", 48)
    add("final_g", 8)
    add("hy_cw", 72)
    add("hy_cb", 24)
    add("hy_bias", 16)
    add("lru_cw", 32)
    add("lru_cb", 8)
    add("lru_gb", 32)
    add("lru_lam", 16)
    add("s0lru", 16)
    add("hy_b1", 2)
    add("hy_b2", 2)
    add("hy_fr", 4)
    add("tnorm", 16)
    add("col127", 1)
    add("colp", 1)
    return off, n


PC, NPC = _pcol_layout()
CT = {"pos": 0, "neg": 128, "eyes": 256, "iota1": 384, "rev": 512, "delta": 640, "logit": 1152, "cmask": 1168,
      "normw": 1200, "nf8": 1328, "nl8": 1336}
NCT = 1344


def build_program():
    nc = bass.Bass("TRN2", target_bir_lowering=False)
    dI = lambda name, shape, dt=F32: nc.dram_tensor(name, list(shape), dt, kind="ExternalInput").ap()
    dO = lambda name, shape, dt=F32: nc.dram_tensor(name, list(shape), dt, kind="ExternalOutput").ap()
    xT_d = dI("xT", [1024, T])
    pcol_d = dI("pcol", [128, NPC])
    ctab_d = dI("ctab", [128, NCT])
    ropec_d = dI("ropec", [128, NT * 64])
    ropes_d = dI("ropes", [128, NT * 64])
    identb_d = dI("identb", [128, 128], BF16)
    s0ret_d = dI("s0ret", [DEPTH, 2, 4, 128, 128])
    dftF_d = dI("dftF", [32, 128, NT * 128], BF16)
    dftI_d = dI("dftI", [32, 128, T], BF16)
    zT_d = dI("zT", [33, T])
    ada_w_d = dI("ada_w", [DEPTH, 1024, 3072])
    w_in_d = dI("w_in", [DEPTH, 1024, 5120])
    w_out_d = dI("w_out", [DEPTH, 1536, 1024])
    hw1_d = dI("hy_ffn_w1", [DEPTH, 33, 64])
    hw2_d = dI("hy_ffn_w2", [DEPTH, 64, 64])
    hw3_d = dI("hy_ffn_w3", [DEPTH, 64, 2048])
    lgw_d = dI("lru_gate_w", [DEPTH, 2, 2, 8, 64, 64])
    yT_d = dO("yT", [1024, T])
    stret_d = dO("st_ret", [DEPTH, 2, 8, 4, 128, 128])
    stlru_d = dO("st_lru", [DEPTH, 2, 4, 128, 8])

    with contextlib.ExitStack() as es:
        P = Prog(nc, es)
        cnt = [0]

        def SB(shape, dt, name=None, stack=None):
            cnt[0] += 1
            nm = "%s_%d" % (name or "t", cnt[0])
            t = (stack or es).enter_context(nc.sbuf_tensor(nm, list(shape), dt))
            r = Res(nm)
            if stack is not None:
                if not hasattr(stack, "_res"):
                    stack._res = []
                stack._res.append(r)
            return t, r

        @contextlib.contextmanager
        def scope():
            st = contextlib.ExitStack()
            try:
                yield st
                P.barrier()
                P.release(getattr(st, "_res", []))
            finally:
                st.close()

        def mm(out, lhsT, rhs, start, stop, reads, writes):
            P.op("pe", lambda e, o=out, l=lhsT, r=rhs, s=start, t=stop: e.matmul(o, lhsT=l, rhs=r, start=s, stop=t),
                 reads, writes)

        def tr(out, in_, reads, writes):
            P.op("pe", lambda e, o=out, i=in_: e.transpose(out=o, in_=i, identity=identb[:]), list(reads) + [r_const], writes)

        def act(out, in_, func, reads, writes, scale=1.0, bias=None):
            if bias is None:
                P.op("act", lambda e, o=out, i=in_, f=func, s=scale: e.activation(out=o, in_=i, func=f, scale=s), reads, writes)
            else:
                P.op("act", lambda e, o=out, i=in_, f=func, s=scale, b=bias: e.activation(out=o, in_=i, func=f, scale=s, bias=b), reads, writes)

        def tt(eng, out, in0, in1, op, reads, writes):
            P.op(eng, lambda e, o=out, a=in0, b=in1, p=op: e.tensor_tensor(out=o, in0=a, in1=b, op=p), reads, writes)

        def ts(eng, out, in0, s1, op0, reads, writes, s2=None, op1=None):
            if op1 is None:
                P.op(eng, lambda e, o=out, a=in0, s=s1, p=op0: e.tensor_scalar(out=o, in0=a, scalar1=s, scalar2=None, op0=p), reads, writes)
            else:
                P.op(eng, lambda e, o=out, a=in0, s=s1, p=op0, s_2=s2, p1=op1: e.tensor_scalar(out=o, in0=a, scalar1=s, scalar2=s_2, op0=p, op1=p1), reads, writes)

        def stt(out, in0, scalar, in1, op0, op1, reads, writes):
            P.op("dve", lambda e, o=out, a=in0, s=scalar, b=in1, p0=op0, p1=op1: e.scalar_tensor_tensor(out=o, in0=a, scalar=s, in1=b, op0=p0, op1=p1), reads, writes)

        def cp(eng, out, in_, reads, writes):
            if eng == "act":
                P.op("act", lambda e, o=out, i=in_: e.copy(out=o, in_=i), reads, writes)
            else:
                P.op(eng, lambda e, o=out, i=in_: e.tensor_copy(out=o, in_=i), reads, writes)

        def dma(eng, out, in_, reads, writes, sres):
            P.dma(eng, lambda e, o=out, i=in_: e.dma_start(out=o, in_=i), reads, writes, sres)

        x, _ = SB([128, 8, T], F32, "x")
        xr = [[Res("x%d_%d" % (k, g)) for g in range(4)] for k in range(8)]
        hT, _ = SB([128, 8, T], BF16, "hT")
        hr = [[Res("h%d_%d" % (k, g)) for g in range(4)] for k in range(8)]
        pcol, r_const = SB([128, NPC], F32, "pcol")
        ctab, _ = SB([128, NCT], F32, "ctab")
        identb, _ = SB([128, 128], BF16, "identb")
        ones32, _ = SB([128, 128], F32, "ones32")
        onesb, _ = SB([128, 128], BF16, "onesb")
        normwb, _ = SB([128, 128], BF16, "normwb")
        modt, r_mod = SB([128, 24], F32, "modt")
        Avec, r_A = SB([128, 8], F32, "Avec")
        sc, r_sc = SB([128, 8], F32, "sc")
        LG, r_LG = SB([128, 16], F32, "LG")
        NW = 2
        wslots = [SB([128, 4096], BF16, "w") for _ in range(NW)]
        wnext = [0]
        PSF = []
        for i in range(8):
            t_ = es.enter_context(nc.psum_tensor("psf%d" % i, [128, 512], F32))
            PSF.append((t_, Res("psf%d" % i)))

        class _BV:
            def __init__(self, t_):
                self.v = t_[:].bitcast(BF16)

            def __getitem__(self, k):
                return self.v[k]
        PSB = [(_BV(PSF[6][0]), PSF[6][1]), (_BV(PSF[7][0]), PSF[7][1])]
        bank_i = [0]

        def bank():
            b = PSF[bank_i[0] % 6]
            bank_i[0] += 1
            return b

        def pc(name, i=0, n=1, parts=128):
            o = PC[name] + i
            return pcol[0:parts, o:o + n]

        dma("sp", pcol[:], pcol_d, [], [r_const], r_const)
        dma("sp", ctab[:], ctab_d, [], [r_const], r_const)
        dma("sp", identb[:], identb_d, [], [r_const], r_const)
        P.op("pool", lambda e: e.memset(ones32[:], 1.0), [], [r_const])
        P.op("pool", lambda e: e.memset(onesb[:], 1.0), [], [r_const])
        act(LG[:], ctab[:, CT["logit"]:CT["logit"] + 16], AF.Exp, [r_const], [r_LG], scale=-1.0)
        ts("dve", LG[:], LG[:], 1.0, ALU.add, [r_LG], [r_LG])
        act(LG[:], LG[:], AF.Ln, [r_LG], [r_LG])
        ts("dve", LG[:], LG[:], -1.0, ALU.mult, [r_LG], [r_LG])

        P.barrier()
        cp("dve", normwb[:], ctab[:, CT["normw"]:CT["normw"] + 128], [r_const], [r_const])
        P.barrier()
        for k in range(8):
            dma("sp", x[:, k, :], xT_d[k * 128:(k + 1) * 128, :], [], xr[k], xr[k][0])

        def wslot(view_k, ncols):
            i = wnext[0] % NW
            wnext[0] += 1
            wt, wr = wslots[i]
            v = wt[:, 0:view_k * ncols].rearrange("p (k c) -> p k c", k=view_k)
            return v, wr

        def w_in_src(l, c0, ncols):
            return w_in_d[l].rearrange("(k p) c -> p k c", p=128)[:, :, c0:c0 + ncols]

        def w_in_cols(l, c0, ncols=512):
            v, wr = wslot(8, ncols)
            dma("pool", v, w_in_src(l, c0, ncols), [], [wr], wr)
            return v, wr

        def bcol(t_, off, name):
            return t_.rearrange("p (s t) -> p s t", t=256)[:, :, off]

        def mod_and_norm(l):
            act(sc[:], pc("cvec", 0, 8), AF.Silu, [r_const], [r_sc])
            with scope() as ls:
                scb, r_scb = SB([128, 8], BF16, "scb", ls)
                cp("dve", scb[:], sc[:], [r_sc], [r_scb])
                aws = [SB([128, 8, 512], BF16, "aw", ls) for _ in range(3)]
                pst, psr = bank()
                for pi in range(6):
                    aw, awr = aws[pi % 3]
                    dma("pool", aw[:], ada_w_d[l].rearrange("(k p) c -> p k c", p=128)[:, :, pi * 512:(pi + 1) * 512], [], [awr], awr)
                    for jj in range(4):
                        j = pi * 4 + jj
                        for k in range(8):
                            mm(pst[:, j:j + 1], aw[:, k, jj * 128:(jj + 1) * 128], scb[:, k:k + 1], k == 0, k == 7, [awr, r_scb], [psr])
                tt("dve", modt[:], pst[:, 0:24], pc("ada_b", l * 24, 24), ALU.add, [psr, r_const], [r_mod])
                ts("dve", Avec[:], modt[:, 8:16], 1.0, ALU.add, [r_mod], [r_A])
                tt("dve", Avec[:], Avec[:], pc("normg", l * 8, 8), ALU.mult, [r_A, r_const], [r_A])
                sqs = [SB([128, 512], BF16, "sq", ls) for _ in range(2)]
                rstds = [SB([128, 512], F32, "rstd", ls) for _ in range(2)]
                tmps = [SB([128, 512], F32, "tmp", ls) for _ in range(2)]
                for g in range(4):
                    rstd, r_rstd = rstds[g % 2]
                    pst, psr = bank()
                    for k in range(8):
                        sq, sqr = sqs[k % 2]
                        act(sq[:], x[:, k, g * 512:(g + 1) * 512], AF.Square, [xr[k][g]], [sqr])
                        mm(pst[:], onesb[:], sq[:], k == 0, k == 7, [sqr, r_const], [psr])
                    ts("dve", rstd[:], pst[:], 1.0 / 1024, ALU.mult, [psr], [r_rstd], s2=EPS, op1=ALU.add)
                    act(rstd[:], rstd[:], AF.Ln, [r_rstd], [r_rstd])
                    act(rstd[:], rstd[:], AF.Exp, [r_rstd], [r_rstd], scale=-0.5)
                    for k in range(8):
                        tmp, tmr = tmps[k % 2]
                        tt("dve" if k % 2 == 0 else "pool", tmp[:], x[:, k, g * 512:(g + 1) * 512], rstd[:], ALU.mult, [xr[k][g], r_rstd], [tmr])
                        act(hT[:, k, g * 512:(g + 1) * 512], tmp[:], AF.Identity, [tmr, r_A, r_mod], [hr[k][g]], scale=Avec[:, k:k + 1], bias=modt[:, k:k + 1])
                P.barrier()

        def out_proj(l, row0, nch, ymix, r_y):
            wo, wor = wslot(nch, 1024)
            dma("pool", wo, w_out_d[l, row0:row0 + nch * 128, :].rearrange("(k p) c -> p k c", p=128), [], [wor], wor)
            for kc in range(8):
                for g in range(4):
                    pst, psr = bank()
                    for c in range(nch):
                        mm(pst[:], wo[:, c, kc * 128:(kc + 1) * 128], ymix[:, c, g * 512:(g + 1) * 512], c == 0, c == nch - 1, [wor, r_y], [psr])
                    stt(x[:, kc, g * 512:(g + 1) * 512], pst[:], modt[:, 16 + kc:17 + kc], x[:, kc, g * 512:(g + 1) * 512], ALU.mult, ALU.add,
                        [psr, r_mod, xr[kc][g]], [xr[kc][g]])

        def retention(l):
            s = 128.0 ** -0.5
            with scope() as ls:
                ropec, r_rope = SB([128, NT, 64], F32, "ropec", ls)
                ropes, _ = SB([128, NT, 64], F32, "ropes", ls)
                dma("sp", ropec[:].rearrange("p n f -> p (n f)"), ropec_d, [], [r_rope], r_rope)
                dma("sp", ropes[:].rearrange("p n f -> p (n f)"), ropes_d, [], [r_rope], r_rope)
                DT, r_DT = SB([128, 4, 128], F32, "DT", ls)
                XF, r_XF = SB([128, 4, 128], F32, "XF", ls)
                XB, _ = SB([128, 4, 128], F32, "XB", ls)
                ZF, r_Z = SB([128, 512], F32, "ZF", ls)
                ZB, _ = SB([128, 512], F32, "ZB", ls)
                zc, r_zc = SB([128, 8], F32, "zc", ls)
                g128, r_g128 = SB([128, 8], F32, "g128", ls)
                DEC, r_DEC = SB([128, 2, NT, 4], F32, "DEC", ls)
                tA, r_tA = SB([128, 128], F32, "tA", ls)
                for h in range(4):
                    lf = LG[:, l * 8 + h:l * 8 + h + 1]
                    lb = LG[:, l * 8 + 4 + h:l * 8 + 4 + h + 1]
                    ts("dve", tA[:], ctab[:, CT["pos"]:CT["pos"] + 128], lf, ALU.mult, [r_const, r_LG], [r_tA])
                    stt(tA[:], ctab[:, CT["neg"]:CT["neg"] + 128], lb, tA[:], ALU.mult, ALU.add, [r_const, r_LG, r_tA], [r_tA])
                    act(tA[:], tA[:], AF.Exp, [r_tA], [r_tA])
                    stt(DT[:, h, :], tA[:], s, ctab[:, CT["eyes"]:CT["eyes"] + 128], ALU.mult, ALU.add, [r_tA, r_const], [r_DT])
                    act(XF[:, h, :], ctab[:, CT["iota1"]:CT["iota1"] + 128], AF.Exp, [r_const, r_LG], [r_XF], scale=lf)
                    act(XB[:, h, :], ctab[:, CT["rev"]:CT["rev"] + 128], AF.Exp, [r_const, r_LG], [r_XF], scale=lb)
                    act(zc[:, h:h + 1], lf, AF.Exp, [r_LG, r_const], [r_zc], scale=pc("col127"))
                    act(zc[:, 4 + h:5 + h], lb, AF.Exp, [r_LG, r_const], [r_zc], scale=pc("colp"))
                ts("dve", zc[:], zc[:], s, ALU.mult, [r_zc], [r_zc])
                for h in range(4):
                    ts("dve", ZF[:, h * 128:(h + 1) * 128], ones32[:], zc[:, h:h + 1], ALU.mult, [r_const, r_zc], [r_Z])
                    ts("dve", ZB[:, h * 128:(h + 1) * 128], ones32[:], zc[:, 4 + h:5 + h], ALU.mult, [r_const, r_zc], [r_Z])
                act(g128[:], LG[:, l * 8:l * 8 + 8], AF.Exp, [r_LG], [r_g128], scale=128.0)
                for d in range(2):
                    for n in range(NT):
                        cm = ctab[:, CT["cmask"] + d * 16 + n:CT["cmask"] + d * 16 + n + 1]
                        ts("dve", DEC[:, d, n, :], g128[:, d * 4:d * 4 + 4], cm, ALU.mult, [r_g128, r_const], [r_DEC])
                s1, r_s1 = SB([128, 4, 64], F32, "s1", ls)
                s2, r_s2 = SB([128, 4, 64], F32, "s2", ls)
                s3, r_s3 = SB([128, 4, 64], F32, "s3", ls)
                s4, r_s4 = SB([128, 4, 64], F32, "s4", ls)
                P.barrier()
                for hp in range(2):
                    with scope() as hs:
                        qk, _ = SB([128, NT, 512], BF16, "qk", hs)
                        vt, _ = SB([128, NT, 256], BF16, "vt", hs)
                        ymix, r_y = SB([128, 2, T], BF16, "ymix", hs)
                        RB, _ = SB([128, NT, 256], BF16, "RB", hs)
                        qk_r = [Res("qk%d" % n) for n in range(NT)]
                        vn_r = [Res("vn%d" % n) for n in range(NT)]
                        rbn_r = [Res("rb%d" % n) for n in range(NT)]
                        WA, rWA = wslot(8, 512)
                        dma("pool", WA[:, :, 0:256], w_in_src(l, hp * 256, 256), [], [rWA], rWA)
                        dma("pool", WA[:, :, 256:512], w_in_src(l, 512 + hp * 256, 256), [], [rWA], rWA)
                        WB, rWB = wslot(8, 512)
                        dma("pool", WB[:, :, 0:256], w_in_src(l, 1024 + hp * 256, 256), [], [rWB], rWB)
                        dma("pool", WB[:, :, 256:512], w_in_src(l, 1536 + hp * 256, 256), [], [rWB], rWB)
                        for n in range(NT):
                            g = n // 4
                            pq, pqr = bank()
                            pv_, pvr = bank()
                            for k in range(8):
                                mm(pq[:], hT[:, k, n * 128:(n + 1) * 128], WA[:, k, :], k == 0, k == 7, [hr[k][g], rWA], [pqr])
                            for k in range(8):
                                mm(pv_[:, 0:256], hT[:, k, n * 128:(n + 1) * 128], WB[:, k, 0:256], k == 0, k == 7, [hr[k][g], rWB], [pvr])
                            pv = pq[:].rearrange("p (h two f) -> p h two f", h=4, two=2)
                            dv = qk[:, n, :].rearrange("p (h two f) -> p h two f", h=4, two=2)
                            cb = ropec[:, n, :].unsqueeze(1).to_broadcast([128, 4, 64])
                            sb_ = ropes[:, n, :].unsqueeze(1).to_broadcast([128, 4, 64])
                            tt("dve", s1[:], pv[:, :, 0, :], cb, ALU.mult, [pqr, r_rope], [r_s1])
                            tt("dve", s2[:], pv[:, :, 1, :], sb_, ALU.mult, [pqr, r_rope], [r_s2])
                            tt("pool", dv[:, :, 0, :], s1[:], s2[:], ALU.subtract, [r_s1, r_s2], [qk_r[n]])
                            tt("dve", s3[:], pv[:, :, 0, :], sb_, ALU.mult, [pqr, r_rope], [r_s3])
                            tt("dve", s4[:], pv[:, :, 1, :], cb, ALU.mult, [pqr, r_rope], [r_s4])
                            tt("pool", dv[:, :, 1, :], s3[:], s4[:], ALU.add, [r_s3, r_s4], [qk_r[n]])
                            cp("act", vt[:, n, :], pv_[:, 0:256], [pvr], [vn_r[n]])
                        Sf, r_Sf = SB([128, 2, 128], F32, "Sf", hs)
                        Sb, r_Sb = SB([128, 2, 128], F32, "Sb", hs)
                        RF, r_RF = SB([128, 256], BF16, "RF", hs)
                        vz, r_vz = SB([128, 256], BF16, "vz", hs)
                        dma("sp", Sf[:], s0ret_d[l, 0, hp * 2:hp * 2 + 2].rearrange("h d e -> d h e"), [], [r_Sf], r_Sf)
                        dma("sp", Sb[:], s0ret_d[l, 1, hp * 2:hp * 2 + 2].rearrange("h d e -> d h e"), [], [r_Sb], r_Sb)
                        H0 = hp * 2
                        for n in range(NT - 1, -1, -1):
                            cmb = ctab[:, CT["cmask"] + 16 + n:CT["cmask"] + 17 + n]
                            ts("dve", RB[:, n, :], Sb[:].rearrange("p h e -> p (h e)"), cmb, ALU.mult, [r_Sb, r_const], [rbn_r[n]])
                            tt("pool", vz[:], vt[:, n, :], ZB[:, H0 * 128:H0 * 128 + 256], ALU.mult, [vn_r[n], r_Z], [r_vz])
                            pst, psr = bank()
                            for hh in range(2):
                                mm(pst[:, hh * 128:(hh + 1) * 128], qk[:, n, 256 + hh * 128:256 + (hh + 1) * 128], vz[:, hh * 128:(hh + 1) * 128], True, True, [qk_r[n], r_vz], [psr])
                            for hh in range(2):
                                stt(Sb[:, hh, :], Sb[:, hh, :], DEC[:, 1, n, H0 + hh:H0 + hh + 1], pst[:, hh * 128:(hh + 1) * 128], ALU.mult, ALU.add, [r_Sb, r_DEC, psr], [r_Sb])
                            if n % 2 == 0:
                                dma("sp", stret_d[l, 1, n // 2, H0:H0 + 2].rearrange("h d e -> d h e"), Sb[:], [r_Sb], [], r_Sb)
                        qT, r_qT = SB([128, 2, 128], BF16, "qT", hs)
                        qxf, r_qxf = SB([128, 2, 128], BF16, "qxf", hs)
                        qxb, r_qxb = SB([128, 2, 128], BF16, "qxb", hs)
                        kT, r_kT = SB([128, 2, 128], BF16, "kT", hs)
                        PT, r_PT = SB([128, 2, 128], BF16, "PT", hs)
                        sqn, r_sqn = SB([128, 512], BF16, "sqn", hs)
                        rn_, r_rn = SB([128, 512], F32, "rn", hs)
                        y1, r_y1 = SB([128, 512], F32, "y1", hs)
                        sgt, r_sgt = SB([128, 512], F32, "sgt", hs)
                        obanks = PSF[0:2]
                        sbank = PSF[4]
                        kbank = PSF[5]
                        for n in range(NT):
                            cmf = ctab[:, CT["cmask"] + n:CT["cmask"] + n + 1]
                            ts("dve", RF[:], Sf[:].rearrange("p h e -> p (h e)"), cmf, ALU.mult, [r_Sf, r_const], [r_RF])
                            tt("pool", vz[:], vt[:, n, :], ZF[:, H0 * 128:H0 * 128 + 256], ALU.mult, [vn_r[n], r_Z], [r_vz])
                            pb, pbr = PSB[n % 2]
                            for i in range(4):
                                tr(pb[:, i * 128:(i + 1) * 128], qk[:, n, i * 128:(i + 1) * 128], [qk_r[n]], [pbr])
                            pq3 = pb[:, 0:256].rearrange("p (h i) -> p h i", h=2)
                            cp("act", qT[:], pq3, [pbr], [r_qT])
                            tt("dve", qxf[:], pq3, XF[:, H0:H0 + 2, :], ALU.mult, [pbr, r_XF], [r_qxf])
                            tt("dve", qxb[:], pq3, XB[:, H0:H0 + 2, :], ALU.mult, [pbr, r_XF], [r_qxb])
                            cp("act", kT[:], pb[:, 256:512].rearrange("p (h i) -> p h i", h=2), [pbr], [r_kT])
                            pss, pssr = sbank
                            for hh in range(2):
                                mm(pss[:, hh * 128:(hh + 1) * 128], kT[:, hh, :], qT[:, hh, :], True, True, [r_kT, r_qT], [pssr])
                            tt("dve", PT[:], pss[:, 0:256].rearrange("p (h i) -> p h i", h=2), DT[:, H0:H0 + 2, :], ALU.mult, [pssr, r_DT], [r_PT])
                            c0 = (n % 4) * 128
                            for hh in range(2):
                                po, por = obanks[hh]
                                mm(po[:, c0:c0 + 128], vt[:, n, hh * 128:(hh + 1) * 128], PT[:, hh, :], True, False, [vn_r[n], r_PT], [por])
                                mm(po[:, c0:c0 + 128], RF[:, hh * 128:(hh + 1) * 128], qxf[:, hh, :], False, False, [r_RF, r_qxf], [por])
                                mm(po[:, c0:c0 + 128], RB[:, n, hh * 128:(hh + 1) * 128], qxb[:, hh, :], False, True, [rbn_r[n], r_qxb], [por])
                            pkv, pkvr = kbank
                            for hh in range(2):
                                mm(pkv[:, hh * 128:(hh + 1) * 128], qk[:, n, 256 + hh * 128:256 + (hh + 1) * 128], vz[:, hh * 128:(hh + 1) * 128], True, True, [qk_r[n], r_vz], [pkvr])
                            for hh in range(2):
                                stt(Sf[:, hh, :], Sf[:, hh, :], DEC[:, 0, n, H0 + hh:H0 + hh + 1], pkv[:, hh * 128:(hh + 1) * 128], ALU.mult, ALU.add, [r_Sf, r_DEC, pkvr], [r_Sf])
                            if n % 2 == 1:
                                dma("sp", stret_d[l, 0, n // 2, H0:H0 + 2].rearrange("h d e -> d h e"), Sf[:], [r_Sf], [], r_Sf)
                            if n % 4 == 3:
                                g = n // 4
                                for hh in range(2):
                                    po, por = obanks[hh]
                                    act(sqn[:], po[:], AF.Square, [por], [r_sqn])
                                    pss, pssr = sbank
                                    mm(pss[:], onesb[:], sqn[:], True, True, [r_sqn, r_const], [pssr])
                                    ts("dve", rn_[:], pss[:], 1.0 / 128, ALU.mult, [pssr], [r_rn], s2=EPS, op1=ALU.add)
                                    act(rn_[:], rn_[:], AF.Ln, [r_rn], [r_rn])
                                    act(rn_[:], rn_[:], AF.Exp, [r_rn], [r_rn], scale=-0.5)
                                    tt("dve", y1[:], po[:], rn_[:], ALU.mult, [por, r_rn], [r_y1])
                                    pg, pgr = PSF[2 + hh]
                                    for k in range(8):
                                        mm(pg[:], WB[:, k, 256 + hh * 128:256 + (hh + 1) * 128], hT[:, k, g * 512:(g + 1) * 512], k == 0, k == 7, [rWB, hr[k][g]], [pgr])
                                    act(sgt[:], pg[:], AF.Silu, [pgr], [r_sgt])
                                    tt("pool", ymix[:, hh, g * 512:(g + 1) * 512], y1[:], sgt[:], ALU.mult, [r_y1, r_sgt], [r_y])
                        out_proj(l, hp * 256, 2, ymix, r_y)
                        P.barrier()

        sv8, r_sv8 = SB([128, 16], F32, "sv8")

        def proj_conv(W, rW, wc, taps, b_ap, dst, dst_r, zfull, r_z):
            for g in range(4):
                pst, psr = bank()
                for k in range(8):
                    mm(pst[:], W[:, k, wc * 128:(wc + 1) * 128], hT[:, k, g * 512:(g + 1) * 512], k == 0, k == 7, [rW, hr[k][g]], [psr])
                cp("act", zfull[:, g * 512:(g + 1) * 512], pst[:], [psr], [r_z])
            for off, wap in taps:
                if off == 0:
                    act(dst, zfull[:], AF.Identity, [r_z, r_const], [dst_r], scale=wap, bias=b_ap)
            nf8 = ctab[:, CT["nf8"]:CT["nf8"] + 8]
            nl8 = ctab[:, CT["nl8"]:CT["nl8"] + 8]
            for off, wap in taps:
                if off == 0:
                    continue
                if off < 0:
                    o = -off
                    cols = [bcol(zfull[:], 255 - i, "z") for i in range(o)]
                    mk = nl8
                else:
                    cols = [bcol(zfull[:], 0, "z")]
                    mk = nf8
                for i, cl in enumerate(cols):
                    cp("dve", sv8[:, i * 8:(i + 1) * 8], cl, [r_z], [r_sv8])
                    tt("dve", cl, sv8[:, i * 8:(i + 1) * 8], mk, ALU.mult, [r_sv8, r_const], [r_z])
                if off < 0:
                    stt(dst[:, o:T], zfull[:, 0:T - o], wap, dst[:, o:T], ALU.mult, ALU.add, [r_z, r_const, dst_r], [dst_r])
                else:
                    stt(dst[:, 0:T - off], zfull[:, off:T], wap, dst[:, 0:T - off], ALU.mult, ALU.add, [r_z, r_const, dst_r], [dst_r])
                for i, cl in enumerate(cols):
                    cp("dve", cl, sv8[:, i * 8:(i + 1) * 8], [r_sv8, dst_r], [r_z])

        def rglru(l):
            with scope() as ls:
                ymix, r_y = SB([128, 4, T], BF16, "ymixl", ls)
                S = [SB([128, T], F32, "S%d" % i, ls) for i in range(6)]
                ub, r_ub = SB([128, T], BF16, "ub", ls)
                gwt = [SB([128, 128], BF16, "gw", ls) for _ in range(16)]
                scv, r_scv = SB([128, 16], F32, "scv", ls)
                stc, r_stc = SB([128, 8], F32, "stc", ls)
                sgl, r_sgl = SB([128, 512], F32, "sgl", ls)
                act(scv[:, 0:8], pc("lru_lam", l * 8, 8), AF.Exp, [r_const], [r_scv], scale=-1.0)
                ts("dve", scv[:, 0:8], scv[:, 0:8], 1.0, ALU.add, [r_scv], [r_scv])
                act(scv[:, 0:8], scv[:, 0:8], AF.Ln, [r_scv], [r_scv])
                ts("dve", scv[:, 8:16], scv[:, 0:8], -16.0, ALU.mult, [r_scv], [r_scv])
                ts("dve", scv[:, 0:8], scv[:, 0:8], -8.0, ALU.mult, [r_scv], [r_scv])
                Wx, rWx = w_in_cols(l, 4096)
                Wgl, rWgl = w_in_cols(l, 4608)
                for gt, gr in gwt:
                    P.op("pool", lambda e, o=gt: e.memset(o[:], 0.0), [], [gr])
                for c_ in range(4):
                    for d_ in range(2):
                        for gi_ in range(2):
                            gt, gr = gwt[(c_ * 2 + d_) * 2 + gi_]
                            for bb in range(2):
                                dma("pool", gt[bb * 64:(bb + 1) * 64, bb * 64:(bb + 1) * 64], lgw_d[l, d_, gi_, c_ * 2 + bb], [], [gr], gr)
                nf8 = ctab[:, CT["nf8"]:CT["nf8"] + 8]
                nl8 = ctab[:, CT["nl8"]:CT["nl8"] + 8]
                for c in range(4):
                    (zfull, r_z), (u, r_u), (scr, r_scr), (h1, r_h1), (h2, r_h2), (rg1, r_rg1) = S
                    taps = [(j - 2, pc("lru_cw", (l * 4 + j) * 4 + c)) for j in range(4)]
                    proj_conv(Wx, rWx, c, taps, pc("lru_cb", l * 4 + c), u[:], r_u, zfull, r_z)
                    cp("pool", ub[:], u[:], [r_u], [r_ub])
                    for d in range(2):
                        gws = []
                        for gi in range(2):
                            gws.append(gwt[(c * 2 + d) * 2 + gi])
                        rg, r_rg = (zfull, r_z) if d == 0 else (rg1, r_rg1)
                        ig, r_ig = (h2, r_h2)
                        for gi, (dstt, dstr) in enumerate(((rg, r_rg), (ig, r_ig))):
                            for g in range(4):
                                pst, psr = bank()
                                mm(pst[:], gws[gi][0][:], ub[:, g * 512:(g + 1) * 512], True, True, [gws[gi][1], r_ub], [psr])
                                act(dstt[:, g * 512:(g + 1) * 512], pst[:], AF.Sigmoid, [psr, r_const], [dstr],
                                    bias=pc("lru_gb", ((l * 2 + d) * 2 + gi) * 4 + c))
                        act(scr[:], rg[:], AF.Exp, [r_rg, r_scv], [r_scr], scale=scv[:, 8 + d * 4 + c:9 + d * 4 + c])
                        act(rg[:], rg[:], AF.Exp, [r_rg, r_scv], [r_rg], scale=scv[:, d * 4 + c:d * 4 + c + 1])
                        ts("dve", scr[:], scr[:], 1.0, ALU.min, [r_scr], [r_scr], s2=-1.0, op1=ALU.mult)
                        act(scr[:], scr[:], AF.Sqrt, [r_scr, r_const], [r_scr], bias=ones32[:, 0:1])
                        tt("dve", scr[:], scr[:], ig[:], ALU.mult, [r_scr, r_ig], [r_scr])
                        tt("pool", scr[:], scr[:], u[:], ALU.mult, [r_scr, r_u], [r_scr])
                        h0 = pc("s0lru", (l * 2 + d) * 4 + c)
                        if d == 0:
                            cl = bcol(rg[:], 0, "a")
                            tt("dve", cl, cl, nf8, ALU.mult, [r_rg, r_const], [r_rg])
                            P.op("dve", lambda e, o=h1, a=rg, b=scr, i0=h0: e.tensor_tensor_scan(out=o[:], data0=a[:], data1=b[:], initial=i0, op0=ALU.mult, op1=ALU.add),
                                 [r_rg, r_scr, r_const], [r_h1])
                            cp("dve", stc[:], bcol(h1[:], 255, "h"), [r_h1], [r_stc])
                        else:
                            cl = bcol(rg[:], 255, "a")
                            tt("dve", cl, cl, nl8, ALU.mult, [r_rg, r_const], [r_rg])

                            def rev(t_):
                                xx = t_[:, :]
                                return AP(xx.tensor, xx.offset + T - 1, [list(xx.ap[0]), [-1, T]])
                            P.op("dve", lambda e, o=rev(h2), a=rev(rg), b=rev(scr), i0=h0: e.tensor_tensor_scan(out=o, data0=a, data1=b, initial=i0, op0=ALU.mult, op1=ALU.add),
                                 [r_rg, r_scr, r_const, r_ig], [r_h2])
                            cp("dve", stc[:], bcol(h2[:], 0, "h"), [r_h2], [r_stc])
                            tt("pool", h1[:], h1[:], h2[:], ALU.add, [r_h1, r_h2], [r_h1])
                        dma("sp", stlru_d[l, d, c], stc[:], [r_stc], [], r_stc)
                    for g in range(4):
                        pst, psr = bank()
                        for k in range(8):
                            mm(pst[:], Wgl[:, k, c * 128:(c + 1) * 128], hT[:, k, g * 512:(g + 1) * 512], k == 0, k == 7, [rWgl, hr[k][g]], [psr])
                        act(sgl[:], pst[:], AF.Silu, [psr], [r_sgl])
                        tt("dve", ymix[:, c, g * 512:(g + 1) * 512], h1[:, g * 512:(g + 1) * 512], sgl[:], ALU.mult, [r_h1, r_sgl], [r_y])
                out_proj(l, 1024, 4, ymix, r_y)
                P.barrier()

        def hyena(l):
            with scope() as ls:
                hidb, r_hid = SB([64, T], BF16, "hidb", ls)
                w3, r_w3 = SB([64, 2048], BF16, "w3", ls)
                dma("pool", w3[:], hw3_d[l], [], [r_w3], r_w3)
                with scope() as fs_:
                    zTt, r_zT = SB([33, T], F32, "zT", fs_)
                    w1, r_w1 = SB([33, 64], F32, "w1", fs_)
                    w2, r_w2 = SB([64, 64], F32, "w2", fs_)
                    fb, r_fb = SB([64, 4], F32, "fb", fs_)
                    frh, r_frh = SB([64, 4], F32, "frh", fs_)
                    sa, r_sa = SB([64, 512], F32, "sa", fs_)
                    sb4, r_sb4 = SB([64, 512], F32, "sb4", fs_)
                    hid1, r_hid1 = SB([64, T], F32, "hid1", fs_)
                    dma("sp", zTt[:], zT_d, [], [r_zT], r_zT)
                    dma("sp", w1[:], hw1_d[l], [], [r_w1], r_w1)
                    dma("sp", w2[:], hw2_d[l], [], [r_w2], r_w2)
                    for i, bn in enumerate(("hy_b1", "hy_b2")):
                        tt("dve", fb[:, i:i + 1], pc(bn, l, 1, 64), pc("hy_fr", l * 2 + i, 1, 64), ALU.mult, [r_const], [r_fb])
                        ts("dve", fb[:, 2 + i:3 + i], fb[:, i:i + 1], 0.25, ALU.mult, [r_fb], [r_fb])
                        ts("dve", fb[:, i:i + 1], fb[:, i:i + 1], 0.5, ALU.mult, [r_fb], [r_fb])
                        ts("dve", frh[:, i:i + 1], pc("hy_fr", l * 2 + i, 1, 64), 0.5, ALU.mult, [r_const], [r_frh])
                        ts("dve", frh[:, 2 + i:3 + i], pc("hy_fr", l * 2 + i, 1, 64), 0.25, ALU.mult, [r_const], [r_frh])

                    def sin_layer(i, lhsT, lr, rhs_t, rr, K, out_t, out_r):
                        for g in range(4):
                            pst, psr = bank()
                            mm(pst[0:64, :], lhsT, rhs_t[0:K, g * 512:(g + 1) * 512], True, True, [lr, rr], [psr])
                            act(sa[:], pst[0:64, :], AF.Sin, [psr, r_frh, r_fb], [r_sa], scale=frh[:, i:i + 1], bias=fb[:, i:i + 1])
                            act(sb4[:], pst[0:64, :], AF.Sin, [psr, r_frh, r_fb], [r_sb4], scale=frh[:, 2 + i:3 + i], bias=fb[:, 2 + i:3 + i])
                            tt("dve", sb4[:], sb4[:], sb4[:], ALU.mult, [r_sb4], [r_sb4])
                            ts("dve", sb4[:], sb4[:], -4.0, ALU.mult, [r_sb4], [r_sb4], s2=2.0, op1=ALU.add)
                            tt("dve", out_t[:, g * 512:(g + 1) * 512], sa[:], sb4[:], ALU.mult, [r_sa, r_sb4], [out_r])
                    sin_layer(0, w1[:], r_w1, zTt, r_zT, 33, hid1, r_hid1)
                    sin_layer(1, w2[:], r_w2, hid1, r_hid1, 64, hidb, r_hid)
                    P.barrier()

                Wv_, Wx1, Wx2, Wg_ = 2048, 2560, 3072, 3584

                for hh in range(2):
                    with scope() as hs:
                        Y, r_Y = SB([128, 32, 256], BF16, "Y", hs)
                        ufm, r_ufm = SB([128, 2, T], BF16, "ufm", hs)
                        ntn, r_ntn = SB([128, NT], F32, "ntn", hs)
                        ts("dve", ntn[:], pc("tnorm", 0, NT), -1.0, ALU.mult, [r_const], [r_ntn])

                        def conv_one(W, rW, cc, tapbase, cacc, r_cacc, zfull, r_z):
                            c = hh * 2 + cc
                            ch12 = tapbase * 4 + c
                            taps = [(j - 1, pc("hy_cw", (l * 3 + j) * 12 + ch12)) for j in range(3)]
                            proj_conv(W, rW, c, taps, pc("hy_cb", l * 12 + ch12), cacc[:], r_cacc, zfull, r_z)

                        def spectral(o):
                            with scope() as s2:
                                FU, r_FU = SB([128, NT, 768], BF16, "FU", s2)
                                for n in range(NT):
                                    pb, pbr = PSB[n % 2]
                                    for cc in range(2):
                                        tr(pb[:, cc * 128:(cc + 1) * 128], ufm[:, cc, n * 128:(n + 1) * 128], [r_ufm], [pbr])
                                    cp("act", FU[:, n, 512:768], pb[:, 0:256], [pbr], [r_FU])
                                with scope() as s3:
                                    decs = [SB([128, 256], F32, "dec", s3) for _ in range(2)]
                                    abs_ = [SB([128, 512], BF16, "ab", s3) for _ in range(3)]
                                    tas = [SB([128, 256], F32, "ta", s3) for _ in range(2)]
                                    tbs = [SB([128, 256], F32, "tb", s3) for _ in range(2)]
                                    rn_, r_rn = SB([128, 512], F32, "rnh", s3)
                                    fun_r = [Res("fun%d" % n) for n in range(NT)]
                                    cf = o * 1024 + hh * 256
                                    cb_ = o * 1024 + 512 + hh * 256
                                    pn, pnr = PSF[5]
                                    for n in range(NT + 2):
                                        if n < NT:
                                            pst, psr = PSF[n % 4]
                                            dec, r_dec = decs[n % 2]
                                            ab, r_ab = abs_[n % 3]
                                            mm(pst[:, 0:256], hidb[0:64, n * 128:(n + 1) * 128], w3[0:64, cf:cf + 256], True, True, [r_hid, r_w3], [psr])
                                            mm(pst[:, 256:512], hidb[0:64, n * 128:(n + 1) * 128], w3[0:64, cb_:cb_ + 256], True, True, [r_hid, r_w3], [psr])
                                            act(dec[:], ctab[:, CT["delta"] + hh * 256:CT["delta"] + hh * 256 + 256], AF.Exp, [r_const, r_ntn], [r_dec], scale=ntn[:, n:n + 1])
                                            tt("dve", FU[:, n, 0:512].rearrange("p (d c) -> p d c", d=2), pst[:].rearrange("p (d c) -> p d c", d=2),
                                               dec[:].unsqueeze(1).to_broadcast([128, 2, 256]), ALU.mult, [psr, r_dec], [fun_r[n], r_FU])
                                            act(ab[:], FU[:, n, 0:512], AF.Abs, [fun_r[n]], [r_ab])
                                        m = n - 2
                                        if m >= 0:
                                            ab, r_ab = abs_[m % 3]
                                            mm(pn[:], normwb[:], ab[:], m == 0, m == NT - 1, [r_const, r_ab], [pnr])
                                    ts("dve", rn_[:], pn[:], EPS, ALU.add, [pnr], [r_rn])
                                    P.op("dve", lambda e, r=rn_: e.reciprocal(out=r[:], in_=r[:]), [r_rn], [r_rn])
                                    for n in range(NT):
                                        ta, r_ta = tas[n % 2]
                                        tb, r_tb = tbs[n % 2]
                                        tt("dve", ta[:], FU[:, n, 0:256], rn_[:, 0:256], ALU.mult, [fun_r[n], r_rn], [r_ta])
                                        tt("dve", tb[:], FU[:, n, 256:512], rn_[:, 256:512], ALU.mult, [fun_r[n], r_rn], [r_tb])
                                        tt("dve", FU[:, n, 0:256], ta[:], tb[:], ALU.add, [r_ta, r_tb], [fun_r[n], r_FU])
                                        tt("pool", FU[:, n, 256:512], ta[:], tb[:], ALU.subtract, [r_ta, r_tb], [fun_r[n], r_FU])
                                with scope() as s4:
                                    fstr = [SB([128, NT, 128], BF16, "fstr", s4) for _ in range(NFS)]
                                    U0, r_U0 = SB([128, 512], F32, "U0", s4)
                                    U1, r_U1 = SB([128, 512], F32, "U1", s4)
                                    t1, r_t1 = SB([128, 256], F32, "t1", s4)
                                    t2, r_t2 = SB([128, 256], F32, "t2", s4)
                                    for j in range(16):
                                        fr_, frr = fstr[(2 * j) % NFS]
                                        fi_, fir = fstr[(2 * j + 1) % NFS]
                                        dma("sp", fr_[:].rearrange("p n r -> p (n r)"), dftF_d[j], [], [frr], frr)
                                        dma("sp", fi_[:].rearrange("p n r -> p (n r)"), dftF_d[16 + j], [], [fir], fir)
                                        pre, prer = PSF[(2 * j) % 6]
                                        pim, pimr = PSF[(2 * j + 1) % 6]
                                        for n in range(NT):
                                            mm(pre[:, 0:256], fr_[:, n, :], FU[:, n, 0:256], n == 0, n == NT - 1, [frr, r_FU], [prer])
                                        for n in range(NT):
                                            mm(pre[:, 256:512], fr_[:, n, :], FU[:, n, 512:768], n == 0, n == NT - 1, [frr, r_FU], [prer])
                                        for n in range(NT):
                                            mm(pim[:], fi_[:, n, :], FU[:, n, 256:768], n == 0, n == NT - 1, [fir, r_FU], [pimr])
                                        cp("act", U0[:], pre[:], [prer], [r_U0])
                                        cp("act", U1[:], pim[:], [pimr], [r_U1])
                                        tt("dve", t1[:], U0[:, 256:512], U0[:, 0:256], ALU.mult, [r_U0], [r_t1])
                                        tt("pool", t2[:], U1[:, 256:512], U1[:, 0:256], ALU.mult, [r_U1], [r_t2])
                                        tt("dve", Y[:, j, :], t1[:], t2[:], ALU.subtract, [r_t1, r_t2], [r_Y])
                                        tt("dve", t1[:], U0[:, 256:512], U1[:, 0:256], ALU.mult, [r_U0, r_U1], [r_t1])
                                        tt("pool", t2[:], U1[:, 256:512], U0[:, 0:256], ALU.mult, [r_U0, r_U1], [r_t2])
                                        tt("dve", Y[:, 16 + j, :], t1[:], t2[:], ALU.add, [r_t1, r_t2], [r_Y])

                        def inverse(Wpre, tapbase, o, gate):
                            with scope() as s5:
                                istr = [SB([128, T], BF16, "istr", s5) for _ in range(4)]
                                zfull, r_z = SB([128, T], F32, "zf", s5)
                                cacc, r_cacc = SB([128, T], F32, "cacc", s5)
                                xc, r_xc = SB([128, 2, T], BF16, "xc", s5)
                                ev1, r_ev1 = SB([128, 512], F32, "ev1", s5)
                                sgh, r_sgh = SB([128, 512], F32, "sgh", s5)
                                (W, rW), gpre = Wpre
                                if gate:
                                    Wg, rWg = gpre
                                for cc in range(2):
                                    c = hh * 2 + cc
                                    conv_one(W, rW, cc, tapbase, cacc, r_cacc, zfull, r_z)
                                    if gate:
                                        for g in range(4):
                                            sl = slice(g * 512, (g + 1) * 512)
                                            pg, pgr = bank()
                                            for k in range(8):
                                                mm(pg[:], Wg[:, k, c * 128:(c + 1) * 128], hT[:, k, sl], k == 0, k == 7, [rWg, hr[k][g]], [pgr])
                                            act(sgh[:], pg[:], AF.Silu, [pgr], [r_sgh])
                                            tt("dve", xc[:, cc, sl], cacc[:, sl], sgh[:], ALU.mult, [r_cacc, r_sgh], [r_xc])
                                    else:
                                        cp("pool", xc[:, cc, :], cacc[:], [r_cacc], [r_xc])
                                for j in range(32):
                                    it, itr = istr[j % 4]
                                    dma("sp", it[:], dftI_d[j], [], [itr], itr)
                                    for cc in range(2):
                                        for g in range(4):
                                            pst, psr = PSF[cc * 4 + g]
                                            mm(pst[:], Y[:, j, cc * 128:(cc + 1) * 128], it[:, g * 512:(g + 1) * 512], j == 0, j == 31, [r_Y, itr], [psr])
                                for cc in range(2):
                                    for g in range(4):
                                        sl = slice(g * 512, (g + 1) * 512)
                                        pst, psr = PSF[cc * 4 + g]
                                        stt(ev1[:], ufm[:, cc, sl], pc("hy_bias", (l * 2 + o) * 4 + hh * 2 + cc), pst[:], ALU.mult, ALU.add, [r_ufm, r_const, psr], [r_ev1])
                                        tt("pool", ufm[:, cc, sl], ev1[:], xc[:, cc, sl], ALU.mult, [r_ev1, r_xc], [r_ufm])

                        with scope() as s1:
                            zfull, r_z = SB([128, T], F32, "zf", s1)
                            cacc, r_cacc = SB([128, T], F32, "cacc", s1)
                            W, rW = w_in_cols(l, Wv_)
                            for cc in range(2):
                                conv_one(W, rW, cc, 0, cacc, r_cacc, zfull, r_z)
                                cp("pool", ufm[:, cc, :], cacc[:], [r_cacc], [r_ufm])
                        pre1 = (w_in_cols(l, Wx1), None)
                        spectral(0)
                        inverse(pre1, 1, 0, False)
                        pre2 = (w_in_cols(l, Wx2), w_in_cols(l, Wg_))
                        spectral(1)
                        inverse(pre2, 2, 1, True)
                        out_proj(l, 512 + hh * 256, 2, ufm, r_ufm)

        for l in range(DEPTH):
            mod_and_norm(l)
            if STAGE in ("all", "ret"):
                retention(l)
            if STAGE in ("all", "lru"):
                rglru(l)
            if STAGE in ("all", "hy"):
                hyena(l)
        with scope() as ls:
            outs = [SB([128, 512], F32, "ob", ls) for _ in range(4)]
            oi = [0]
            sqs = [SB([128, 512], BF16, "sq", ls) for _ in range(2)]
            rstds = [SB([128, 512], F32, "rstd", ls) for _ in range(2)]
            for g in range(4):
                rstd, r_rstd = rstds[g % 2]
                pst, psr = bank()
                for k in range(8):
                    sq, sqr = sqs[k % 2]
                    act(sq[:], x[:, k, g * 512:(g + 1) * 512], AF.Square, [xr[k][g]], [sqr])
                    mm(pst[:], onesb[:], sq[:], k == 0, k == 7, [sqr, r_const], [psr])
                ts("dve", rstd[:], pst[:], 1.0 / 1024, ALU.mult, [psr], [r_rstd], s2=EPS, op1=ALU.add)
                act(rstd[:], rstd[:], AF.Ln, [r_rstd], [r_rstd])
                act(rstd[:], rstd[:], AF.Exp, [r_rstd], [r_rstd], scale=-0.5)
                for k in range(8):
                    ob, obr = outs[oi[0] % 4]
                    oi[0] += 1
                    stt(ob[:], x[:, k, g * 512:(g + 1) * 512], pc("final_g", k), rstd[:], ALU.mult, ALU.mult, [xr[k][g], r_const, r_rstd], [obr])
                    dma("sp", yT_d[k * 128:(k + 1) * 128, g * 512:(g + 1) * 512], ob[:], [obr], [], obr)
            P.barrier()
        P.emit()
    return nc


def _bf16(a):
    return np.asarray(a, dtype=np.float32).astype(ml_dtypes.bfloat16)


def _dft_mats(L, nseq):
    N = 2 * L
    t = np.arange(L, dtype=np.int64)
    k = np.arange(L, dtype=np.int64)
    ph = ((2 * k[None, :] + 1) * t[:, None]) % (2 * N)
    ang = ph.astype(np.float64) * (math.pi / N)
    C = np.cos(ang)
    S = -np.sin(ang)
    Fwd = np.zeros((T, 2 * T), np.float32)
    for b in range(nseq):
        Fwd[b * L:(b + 1) * L, b * L:(b + 1) * L] = C
        Fwd[b * L:(b + 1) * L, T + b * L:T + (b + 1) * L] = S
    dftF = np.ascontiguousarray(Fwd.reshape(NT, 128, 32, 128).transpose(2, 1, 0, 3)).reshape(32, 128, NT * 128)
    Inv = (Fwd.T * (2.0 / N)).astype(np.float32)
    dftI = np.ascontiguousarray(Inv.reshape(32, 128, T))
    return _bf16(dftF), _bf16(dftI)


def _core_consts(L, nseq, rope_on):
    f32 = np.float32
    pos_in = np.arange(T) % L
    m_int = 1.0 if L == T else 0.0
    nf8 = np.array([1.0] + [m_int] * 7, f32)
    nl8 = np.array([m_int] * 7 + [1.0], f32)
    masks = (nf8, nl8)
    if rope_on:
        rows = T // 64
        row = np.repeat(np.arange(rows, dtype=f32), 64)
        col = np.tile(np.arange(64, dtype=f32), rows)
        inv = (f32(10000.0) ** (-np.arange(32, dtype=f32) / f32(32))).astype(f32)
        ang = np.concatenate([row[:, None] * inv[None], col[:, None] * inv[None]], axis=-1).astype(f32)
        c, s = np.cos(ang).astype(f32), np.sin(ang).astype(f32)
    else:
        c, s = np.ones((T, 64), f32), np.zeros((T, 64), f32)
    ropec = np.ascontiguousarray(c.reshape(NT, 128, 64).transpose(1, 0, 2)).reshape(128, NT * 64)
    ropes = np.ascontiguousarray(s.reshape(NT, 128, 64).transpose(1, 0, 2)).reshape(128, NT * 64)
    tl = np.linspace(0.0, 1.0, L, dtype=f32)
    f = np.linspace(1e-4, 15.0, 16, dtype=f32)
    ang = (f32(2.0 * math.pi / L) * np.arange(L, dtype=f32)[:, None] * f[None, :]).astype(f32)
    z = np.concatenate([tl[:, None], np.cos(ang), -np.sin(ang)], axis=-1).astype(f32)
    zT = np.ascontiguousarray(np.tile(z, (nseq, 1)).T)
    tnorm = np.tile(tl, nseq).reshape(NT, 128).T
    cm = np.ones((2, NT), f32)
    cpl = L // 128
    for n in range(NT):
        if n % cpl == 0 and n != 0:
            cm[0, n] = 0.0
        if n % cpl == cpl - 1 and n != NT - 1:
            cm[1, n] = 0.0
    return masks, ropec, ropes, zT, np.ascontiguousarray(tnorm), cm, 1.0 / nseq


def _cols(a):
    a = np.asarray(a, np.float32)
    lead = int(np.prod(a.shape[:-1])) if a.ndim > 1 else 1
    n = a.shape[-1] // 128
    return np.ascontiguousarray(a.reshape(lead, n, 128).transpose(2, 0, 1)).reshape(128, lead * n)


_PROG = {}


def kernel(x_prompt, x_sample, state_ret, state_lru, c, c_ctx, norm_g, ada_w, ada_b, w_in,
           ret_decay_logit, hy_conv_w, hy_conv_b, hy_ffn_w1, hy_ffn_b1, hy_ffn_w2, hy_ffn_b2,
           hy_ffn_w3, hy_freq, hy_bias, lru_conv_w, lru_conv_b, lru_gate_w, lru_gate_b,
           lru_lambda, w_out, final_g):
    f32 = np.float32
    A = lambda a: np.ascontiguousarray(np.asarray(a, f32))
    x_prompt, x_sample = A(x_prompt), A(x_sample)
    ncores = 8
    if "nc" not in _PROG:
        _PROG["nc"] = build_program()
    nc = _PROG["nc"]
    dft_s = _dft_mats(2048, 1)
    dft_p = _dft_mats(256, 8)
    cc_s = _core_consts(2048, 1, True)
    cc_p = _core_consts(256, 8, False)
    identb = _bf16(np.eye(128))
    idx = np.arange(128, dtype=f32)
    diff = idx[None, :] - idx[:, None]
    s = f32(128.0 ** -0.5)
    hcw = np.asarray(hy_conv_w, f32)
    deltas = np.abs(np.linspace(math.log(1e-2) / 1.5, math.log(1e-2) / 0.3, 512, dtype=f32)).astype(f32)

    def pcol_for(cv, s0l):
        pc_ = np.zeros((128, NPC), f32)

        def put(name, arr):
            arr = np.asarray(arr, f32)
            pc_[:arr.shape[0], PC[name]:PC[name] + arr.shape[1]] = arr
        put("cvec", _cols(cv))
        put("normg", _cols(norm_g))
        put("ada_b", _cols(ada_b))
        put("final_g", _cols(final_g))
        put("hy_cw", _cols(hcw))
        put("hy_cb", _cols(hy_conv_b))
        put("hy_bias", _cols(hy_bias))
        put("lru_cw", _cols(lru_conv_w))
        put("lru_cb", _cols(lru_conv_b))
        put("lru_gb", _cols(lru_gate_b))
        put("lru_lam", _cols(lru_lambda))
        put("s0lru", _cols(s0l))
        put("hy_b1", np.asarray(hy_ffn_b1, f32).T)
        put("hy_b2", np.asarray(hy_ffn_b2, f32).T)
        put("hy_fr", np.asarray(hy_freq, f32).reshape(4, 64).T)
        put("col127", (127.0 - idx)[:, None])
        put("colp", idx[:, None])
        return pc_

    def ctab_for(cm, nw, bm):
        ct = np.zeros((128, NCT), f32)
        ct[:, CT["pos"]:CT["pos"] + 128] = np.maximum(diff, 0)
        ct[:, CT["neg"]:CT["neg"] + 128] = np.maximum(-diff, 0)
        ct[:, CT["eyes"]:CT["eyes"] + 128] = np.eye(128, dtype=f32) * s
        ct[:, CT["iota1"]:CT["iota1"] + 128] = (idx + 1)[None, :]
        ct[:, CT["rev"]:CT["rev"] + 128] = (128 - idx)[None, :]
        ct[:, CT["delta"]:CT["delta"] + 512] = deltas[None, :]
        ct[:, CT["logit"]:CT["logit"] + 16] = np.asarray(ret_decay_logit, f32).reshape(1, 16)
        ct[:, CT["cmask"]:CT["cmask"] + 32] = cm.reshape(1, 32)
        ct[:, CT["normw"]:CT["normw"] + 128] = nw
        ct[:, CT["nf8"]:CT["nf8"] + 8] = bm[0][None, :]
        ct[:, CT["nl8"]:CT["nl8"] + 8] = bm[1][None, :]
        return ct

    shared = dict(ada_w=A(ada_w), w_in=A(w_in), w_out=A(w_out), hy_ffn_w1=A(hy_ffn_w1), hy_ffn_w2=A(hy_ffn_w2),
                  hy_ffn_w3=A(hy_ffn_w3), lru_gate_w=A(lru_gate_w), identb=identb)
    in_maps = []
    for core in range(ncores):
        if core < 4:
            xs = x_sample[core]
            cv = np.asarray(c, f32)[core]
            s0r = A(state_ret)[core]
            s0l = np.asarray(state_lru, f32)[core]
            cc, dft = cc_s, dft_s
        else:
            pb = (core - 4) % 2
            xs = x_prompt[pb * 8:(pb + 1) * 8].reshape(T, 1024)
            cv = np.asarray(c_ctx, f32)
            s0r = np.zeros((DEPTH, 2, 4, 128, 128), f32)
            s0l = np.zeros((DEPTH, 2, 512), f32)
            cc, dft = cc_p, dft_p
        masks, ropec, ropes, zT, tnorm, cm, nw = cc
        pc_ = pcol_for(cv, s0l)
        pc_[:, PC["tnorm"]:PC["tnorm"] + 16] = tnorm
        m = dict(shared)
        m.update(xT=np.ascontiguousarray(xs.T), pcol=pc_, ctab=ctab_for(cm, nw, masks), ropec=ropec, ropes=ropes,
                 s0ret=np.ascontiguousarray(s0r), dftF=dft[0], dftI=dft[1], zT=zT)
        in_maps.append(m)
    res = run_bass_kernel_spmd(nc, in_maps, core_ids=list(range(ncores)))
    R = res.results
    y_sample = np.stack([np.ascontiguousarray(R[j]["yT"].T) for j in range(4)]).astype(f32)
    y_prompt = np.concatenate([np.ascontiguousarray(R[4 + pb]["yT"].T).reshape(8, 256, 1024) for pb in range(2)]).astype(f32)
    nsr = np.concatenate([R[4 + pb]["st_ret"].transpose(2, 0, 1, 3, 4, 5) for pb in range(2)]).astype(f32)
    nsl = np.concatenate([R[4 + pb]["st_lru"].transpose(4, 0, 1, 2, 3).reshape(8, DEPTH, 2, 512) for pb in range(2)]).astype(f32)
    return (y_prompt, y_sample, np.ascontiguousarray(nsr), np.ascontiguousarray(nsl))
```

```python
import contextlib
import math
import os
import numpy as np
import ml_dtypes
import concourse.bass as bass
import concourse.mybir as mybir
from concourse.bass_utils import run_bass_kernel_spmd
from concourse.ap import AP

F32 = mybir.dt.float32
BF16 = mybir.dt.bfloat16
ALU = mybir.AluOpType
AF = mybir.ActivationFunctionType

EPOCH = 16000
SAME_ENGINE_SYNC = True
EPS = 1e-6
T = 2048
NT = 16
NFS = 4
DEPTH = 2
STAGE = os.environ.get("KSTAGE", "all")


class Res:
    __slots__ = ("name", "w", "r", "dsem", "dcnt", "dkind")

    def __init__(self, name):
        self.name = name
        self.w = None
        self.r = {}
        self.dsem = None
        self.dcnt = 0
        self.dkind = None


class Prog:
    ENG = ("pe", "act", "dve", "pool", "sp")

    def __init__(self, nc, es):
        self.nc = nc
        self.es = es
        self.ops = {e: [] for e in self.ENG}
        self.sem = {}
        self.cnt = {e: 0 for e in self.ENG}
        self.seen = {e: {} for e in self.ENG}
        self.nsem = 0
        self.dres = []
        self.sempool = {"sw": [], "hw": []}
        self.allsems = []
        for e in self.ENG:
            self._new_epoch(e)

    def new_sem(self, name):
        self.nsem += 1
        return self.es.enter_context(self.nc.semaphore("%s_%d" % (name, self.nsem)))

    def _new_epoch(self, e):
        self.sem[e] = self.new_sem("e_" + e)
        self.cnt[e] = 0
        self.allsems.append((e, self.sem[e]))

    def _waits(self, eng, reads, writes):
        need = {}

        def add(ev):
            if ev is None:
                return
            sem, val, e = ev
            if e == eng and (eng == "pe" or not SAME_ENGINE_SYNC):
                return
            k = id(sem)
            if self.seen[eng].get(k, 0) >= val:
                return
            if k not in need or need[k][1] < val:
                need[k] = (sem, val)

        for r in reads:
            add(r.w)
        for w in writes:
            add(w.w)
            for ev in w.r.values():
                add(ev)
        out = list(need.values())
        for sem, val in out:
            self.seen[eng][id(sem)] = val
        return out

    def op(self, eng, fn, reads=(), writes=()):
        waits = self._waits(eng, reads, writes)
        if self.cnt[eng] >= EPOCH:
            self._new_epoch(eng)
        self.cnt[eng] += 1
        ev = (self.sem[eng], self.cnt[eng], eng)
        self.ops[eng].append((waits, fn, (self.sem[eng], 1)))
        for r in reads:
            r.r[id(ev[0])] = ev
        for w in writes:
            w.w = ev
            w.r = {}
        return ev

    def dma(self, eng, fn, reads=(), writes=(), sres=None):
        waits = self._waits(eng, reads, writes)
        kind = "sw" if eng == "pool" else "hw"
        if sres.dsem is None:
            if self.sempool[kind]:
                sres.dsem, sres.dcnt = self.sempool[kind].pop()
            else:
                sres.dsem = self.new_sem("d" + kind)
            sres.dkind = kind
            self.dres.append(sres)
        assert sres.dkind == kind, "semaphore shared between SW and HW DGE: %s" % sres.name
        sres.dcnt += 16
        ev = (sres.dsem, sres.dcnt, "dma")
        self.ops[eng].append((waits, fn, (sres.dsem, 16)))
        for r in reads:
            r.r[id(ev[0])] = ev
        for w in writes:
            w.w = ev
            w.r = {}
        return ev

    def release(self, res_list):
        for r in res_list:
            if r.dsem is not None:
                self.sempool[r.dkind].append((r.dsem, r.dcnt))
                if r in self.dres:
                    self.dres.remove(r)
                r.dsem = None

    def wait_event(self, eng, ev):
        sem, val, e = ev
        if self.seen[eng].get(id(sem), 0) >= val:
            return
        self.seen[eng][id(sem)] = val
        self.ops[eng].append(([(sem, val)], None, None))

    def barrier(self):
        evs = []
        for e in self.ENG:
            if self.cnt[e] > 0:
                evs.append((self.sem[e], self.cnt[e], e))
        for r in self.dres:
            evs.append((r.dsem, r.dcnt, "dma"))
        for e in self.ENG:
            for ev in evs:
                if ev[2] == e:
                    continue
                self.wait_event(e, ev)

    def emit(self):
        nc = self.nc
        with nc.Block() as block:
            def mk(e):
                def body(engh):
                    for waits, fn, inc in self.ops[e]:
                        for sem, val in waits:
                            engh.wait_ge(sem, val)
                        if fn is not None:
                            ins = fn(engh)
                            ins.then_inc(inc[0], inc[1])
                return body
            block.tensor(mk("pe"))
            block.scalar(mk("act"))
            block.vector(mk("dve"))
            block.gpsimd(mk("pool"))
            block.sync(mk("sp"))


def _pcol_layout():
    off = {}
    n = 0

    def add(name, w):
        nonlocal n
        off[name] = n
        n += w
    add("cvec", 8)
    add("normg", 16)
    add("ada_b", 48)
    add("final_g", 8)
    add("hy_cw", 72)
    add("hy_cb", 24)
    add("hy_bias", 16)
    add("lru_cw", 32)
    add("lru_cb", 8)
    add("lru_gb", 32)
    add("lru_lam", 16)
    add("s0lru", 16)
    add("hy_b1", 2)
    add("hy_b2", 2)
    add("hy_fr", 4)
    add("tnorm", 16)
    add("col127", 1)
    add("colp", 1)
    return off, n


PC, NPC = _pcol_layout()
CT = {"pos": 0, "neg": 128, "eyes": 256, "iota1": 384, "rev": 512, "delta": 640, "logit": 1152, "cmask": 1168,
      "normw": 1200, "nf8": 1328, "nl8": 1336}
NCT = 1344


def build_program():
    nc = bass.Bass("TRN2", target_bir_lowering=False)
    dI = lambda name, shape, dt=F32: nc.dram_tensor(name, list(shape), dt, kind="ExternalInput").ap()
    dO = lambda name, shape, dt=F32: nc.dram_tensor(name, list(shape), dt, kind="ExternalOutput").ap()
    xT_d = dI("xT", [1024, T])
    pcol_d = dI("pcol", [128, NPC])
    ctab_d = dI("ctab", [128, NCT])
    ropec_d = dI("ropec", [128, NT * 64])
    ropes_d = dI("ropes", [128, NT * 64])
    identb_d = dI("identb", [128, 128], BF16)
    s0ret_d = dI("s0ret", [DEPTH, 2, 4, 128, 128])
    dftF_d = dI("dftF", [32, 128, NT * 128], BF16)
    dftI_d = dI("dftI", [32, 128, T], BF16)
    zT_d = dI("zT", [33, T])
    ada_w_d = dI("ada_w", [DEPTH, 1024, 3072])
    w_in_d = dI("w_in", [DEPTH, 1024, 5120])
    w_out_d = dI("w_out", [DEPTH, 1536, 1024])
    hw1_d = dI("hy_ffn_w1", [DEPTH, 33, 64])
    hw2_d = dI("hy_ffn_w2", [DEPTH, 64, 64])
    hw3_d = dI("hy_ffn_w3", [DEPTH, 64, 2048])
    lgw_d = dI("lru_gate_w", [DEPTH, 2, 2, 8, 64, 64])
    yT_d = dO("yT", [1024, T])
    stret_d = dO("st_ret", [DEPTH, 2, 8, 4, 128, 128])
    stlru_d = dO("st_lru", [DEPTH, 2, 4, 128, 8])

    with contextlib.ExitStack() as es:
        P = Prog(nc, es)
        cnt = [0]

        def SB(shape, dt, name=None, stack=None):
            cnt[0] += 1
            nm = "%s_%d" % (name or "t", cnt[0])
            t = (stack or es).enter_context(nc.sbuf_tensor(nm, list(shape), dt))
            r = Res(nm)
            if stack is not None:
                if not hasattr(stack, "_res"):
                    stack._res = []
                stack._res.append(r)
            return t, r

        @contextlib.contextmanager
        def scope():
            st = contextlib.ExitStack()
            try:
                yield st
                P.barrier()
                P.release(getattr(st, "_res", []))
            finally:
                st.close()

        def mm(out, lhsT, rhs, start, stop, reads, writes):
            P.op("pe", lambda e, o=out, l=lhsT, r=rhs, s=start, t=stop: e.matmul(o, lhsT=l, rhs=r, start=s, stop=t),
                 reads, writes)

        def tr(out, in_, reads, writes):
            P.op("pe", lambda e, o=out, i=in_: e.transpose(out=o, in_=i, identity=identb[:]), list(reads) + [r_const], writes)

        def act(out, in_, func, reads, writes, scale=1.0, bias=None):
            if bias is None:
                P.op("act", lambda e, o=out, i=in_, f=func, s=scale: e.activation(out=o, in_=i, func=f, scale=s), reads, writes)
            else:
                P.op("act", lambda e, o=out, i=in_, f=func, s=scale, b=bias: e.activation(out=o, in_=i, func=f, scale=s, bias=b), reads, writes)

        def tt(eng, out, in0, in1, op, reads, writes):
            P.op(eng, lambda e, o=out, a=in0, b=in1, p=op: e.tensor_tensor(out=o, in0=a, in1=b, op=p), reads, writes)

        def ts(eng, out, in0, s1, op0, reads, writes, s2=None, op1=None):
            if op1 is None:
                P.op(eng, lambda e, o=out, a=in0, s=s1, p=op0: e.tensor_scalar(out=o, in0=a, scalar1=s, scalar2=None, op0=p), reads, writes)
            else:
                P.op(eng, lambda e, o=out, a=in0, s=s1, p=op0, s_2=s2, p1=op1: e.tensor_scalar(out=o, in0=a, scalar1=s, scalar2=s_2, op0=p, op1=p1), reads, writes)

        def stt(out, in0, scalar, in1, op0, op1, reads, writes):
            P.op("dve", lambda e, o=out, a=in0, s=scalar, b=in1, p0=op0, p1=op1: e.scalar_tensor_tensor(out=o, in0=a, scalar=s, in1=b, op0=p0, op1=p1), reads, writes)

        def cp(eng, out, in_, reads, writes):
            if eng == "act":
                P.op("act", lambda e, o=out, i=in_: e.copy(out=o, in_=i), reads, writes)
            else:
                P.op(eng, lambda e, o=out, i=in_: e.tensor_copy(out=o, in_=i), reads, writes)

        def dma(eng, out, in_, reads, writes, sres):
            P.dma(eng, lambda e, o=out, i=in_: e.dma_start(out=o, in_=i), reads, writes, sres)

        x, _ = SB([128, 8, T], F32, "x")
        xr = [[Res("x%d_%d" % (k, g)) for g in range(4)] for k in range(8)]
        hT, _ = SB([128, 8, T], BF16, "hT")
        hr = [[Res("h%d_%d" % (k, g)) for g in range(4)] for k in range(8)]
        pcol, r_const = SB([128, NPC], F32, "pcol")
        ctab, _ = SB([128, NCT], F32, "ctab")
        identb, _ = SB([128, 128], BF16, "identb")
        ones32, _ = SB([128, 128], F32, "ones32")
        onesb, _ = SB([128, 128], BF16, "onesb")
        normwb, _ = SB([128, 128], BF16, "normwb")
        modt, r_mod = SB([128, 24], F32, "modt")
        Avec, r_A = SB([128, 8], F32, "Avec")
        sc, r_sc = SB([128, 8], F32, "sc")
        LG, r_LG = SB([128, 16], F32, "LG")
        NW = 2
        wslots = [SB([128, 4096], BF16, "w") for _ in range(NW)]
        wnext = [0]
        PSF = []
        for i in range(8):
            t_ = es.enter_context(nc.psum_tensor("psf%d" % i, [128, 512], F32))
            PSF.append((t_, Res("psf%d" % i)))

        class _BV:
            def __init__(self, t_):
                self.v = t_[:].bitcast(BF16)

            def __getitem__(self, k):
                return self.v[k]
        PSB = [(_BV(PSF[6][0]), PSF[6][1]), (_BV(PSF[7][0]), PSF[7][1])]
        bank_i = [0]

        def bank():
            b = PSF[bank_i[0] % 6]
            bank_i[0] += 1
            return b

        def pc(name, i=0, n=1, parts=128):
            o = PC[name] + i
            return pcol[0:parts, o:o + n]

        dma("sp", pcol[:], pcol_d, [], [r_const], r_const)
        dma("sp", ctab[:], ctab_d, [], [r_const], r_const)
        dma("sp", identb[:], identb_d, [], [r_const], r_const)
        P.op("pool", lambda e: e.memset(ones32[:], 1.0), [], [r_const])
        P.op("pool", lambda e: e.memset(onesb[:], 1.0), [], [r_const])
        act(LG[:], ctab[:, CT["logit"]:CT["logit"] + 16], AF.Exp, [r_const], [r_LG], scale=-1.0)
        ts("dve", LG[:], LG[:], 1.0, ALU.add, [r_LG], [r_LG])
        act(LG[:], LG[:], AF.Ln, [r_LG], [r_LG])
        ts("dve", LG[:], LG[:], -1.0, ALU.mult, [r_LG], [r_LG])

        P.barrier()
        cp("dve", normwb[:], ctab[:, CT["normw"]:CT["normw"] + 128], [r_const], [r_const])
        P.barrier()
        for k in range(8):
            dma("sp", x[:, k, :], xT_d[k * 128:(k + 1) * 128, :], [], xr[k], xr[k][0])

        def wslot(view_k, ncols):
            i = wnext[0] % NW
            wnext[0] += 1
            wt, wr = wslots[i]
            v = wt[:, 0:view_k * ncols].rearrange("p (k c) -> p k c", k=view_k)
            return v, wr

        def w_in_src(l, c0, ncols):
            return w_in_d[l].rearrange("(k p) c -> p k c", p=128)[:, :, c0:c0 + ncols]

        def w_in_cols(l, c0, ncols=512):
            v, wr = wslot(8, ncols)
            dma("pool", v, w_in_src(l, c0, ncols), [], [wr], wr)
            return v, wr

        def bcol(t_, off, name):
            return t_.rearrange("p (s t) -> p s t", t=256)[:, :, off]

        def mod_and_norm(l):
            act(sc[:], pc("cvec", 0, 8), AF.Silu, [r_const], [r_sc])
            with scope() as ls:
                scb, r_scb = SB([128, 8], BF16, "scb", ls)
                cp("dve", scb[:], sc[:], [r_sc], [r_scb])
                aws = [SB([128, 8, 512], BF16, "aw", ls) for _ in range(3)]
                pst, psr = bank()
                for pi in range(6):
                    aw, awr = aws[pi % 3]
                    dma("pool", aw[:], ada_w_d[l].rearrange("(k p) c -> p k c", p=128)[:, :, pi * 512:(pi + 1) * 512], [], [awr], awr)
                    for jj in range(4):
                        j = pi * 4 + jj
                        for k in range(8):
                            mm(pst[:, j:j + 1], aw[:, k, jj * 128:(jj + 1) * 128], scb[:, k:k + 1], k == 0, k == 7, [awr, r_scb], [psr])
                tt("dve", modt[:], pst[:, 0:24], pc("ada_b", l * 24, 24), ALU.add, [psr, r_const], [r_mod])
                ts("dve", Avec[:], modt[:, 8:16], 1.0, ALU.add, [r_mod], [r_A])
                tt("dve", Avec[:], Avec[:], pc("normg", l * 8, 8), ALU.mult, [r_A, r_const], [r_A])
                sqs = [SB([128, 512], BF16, "sq", ls) for _ in range(2)]
                rstds = [SB([128, 512], F32, "rstd", ls) for _ in range(2)]
                tmps = [SB([128, 512], F32, "tmp", ls) for _ in range(2)]
                for g in range(4):
                    rstd, r_rstd = rstds[g % 2]
                    pst, psr = bank()
                    for k in range(8):
                        sq, sqr = sqs[k % 2]
                        act(sq[:], x[:, k, g * 512:(g + 1) * 512], AF.Square, [xr[k][g]], [sqr])
                        mm(pst[:], onesb[:], sq[:], k == 0, k == 7, [sqr, r_const], [psr])
                    ts("dve", rstd[:], pst[:], 1.0 / 1024, ALU.mult, [psr], [r_rstd], s2=EPS, op1=ALU.add)
                    act(rstd[:], rstd[:], AF.Ln, [r_rstd], [r_rstd])
                    act(rstd[:], rstd[:], AF.Exp, [r_rstd], [r_rstd], scale=-0.5)
                    for k in range(8):
                        tmp, tmr = tmps[k % 2]
                        tt("dve" if k % 2 == 0 else "pool", tmp[:], x[:, k, g * 512:(g + 1) * 512], rstd[:], ALU.mult, [xr[k][g], r_rstd], [tmr])
                        act(hT[:, k, g * 512:(g + 1) * 512], tmp[:], AF.Identity, [tmr, r_A, r_mod], [hr[k][g]], scale=Avec[:, k:k + 1], bias=modt[:, k:k + 1])
                P.barrier()

        def out_proj(l, row0, nch, ymix, r_y):
            wo, wor = wslot(nch, 1024)
            dma("pool", wo, w_out_d[l, row0:row0 + nch * 128, :].rearrange("(k p) c -> p k c", p=128), [], [wor], wor)
            for kc in range(8):
                for g in range(4):
                    pst, psr = bank()
                    for c in range(nch):
                        mm(pst[:], wo[:, c, kc * 128:(kc + 1) * 128], ymix[:, c, g * 512:(g + 1) * 512], c == 0, c == nch - 1, [wor, r_y], [psr])
                    stt(x[:, kc, g * 512:(g + 1) * 512], pst[:], modt[:, 16 + kc:17 + kc], x[:, kc, g * 512:(g + 1) * 512], ALU.mult, ALU.add,
                        [psr, r_mod, xr[kc][g]], [xr[kc][g]])

        def retention(l):
            s = 128.0 ** -0.5
            with scope() as ls:
                ropec, r_rope = SB([128, NT, 64], F32, "ropec", ls)
                ropes, _ = SB([128, NT, 64], F32, "ropes", ls)
                dma("sp", ropec[:].rearrange("p n f -> p (n f)"), ropec_d, [], [r_rope], r_rope)
                dma("sp", ropes[:].rearrange("p n f -> p (n f)"), ropes_d, [], [r_rope], r_rope)
                DT, r_DT = SB([128, 4, 128], F32, "DT", ls)
                XF, r_XF = SB([128, 4, 128], F32, "XF", ls)
                XB, _ = SB([128, 4, 128], F32, "XB", ls)
                ZF, r_Z = SB([128, 512], F32, "ZF", ls)
                ZB, _ = SB([128, 512], F32, "ZB", ls)
                zc, r_zc = SB([128, 8], F32, "zc", ls)
                g128, r_g128 = SB([128, 8], F32, "g128", ls)
                DEC, r_DEC = SB([128, 2, NT, 4], F32, "DEC", ls)
                tA, r_tA = SB([128, 128], F32, "tA", ls)
                for h in range(4):
                    lf = LG[:, l * 8 + h:l * 8 + h + 1]
                    lb = LG[:, l * 8 + 4 + h:l * 8 + 4 + h + 1]
                    ts("dve", tA[:], ctab[:, CT["pos"]:CT["pos"] + 128], lf, ALU.mult, [r_const, r_LG], [r_tA])
                    stt(tA[:], ctab[:, CT["neg"]:CT["neg"] + 128], lb, tA[:], ALU.mult, ALU.add, [r_const, r_LG, r_tA], [r_tA])
                    act(tA[:], tA[:], AF.Exp, [r_tA], [r_tA])
                    stt(DT[:, h, :], tA[:], s, ctab[:, CT["eyes"]:CT["eyes"] + 128], ALU.mult, ALU.add, [r_tA, r_const], [r_DT])
                    act(XF[:, h, :], ctab[:, CT["iota1"]:CT["iota1"] + 128], AF.Exp, [r_const, r_LG], [r_XF], scale=lf)
                    act(XB[:, h, :], ctab[:, CT["rev"]:CT["rev"] + 128], AF.Exp, [r_const, r_LG], [r_XF], scale=lb)
                    act(zc[:, h:h + 1], lf, AF.Exp, [r_LG, r_const], [r_zc], scale=pc("col127"))
                    act(zc[:, 4 + h:5 + h], lb, AF.Exp, [r_LG, r_const], [r_zc], scale=pc("colp"))
                ts("dve", zc[:], zc[:], s, ALU.mult, [r_zc], [r_zc])
                for h in range(4):
                    ts("dve", ZF[:, h * 128:(h + 1) * 128], ones32[:], zc[:, h:h + 1], ALU.mult, [r_const, r_zc], [r_Z])
                    ts("dve", ZB[:, h * 128:(h + 1) * 128], ones32[:], zc[:, 4 + h:5 + h], ALU.mult, [r_const, r_zc], [r_Z])
                act(g128[:], LG[:, l * 8:l * 8 + 8], AF.Exp, [r_LG], [r_g128], scale=128.0)
                for d in range(2):
                    for n in range(NT):
                        cm = ctab[:, CT["cmask"] + d * 16 + n:CT["cmask"] + d * 16 + n + 1]
                        ts("dve", DEC[:, d, n, :], g128[:, d * 4:d * 4 + 4], cm, ALU.mult, [r_g128, r_const], [r_DEC])
                s1, r_s1 = SB([128, 4, 64], F32, "s1", ls)
                s2, r_s2 = SB([128, 4, 64], F32, "s2", ls)
                s3, r_s3 = SB([128, 4, 64], F32, "s3", ls)
                s4, r_s4 = SB([128, 4, 64], F32, "s4", ls)
                P.barrier()
                for hp in range(2):
                    with scope() as hs:
                        qk, _ = SB([128, NT, 512], BF16, "qk", hs)
                        vt, _ = SB([128, NT, 256], BF16, "vt", hs)
                        ymix, r_y = SB([128, 2, T], BF16, "ymix", hs)
                        RB, _ = SB([128, NT, 256], BF16, "RB", hs)
                        qk_r = [Res("qk%d" % n) for n in range(NT)]
                        vn_r = [Res("vn%d" % n) for n in range(NT)]
                        rbn_r = [Res("rb%d" % n) for n in range(NT)]
                        WA, rWA = wslot(8, 512)
                        dma("pool", WA[:, :, 0:256], w_in_src(l, hp * 256, 256), [], [rWA], rWA)
                        dma("pool", WA[:, :, 256:512], w_in_src(l, 512 + hp * 256, 256), [], [rWA], rWA)
                        WB, rWB = wslot(8, 512)
                        dma("pool", WB[:, :, 0:256], w_in_src(l, 1024 + hp * 256, 256), [], [rWB], rWB)
                        dma("pool", WB[:, :, 256:512], w_in_src(l, 1536 + hp * 256, 256), [], [rWB], rWB)
                        for n in range(NT):
                            g = n // 4
                            pq, pqr = bank()
                            pv_, pvr = bank()
                            for k in range(8):
                                mm(pq[:], hT[:, k, n * 128:(n + 1) * 128], WA[:, k, :], k == 0, k == 7, [hr[k][g], rWA], [pqr])
                            for k in range(8):
                                mm(pv_[:, 0:256], hT[:, k, n * 128:(n + 1) * 128], WB[:, k, 0:256], k == 0, k == 7, [hr[k][g], rWB], [pvr])
                            pv = pq[:].rearrange("p (h two f) -> p h two f", h=4, two=2)
                            dv = qk[:, n, :].rearrange("p (h two f) -> p h two f", h=4, two=2)
                            cb = ropec[:, n, :].unsqueeze(1).to_broadcast([128, 4, 64])
                            sb_ = ropes[:, n, :].unsqueeze(1).to_broadcast([128, 4, 64])
                            tt("dve", s1[:], pv[:, :, 0, :], cb, ALU.mult, [pqr, r_rope], [r_s1])
                            tt("dve", s2[:], pv[:, :, 1, :], sb_, ALU.mult, [pqr, r_rope], [r_s2])
                            tt("pool", dv[:, :, 0, :], s1[:], s2[:], ALU.subtract, [r_s1, r_s2], [qk_r[n]])
                            tt("dve", s3[:], pv[:, :, 0, :], sb_, ALU.mult, [pqr, r_rope], [r_s3])
                            tt("dve", s4[:], pv[:, :, 1, :], cb, ALU.mult, [pqr, r_rope], [r_s4])
                            tt("pool", dv[:, :, 1, :], s3[:], s4[:], ALU.add, [r_s3, r_s4], [qk_r[n]])
                            cp("act", vt[:, n, :], pv_[:, 0:256], [pvr], [vn_r[n]])
                        Sf, r_Sf = SB([128, 2, 128], F32, "Sf", hs)
                        Sb, r_Sb = SB([128, 2, 128], F32, "Sb", hs)
                        RF, r_RF = SB([128, 256], BF16, "RF", hs)
                        vz, r_vz = SB([128, 256], BF16, "vz", hs)
                        dma("sp", Sf[:], s0ret_d[l, 0, hp * 2:hp * 2 + 2].rearrange("h d e -> d h e"), [], [r_Sf], r_Sf)
                        dma("sp", Sb[:], s0ret_d[l, 1, hp * 2:hp * 2 + 2].rearrange("h d e -> d h e"), [], [r_Sb], r_Sb)
                        H0 = hp * 2
                        for n in range(NT - 1, -1, -1):
                            cmb = ctab[:, CT["cmask"] + 16 + n:CT["cmask"] + 17 + n]
                            ts("dve", RB[:, n, :], Sb[:].rearrange("p h e -> p (h e)"), cmb, ALU.mult, [r_Sb, r_const], [rbn_r[n]])
                            tt("pool", vz[:], vt[:, n, :], ZB[:, H0 * 128:H0 * 128 + 256], ALU.mult, [vn_r[n], r_Z], [r_vz])
                            pst, psr = bank()
                            for hh in range(2):
                                mm(pst[:, hh * 128:(hh + 1) * 128], qk[:, n, 256 + hh * 128:256 + (hh + 1) * 128], vz[:, hh * 128:(hh + 1) * 128], True, True, [qk_r[n], r_vz], [psr])
                            for hh in range(2):
                                stt(Sb[:, hh, :], Sb[:, hh, :], DEC[:, 1, n, H0 + hh:H0 + hh + 1], pst[:, hh * 128:(hh + 1) * 128], ALU.mult, ALU.add, [r_Sb, r_DEC, psr], [r_Sb])
                            if n % 2 == 0:
                                dma("sp", stret_d[l, 1, n // 2, H0:H0 + 2].rearrange("h d e -> d h e"), Sb[:], [r_Sb], [], r_Sb)
                        qT, r_qT = SB([128, 2, 128], BF16, "qT", hs)
                        qxf, r_qxf = SB([128, 2, 128], BF16, "qxf", hs)
                        qxb, r_qxb = SB([128, 2, 128], BF16, "qxb", hs)
                        kT, r_kT = SB([128, 2, 128], BF16, "kT", hs)
                        PT, r_PT = SB([128, 2, 128], BF16, "PT", hs)
                        sqn, r_sqn = SB([128, 512], BF16, "sqn", hs)
                        rn_, r_rn = SB([128, 512], F32, "rn", hs)
                        y1, r_y1 = SB([128, 512], F32, "y1", hs)
                        sgt, r_sgt = SB([128, 512], F32, "sgt", hs)
                        obanks = PSF[0:2]
                        sbank = PSF[4]
                        kbank = PSF[5]
                        for n in range(NT):
                            cmf = ctab[:, CT["cmask"] + n:CT["cmask"] + n + 1]
                            ts("dve", RF[:], Sf[:].rearrange("p h e -> p (h e)"), cmf, ALU.mult, [r_Sf, r_const], [r_RF])
                            tt("pool", vz[:], vt[:, n, :], ZF[:, H0 * 128:H0 * 128 + 256], ALU.mult, [vn_r[n], r_Z], [r_vz])
                            pb, pbr = PSB[n % 2]
                            for i in range(4):
                                tr(pb[:, i * 128:(i + 1) * 128], qk[:, n, i * 128:(i + 1) * 128], [qk_r[n]], [pbr])
                            pq3 = pb[:, 0:256].rearrange("p (h i) -> p h i", h=2)
                            cp("act", qT[:], pq3, [pbr], [r_qT])
                            tt("dve", qxf[:], pq3, XF[:, H0:H0 + 2, :], ALU.mult, [pbr, r_XF], [r_qxf])
                            tt("dve", qxb[:], pq3, XB[:, H0:H0 + 2, :], ALU.mult, [pbr, r_XF], [r_qxb])
                            cp("act", kT[:], pb[:, 256:512].rearrange("p (h i) -> p h i", h=2), [pbr], [r_kT])
                            pss, pssr = sbank
                            for hh in range(2):
                                mm(pss[:, hh * 128:(hh + 1) * 128], kT[:, hh, :], qT[:, hh, :], True, True, [r_kT, r_qT], [pssr])
                            tt("dve", PT[:], pss[:, 0:256].rearrange("p (h i) -> p h i", h=2), DT[:, H0:H0 + 2, :], ALU.mult, [pssr, r_DT], [r_PT])
                            c0 = (n % 4) * 128
                            for hh in range(2):
                                po, por = obanks[hh]
                                mm(po[:, c0:c0 + 128], vt[:, n, hh * 128:(hh + 1) * 128], PT[:, hh, :], True, False, [vn_r[n], r_PT], [por])
                                mm(po[:, c0:c0 + 128], RF[:, hh * 128:(hh + 1) * 128], qxf[:, hh, :], False, False, [r_RF, r_qxf], [por])
                                mm(po[:, c0:c0 + 128], RB[:, n, hh * 128:(hh + 1) * 128], qxb[:, hh, :], False, True, [rbn_r[n], r_qxb], [por])
                            pkv, pkvr = kbank
                            for hh in range(2):
                                mm(pkv[:, hh * 128:(hh + 1) * 128], qk[:, n, 256 + hh * 128:256 + (hh + 1) * 128], vz[:, hh * 128:(hh + 1) * 128], True, True, [qk_r[n], r_vz], [pkvr])
                            for hh in range(2):
                                stt(Sf[:, hh, :], Sf[:, hh, :], DEC[:, 0, n, H0 + hh:H0 + hh + 1], pkv[:, hh * 128:(hh + 1) * 128], ALU.mult, ALU.add, [r_Sf, r_DEC, pkvr], [r_Sf])
                            if n % 2 == 1:
                                dma("sp", stret_d[l, 0, n // 2, H0:H0 + 2].rearrange("h d e -> d h e"), Sf[:], [r_Sf], [], r_Sf)
                            if n % 4 == 3:
                                g = n // 4
                                for hh in range(2):
                                    po, por = obanks[hh]
                                    act(sqn[:], po[:], AF.Square, [por], [r_sqn])
                                    pss, pssr = sbank
                                    mm(pss[:], onesb[:], sqn[:], True, True, [r_sqn, r_const], [pssr])
                                    ts("dve", rn_[:], pss[:], 1.0 / 128, ALU.mult, [pssr], [r_rn], s2=EPS, op1=ALU.add)
                                    act(rn_[:], rn_[:], AF.Ln, [r_rn], [r_rn])
                                    act(rn_[:], rn_[:], AF.Exp, [r_rn], [r_rn], scale=-0.5)
                                    tt("dve", y1[:], po[:], rn_[:], ALU.mult, [por, r_rn], [r_y1])
                                    pg, pgr = PSF[2 + hh]
                                    for k in range(8):
                                        mm(pg[:], WB[:, k, 256 + hh * 128:256 + (hh + 1) * 128], hT[:, k, g * 512:(g + 1) * 512], k == 0, k == 7, [rWB, hr[k][g]], [pgr])
                                    act(sgt[:], pg[:], AF.Silu, [pgr], [r_sgt])
                                    tt("pool", ymix[:, hh, g * 512:(g + 1) * 512], y1[:], sgt[:], ALU.mult, [r_y1, r_sgt], [r_y])
                        out_proj(l, hp * 256, 2, ymix, r_y)
                        P.barrier()

        sv8, r_sv8 = SB([128, 16], F32, "sv8")

        def proj_conv(W, rW, wc, taps, b_ap, dst, dst_r, zfull, r_z):
            for g in range(4):
                pst, psr = bank()
                for k in range(8):
                    mm(pst[:], W[:, k, wc * 128:(wc + 1) * 128], hT[:, k, g * 512:(g + 1) * 512], k == 0, k == 7, [rW, hr[k][g]], [psr])
                cp("act", zfull[:, g * 512:(g + 1) * 512], pst[:], [psr], [r_z])
            for off, wap in taps:
                if off == 0:
                    act(dst, zfull[:], AF.Identity, [r_z, r_const], [dst_r], scale=wap, bias=b_ap)
            nf8 = ctab[:, CT["nf8"]:CT["nf8"] + 8]
            nl8 = ctab[:, CT["nl8"]:CT["nl8"] + 8]
            for off, wap in taps:
                if off == 0:
                    continue
                if off < 0:
                    o = -off
                    cols = [bcol(zfull[:], 255 - i, "z") for i in range(o)]
                    mk = nl8
                else:
                    cols = [bcol(zfull[:], 0, "z")]
                    mk = nf8
                for i, cl in enumerate(cols):
                    cp("dve", sv8[:, i * 8:(i + 1) * 8], cl, [r_z], [r_sv8])
                    tt("dve", cl, sv8[:, i * 8:(i + 1) * 8], mk, ALU.mult, [r_sv8, r_const], [r_z])
                if off < 0:
                    stt(dst[:, o:T], zfull[:, 0:T - o], wap, dst[:, o:T], ALU.mult, ALU.add, [r_z, r_const, dst_r], [dst_r])
                else:
                    stt(dst[:, 0:T - off], zfull[:, off:T], wap, dst[:, 0:T - off], ALU.mult, ALU.add, [r_z, r_const, dst_r], [dst_r])
                for i, cl in enumerate(cols):
                    cp("dve", cl, sv8[:, i * 8:(i + 1) * 8], [r_sv8, dst_r], [r_z])

        def rglru(l):
            with scope() as ls:
                ymix, r_y = SB([128, 4, T], BF16, "ymixl", ls)
                S = [SB([128, T], F32, "S%d" % i, ls) for i in range(6)]
                ub, r_ub = SB([128, T], BF16, "ub", ls)
                gwt = [SB([128, 128], BF16, "gw", ls) for _ in range(16)]
                scv, r_scv = SB([128, 16], F32, "scv", ls)
                stc, r_stc = SB([128, 8], F32, "stc", ls)
                sgl, r_sgl = SB([128, 512], F32, "sgl", ls)
                act(scv[:, 0:8], pc("lru_lam", l * 8, 8), AF.Exp, [r_const], [r_scv], scale=-1.0)
                ts("dve", scv[:, 0:8], scv[:, 0:8], 1.0, ALU.add, [r_scv], [r_scv])
                act(scv[:, 0:8], scv[:, 0:8], AF.Ln, [r_scv], [r_scv])
                ts("dve", scv[:, 8:16], scv[:, 0:8], -16.0, ALU.mult, [r_scv], [r_scv])
                ts("dve", scv[:, 0:8], scv[:, 0:8], -8.0, ALU.mult, [r_scv], [r_scv])
                Wx, rWx = w_in_cols(l, 4096)
                Wgl, rWgl = w_in_cols(l, 4608)
                for gt, gr in gwt:
                    P.op("pool", lambda e, o=gt: e.memset(o[:], 0.0), [], [gr])
                for c_ in range(4):
                    for d_ in range(2):
                        for gi_ in range(2):
                            gt, gr = gwt[(c_ * 2 + d_) * 2 + gi_]
                            for bb in range(2):
                                dma("pool", gt[bb * 64:(bb + 1) * 64, bb * 64:(bb + 1) * 64], lgw_d[l, d_, gi_, c_ * 2 + bb], [], [gr], gr)
                nf8 = ctab[:, CT["nf8"]:CT["nf8"] + 8]
                nl8 = ctab[:, CT["nl8"]:CT["nl8"] + 8]
                for c in range(4):
                    (zfull, r_z), (u, r_u), (scr, r_scr), (h1, r_h1), (h2, r_h2), (rg1, r_rg1) = S
                    taps = [(j - 2, pc("lru_cw", (l * 4 + j) * 4 + c)) for j in range(4)]
                    proj_conv(Wx, rWx, c, taps, pc("lru_cb", l * 4 + c), u[:], r_u, zfull, r_z)
                    cp("pool", ub[:], u[:], [r_u], [r_ub])
                    for d in range(2):
                        gws = []
                        for gi in range(2):
                            gws.append(gwt[(c * 2 + d) * 2 + gi])
                        rg, r_rg = (zfull, r_z) if d == 0 else (rg1, r_rg1)
                        ig, r_ig = (h2, r_h2)
                        for gi, (dstt, dstr) in enumerate(((rg, r_rg), (ig, r_ig))):
                            for g in range(4):
                                pst, psr = bank()
                                mm(pst[:], gws[gi][0][:], ub[:, g * 512:(g + 1) * 512], True, True, [gws[gi][1], r_ub], [psr])
                                act(dstt[:, g * 512:(g + 1) * 512], pst[:], AF.Sigmoid, [psr, r_const], [dstr],
                                    bias=pc("lru_gb", ((l * 2 + d) * 2 + gi) * 4 + c))
                        act(scr[:], rg[:], AF.Exp, [r_rg, r_scv], [r_scr], scale=scv[:, 8 + d * 4 + c:9 + d * 4 + c])
                        act(rg[:], rg[:], AF.Exp, [r_rg, r_scv], [r_rg], scale=scv[:, d * 4 + c:d * 4 + c + 1])
                        ts("dve", scr[:], scr[:], 1.0, ALU.min, [r_scr], [r_scr], s2=-1.0, op1=ALU.mult)
                        act(scr[:], scr[:], AF.Sqrt, [r_scr, r_const], [r_scr], bias=ones32[:, 0:1])
                        tt("dve", scr[:], scr[:], ig[:], ALU.mult, [r_scr, r_ig], [r_scr])
                        tt("pool", scr[:], scr[:], u[:], ALU.mult, [r_scr, r_u], [r_scr])
                        h0 = pc("s0lru", (l * 2 + d) * 4 + c)
                        if d == 0:
                            cl = bcol(rg[:], 0, "a")
                            tt("dve", cl, cl, nf8, ALU.mult, [r_rg, r_const], [r_rg])
                            P.op("dve", lambda e, o=h1, a=rg, b=scr, i0=h0: e.tensor_tensor_scan(out=o[:], data0=a[:], data1=b[:], initial=i0, op0=ALU.mult, op1=ALU.add),
                                 [r_rg, r_scr, r_const], [r_h1])
                            cp("dve", stc[:], bcol(h1[:], 255, "h"), [r_h1], [r_stc])
                        else:
                            cl = bcol(rg[:], 255, "a")
                            tt("dve", cl, cl, nl8, ALU.mult, [r_rg, r_const], [r_rg])

                            def rev(t_):
                                xx = t_[:, :]
                                return AP(xx.tensor, xx.offset + T - 1, [list(xx.ap[0]), [-1, T]])
                            P.op("dve", lambda e, o=rev(h2), a=rev(rg), b=rev(scr), i0=h0: e.tensor_tensor_scan(out=o, data0=a, data1=b, initial=i0, op0=ALU.mult, op1=ALU.add),
                                 [r_rg, r_scr, r_const, r_ig], [r_h2])
                            cp("dve", stc[:], bcol(h2[:], 0, "h"), [r_h2], [r_stc])
                            tt("pool", h1[:], h1[:], h2[:], ALU.add, [r_h1, r_h2], [r_h1])
                        dma("sp", stlru_d[l, d, c], stc[:], [r_stc], [], r_stc)
                    for g in range(4):
                        pst, psr = bank()
                        for k in range(8):
                            mm(pst[:], Wgl[:, k, c * 128:(c + 1) * 128], hT[:, k, g * 512:(g + 1) * 512], k == 0, k == 7, [rWgl, hr[k][g]], [psr])
                        act(sgl[:], pst[:], AF.Silu, [psr], [r_sgl])
                        tt("dve", ymix[:, c, g * 512:(g + 1) * 512], h1[:, g * 512:(g + 1) * 512], sgl[:], ALU.mult, [r_h1, r_sgl], [r_y])
                out_proj(l, 1024, 4, ymix, r_y)
                P.barrier()

        def hyena(l):
            with scope() as ls:
                hidb, r_hid = SB([64, T], BF16, "hidb", ls)
                w3, r_w3 = SB([64, 2048], BF16, "w3", ls)
                dma("pool", w3[:], hw3_d[l], [], [r_w3], r_w3)
                with scope() as fs_:
                    zTt, r_zT = SB([33, T], F32, "zT", fs_)
                    w1, r_w1 = SB([33, 64], F32, "w1", fs_)
                    w2, r_w2 = SB([64, 64], F32, "w2", fs_)
                    fb, r_fb = SB([64, 4], F32, "fb", fs_)
                    frh, r_frh = SB([64, 4], F32, "frh", fs_)
                    sa, r_sa = SB([64, 512], F32, "sa", fs_)
                    sb4, r_sb4 = SB([64, 512], F32, "sb4", fs_)
                    hid1, r_hid1 = SB([64, T], F32, "hid1", fs_)
                    dma("sp", zTt[:], zT_d, [], [r_zT], r_zT)
                    dma("sp", w1[:], hw1_d[l], [], [r_w1], r_w1)
                    dma("sp", w2[:], hw2_d[l], [], [r_w2], r_w2)
                    for i, bn in enumerate(("hy_b1", "hy_b2")):
                        tt("dve", fb[:, i:i + 1], pc(bn, l, 1, 64), pc("hy_fr", l * 2 + i, 1, 64), ALU.mult, [r_const], [r_fb])
                        ts("dve", fb[:, 2 + i:3 + i], fb[:, i:i + 1], 0.25, ALU.mult, [r_fb], [r_fb])
                        ts("dve", fb[:, i:i + 1], fb[:, i:i + 1], 0.5, ALU.mult, [r_fb], [r_fb])
                        ts("dve", frh[:, i:i + 1], pc("hy_fr", l * 2 + i, 1, 64), 0.5, ALU.mult, [r_const], [r_frh])
                        ts("dve", frh[:, 2 + i:3 + i], pc("hy_fr", l * 2 + i, 1, 64), 0.25, ALU.mult, [r_const], [r_frh])

                    def sin_layer(i, lhsT, lr, rhs_t, rr, K, out_t, out_r):
                        for g in range(4):
                            pst, psr = bank()
                            mm(pst[0:64, :], lhsT, rhs_t[0:K, g * 512:(g + 1) * 512], True, True, [lr, rr], [psr])
                            act(sa[:], pst[0:64, :], AF.Sin, [psr, r_frh, r_fb], [r_sa], scale=frh[:, i:i + 1], bias=fb[:, i:i + 1])
                            act(sb4[:], pst[0:64, :], AF.Sin, [psr, r_frh, r_fb], [r_sb4], scale=frh[:, 2 + i:3 + i], bias=fb[:, 2 + i:3 + i])
                            tt("dve", sb4[:], sb4[:], sb4[:], ALU.mult, [r_sb4], [r_sb4])
                            ts("dve", sb4[:], sb4[:], -4.0, ALU.mult, [r_sb4], [r_sb4], s2=2.0, op1=ALU.add)
                            tt("dve", out_t[:, g * 512:(g + 1) * 512], sa[:], sb4[:], ALU.mult, [r_sa, r_sb4], [out_r])
                    sin_layer(0, w1[:], r_w1, zTt, r_zT, 33, hid1, r_hid1)
                    sin_layer(1, w2[:], r_w2, hid1, r_hid1, 64, hidb, r_hid)
                    P.barrier()

                Wv_, Wx1, Wx2, Wg_ = 2048, 2560, 3072, 3584

                for hh in range(2):
                    with scope() as hs:
                        Y, r_Y = SB([128, 32, 256], BF16, "Y", hs)
                        ufm, r_ufm = SB([128, 2, T], BF16, "ufm", hs)
                        ntn, r_ntn = SB([128, NT], F32, "ntn", hs)
                        ts("dve", ntn[:], pc("tnorm", 0, NT), -1.0, ALU.mult, [r_const], [r_ntn])

                        def conv_one(W, rW, cc, tapbase, cacc, r_cacc, zfull, r_z):
                            c = hh * 2 + cc
                            ch12 = tapbase * 4 + c
                            taps = [(j - 1, pc("hy_cw", (l * 3 + j) * 12 + ch12)) for j in range(3)]
                            proj_conv(W, rW, c, taps, pc("hy_cb", l * 12 + ch12), cacc[:], r_cacc, zfull, r_z)

                        def spectral(o):
                            with scope() as s2:
                                FU, r_FU = SB([128, NT, 768], BF16, "FU", s2)
                                for n in range(NT):
                                    pb, pbr = PSB[n % 2]
                                    for cc in range(2):
                                        tr(pb[:, cc * 128:(cc + 1) * 128], ufm[:, cc, n * 128:(n + 1) * 128], [r_ufm], [pbr])
                                    cp("act", FU[:, n, 512:768], pb[:, 0:256], [pbr], [r_FU])
                                with scope() as s3:
                                    decs = [SB([128, 256], F32, "dec", s3) for _ in range(2)]
                                    abs_ = [SB([128, 512], BF16, "ab", s3) for _ in range(3)]
                                    tas = [SB([128, 256], F32, "ta", s3) for _ in range(2)]
                                    tbs = [SB([128, 256], F32, "tb", s3) for _ in range(2)]
                                    rn_, r_rn = SB([128, 512], F32, "rnh", s3)
                                    fun_r = [Res("fun%d" % n) for n in range(NT)]
                                    cf = o * 1024 + hh * 256
                                    cb_ = o * 1024 + 512 + hh * 256
                                    pn, pnr = PSF[5]
                                    for n in range(NT + 2):
                                        if n < NT:
                                            pst, psr = PSF[n % 4]
                                            dec, r_dec = decs[n % 2]
                                            ab, r_ab = abs_[n % 3]
                                            mm(pst[:, 0:256], hidb[0:64, n * 128:(n + 1) * 128], w3[0:64, cf:cf + 256], True, True, [r_hid, r_w3], [psr])
                                            mm(pst[:, 256:512], hidb[0:64, n * 128:(n + 1) * 128], w3[0:64, cb_:cb_ + 256], True, True, [r_hid, r_w3], [psr])
                                            act(dec[:], ctab[:, CT["delta"] + hh * 256:CT["delta"] + hh * 256 + 256], AF.Exp, [r_const, r_ntn], [r_dec], scale=ntn[:, n:n + 1])
                                            tt("dve", FU[:, n, 0:512].rearrange("p (d c) -> p d c", d=2), pst[:].rearrange("p (d c) -> p d c", d=2),
                                               dec[:].unsqueeze(1).to_broadcast([128, 2, 256]), ALU.mult, [psr, r_dec], [fun_r[n], r_FU])
                                            act(ab[:], FU[:, n, 0:512], AF.Abs, [fun_r[n]], [r_ab])
                                        m = n - 2
                                        if m >= 0:
                                            ab, r_ab = abs_[m % 3]
                                            mm(pn[:], normwb[:], ab[:], m == 0, m == NT - 1, [r_const, r_ab], [pnr])
                                    ts("dve", rn_[:], pn[:], EPS, ALU.add, [pnr], [r_rn])
                                    P.op("dve", lambda e, r=rn_: e.reciprocal(out=r[:], in_=r[:]), [r_rn], [r_rn])
                                    for n in range(NT):
                                        ta, r_ta = tas[n % 2]
                                        tb, r_tb = tbs[n % 2]
                                        tt("dve", ta[:], FU[:, n, 0:256], rn_[:, 0:256], ALU.mult, [fun_r[n], r_rn], [r_ta])
                                        tt("dve", tb[:], FU[:, n, 256:512], rn_[:, 256:512], ALU.mult, [fun_r[n], r_rn], [r_tb])
                                        tt("dve", FU[:, n, 0:256], ta[:], tb[:], ALU.add, [r_ta, r_tb], [fun_r[n], r_FU])
                                        tt("pool", FU[:, n, 256:512], ta[:], tb[:], ALU.subtract, [r_ta, r_tb], [fun_r[n], r_FU])
                                with scope() as s4:
                                    fstr = [SB([128, NT, 128], BF16, "fstr", s4) for _ in range(NFS)]
                                    U0, r_U0 = SB([128, 512], F32, "U0", s4)
                                    U1, r_U1 = SB([128, 512], F32, "U1", s4)
                                    t1, r_t1 = SB([128, 256], F32, "t1", s4)
                                    t2, r_t2 = SB([128, 256], F32, "t2", s4)
                                    for j in range(16):
                                        fr_, frr = fstr[(2 * j) % NFS]
                                        fi_, fir = fstr[(2 * j + 1) % NFS]
                                        dma("sp", fr_[:].rearrange("p n r -> p (n r)"), dftF_d[j], [], [frr], frr)
                                        dma("sp", fi_[:].rearrange("p n r -> p (n r)"), dftF_d[16 + j], [], [fir], fir)
                                        pre, prer = PSF[(2 * j) % 6]
                                        pim, pimr = PSF[(2 * j + 1) % 6]
                                        for n in range(NT):
                                            mm(pre[:, 0:256], fr_[:, n, :], FU[:, n, 0:256], n == 0, n == NT - 1, [frr, r_FU], [prer])
                                        for n in range(NT):
                                            mm(pre[:, 256:512], fr_[:, n, :], FU[:, n, 512:768], n == 0, n == NT - 1, [frr, r_FU], [prer])
                                        for n in range(NT):
                                            mm(pim[:], fi_[:, n, :], FU[:, n, 256:768], n == 0, n == NT - 1, [fir, r_FU], [pimr])
                                        cp("act", U0[:], pre[:], [prer], [r_U0])
                                        cp("act", U1[:], pim[:], [pimr], [r_U1])
                                        tt("dve", t1[:], U0[:, 256:512], U0[:, 0:256], ALU.mult, [r_U0], [r_t1])
                                        tt("pool", t2[:], U1[:, 256:512], U1[:, 0:256], ALU.mult, [r_U1], [r_t2])
                                        tt("dve", Y[:, j, :], t1[:], t2[:], ALU.subtract, [r_t1, r_t2], [r_Y])
                                        tt("dve", t1[:], U0[:, 256:512], U1[:, 0:256], ALU.mult, [r_U0, r_U1], [r_t1])
                                        tt("pool", t2[:], U1[:, 256:512], U0[:, 0:256], ALU.mult, [r_U0, r_U1], [r_t2])
                                        tt("dve", Y[:, 16 + j, :], t1[:], t2[:], ALU.add, [r_t1, r_t2], [r_Y])

                        def inverse(Wpre, tapbase, o, gate):
                            with scope() as s5:
                                istr = [SB([128, T], BF16, "istr", s5) for _ in range(4)]
                                zfull, r_z = SB([128, T], F32, "zf", s5)
                                cacc, r_cacc = SB([128, T], F32, "cacc", s5)
                                xc, r_xc = SB([128, 2, T], BF16, "xc", s5)
                                ev1, r_ev1 = SB([128, 512], F32, "ev1", s5)
                                sgh, r_sgh = SB([128, 512], F32, "sgh", s5)
                                (W, rW), gpre = Wpre
                                if gate:
                                    Wg, rWg = gpre
                                for j in range(4):
                                    it, itr = istr[j % 4]
                                    dma("sp", it[:], dftI_d[j], [], [itr], itr)
                                for cc in range(2):
                                    c = hh * 2 + cc
                                    conv_one(W, rW, cc, tapbase, cacc, r_cacc, zfull, r_z)
                                    if gate:
                                        for g in range(4):
                                            sl = slice(g * 512, (g + 1) * 512)
                                            pg, pgr = bank()
                                            for k in range(8):
                                                mm(pg[:], Wg[:, k, c * 128:(c + 1) * 128], hT[:, k, sl], k == 0, k == 7, [rWg, hr[k][g]], [pgr])
                                            act(sgh[:], pg[:], AF.Silu, [pgr], [r_sgh])
                                            tt("dve", xc[:, cc, sl], cacc[:, sl], sgh[:], ALU.mult, [r_cacc, r_sgh], [r_xc])
                                    else:
                                        cp("pool", xc[:, cc, :], cacc[:], [r_cacc], [r_xc])
                                for j in range(32):
                                    it, itr = istr[j % 4]
                                    if j >= 4:
                                        dma("sp", it[:], dftI_d[j], [], [itr], itr)
                                    for cc in range(2):
                                        for g in range(4):
                                            pst, psr = PSF[cc * 4 + g]
                                            mm(pst[:], Y[:, j, cc * 128:(cc + 1) * 128], it[:, g * 512:(g + 1) * 512], j == 0, j == 31, [r_Y, itr], [psr])
                                for cc in range(2):
                                    for g in range(4):
                                        sl = slice(g * 512, (g + 1) * 512)
                                        pst, psr = PSF[cc * 4 + g]
                                        stt(ev1[:], ufm[:, cc, sl], pc("hy_bias", (l * 2 + o) * 4 + hh * 2 + cc), pst[:], ALU.mult, ALU.add, [r_ufm, r_const, psr], [r_ev1])
                                        tt("pool", ufm[:, cc, sl], ev1[:], xc[:, cc, sl], ALU.mult, [r_ev1, r_xc], [r_ufm])

                        with scope() as s1:
                            zfull, r_z = SB([128, T], F32, "zf", s1)
                            cacc, r_cacc = SB([128, T], F32, "cacc", s1)
                            W, rW = w_in_cols(l, Wv_)
                            for cc in range(2):
                                conv_one(W, rW, cc, 0, cacc, r_cacc, zfull, r_z)
                                cp("pool", ufm[:, cc, :], cacc[:], [r_cacc], [r_ufm])
                        pre1 = (w_in_cols(l, Wx1), None)
                        spectral(0)
                        inverse(pre1, 1, 0, False)
                        pre2 = (w_in_cols(l, Wx2), w_in_cols(l, Wg_))
                        spectral(1)
                        inverse(pre2, 2, 1, True)
                        out_proj(l, 512 + hh * 256, 2, ufm, r_ufm)

        for l in range(DEPTH):
            mod_and_norm(l)
            if STAGE in ("all", "ret"):
                retention(l)
            if STAGE in ("all", "lru"):
                rglru(l)
            if STAGE in ("all", "hy"):
                hyena(l)
        with scope() as ls:
            outs = [SB([128, 512], F32, "ob", ls) for _ in range(4)]
            oi = [0]
            sqs = [SB([128, 512], BF16, "sq", ls) for _ in range(2)]
            rstds = [SB([128, 512], F32, "rstd", ls) for _ in range(2)]
            for g in range(4):
                rstd, r_rstd = rstds[g % 2]
                pst, psr = bank()
                for k in range(8):
                    sq, sqr = sqs[k % 2]
                    act(sq[:], x[:, k, g * 512:(g + 1) * 512], AF.Square, [xr[k][g]], [sqr])
                    mm(pst[:], onesb[:], sq[:], k == 0, k == 7, [sqr, r_const], [psr])
                ts("dve", rstd[:], pst[:], 1.0 / 1024, ALU.mult, [psr], [r_rstd], s2=EPS, op1=ALU.add)
                act(rstd[:], rstd[:], AF.Ln, [r_rstd], [r_rstd])
                act(rstd[:], rstd[:], AF.Exp, [r_rstd], [r_rstd], scale=-0.5)
                for k in range(8):
                    ob, obr = outs[oi[0] % 4]
                    oi[0] += 1
                    stt(ob[:], x[:, k, g * 512:(g + 1) * 512], pc("final_g", k), rstd[:], ALU.mult, ALU.mult, [xr[k][g], r_const, r_rstd], [obr])
                    dma("sp", yT_d[k * 128:(k + 1) * 128, g * 512:(g + 1) * 512], ob[:], [obr], [], obr)
            P.barrier()
        P.emit()
    return nc


def _bf16(a):
    return np.asarray(a, dtype=np.float32).astype(ml_dtypes.bfloat16)


def _dft_mats(L, nseq):
    N = 2 * L
    t = np.arange(L, dtype=np.int64)
    k = np.arange(L, dtype=np.int64)
    ph = ((2 * k[None, :] + 1) * t[:, None]) % (2 * N)
    ang = ph.astype(np.float64) * (math.pi / N)
    C = np.cos(ang)
    S = -np.sin(ang)
    Fwd = np.zeros((T, 2 * T), np.float32)
    for b in range(nseq):
        Fwd[b * L:(b + 1) * L, b * L:(b + 1) * L] = C
        Fwd[b * L:(b + 1) * L, T + b * L:T + (b + 1) * L] = S
    dftF = np.ascontiguousarray(Fwd.reshape(NT, 128, 32, 128).transpose(2, 1, 0, 3)).reshape(32, 128, NT * 128)
    Inv = (Fwd.T * (2.0 / N)).astype(np.float32)
    dftI = np.ascontiguousarray(Inv.reshape(32, 128, T))
    return _bf16(dftF), _bf16(dftI)


def _core_consts(L, nseq, rope_on):
    f32 = np.float32
    pos_in = np.arange(T) % L
    m_int = 1.0 if L == T else 0.0
    nf8 = np.array([1.0] + [m_int] * 7, f32)
    nl8 = np.array([m_int] * 7 + [1.0], f32)
    masks = (nf8, nl8)
    if rope_on:
        rows = T // 64
        row = np.repeat(np.arange(rows, dtype=f32), 64)
        col = np.tile(np.arange(64, dtype=f32), rows)
        inv = (f32(10000.0) ** (-np.arange(32, dtype=f32) / f32(32))).astype(f32)
        ang = np.concatenate([row[:, None] * inv[None], col[:, None] * inv[None]], axis=-1).astype(f32)
        c, s = np.cos(ang).astype(f32), np.sin(ang).astype(f32)
    else:
        c, s = np.ones((T, 64), f32), np.zeros((T, 64), f32)
    ropec = np.ascontiguousarray(c.reshape(NT, 128, 64).transpose(1, 0, 2)).reshape(128, NT * 64)
    ropes = np.ascontiguousarray(s.reshape(NT, 128, 64).transpose(1, 0, 2)).reshape(128, NT * 64)
    tl = np.linspace(0.0, 1.0, L, dtype=f32)
    f = np.linspace(1e-4, 15.0, 16, dtype=f32)
    ang = (f32(2.0 * math.pi / L) * np.arange(L, dtype=f32)[:, None] * f[None, :]).astype(f32)
    z = np.concatenate([tl[:, None], np.cos(ang), -np.sin(ang)], axis=-1).astype(f32)
    zT = np.ascontiguousarray(np.tile(z, (nseq, 1)).T)
    tnorm = np.tile(tl, nseq).reshape(NT, 128).T
    cm = np.ones((2, NT), f32)
    cpl = L // 128
    for n in range(NT):
        if n % cpl == 0 and n != 0:
            cm[0, n] = 0.0
        if n % cpl == cpl - 1 and n != NT - 1:
            cm[1, n] = 0.0
    return masks, ropec, ropes, zT, np.ascontiguousarray(tnorm), cm, 1.0 / nseq


def _cols(a):
    a = np.asarray(a, np.float32)
    lead = int(np.prod(a.shape[:-1])) if a.ndim > 1 else 1
    n = a.shape[-1] // 128
    return np.ascontiguousarray(a.reshape(lead, n, 128).transpose(2, 0, 1)).reshape(128, lead * n)


_PROG = {}


def kernel(x_prompt, x_sample, state_ret, state_lru, c, c_ctx, norm_g, ada_w, ada_b, w_in,
           ret_decay_logit, hy_conv_w, hy_conv_b, hy_ffn_w1, hy_ffn_b1, hy_ffn_w2, hy_ffn_b2,
           hy_ffn_w3, hy_freq, hy_bias, lru_conv_w, lru_conv_b, lru_gate_w, lru_gate_b,
           lru_lambda, w_out, final_g):
    f32 = np.float32
    A = lambda a: np.ascontiguousarray(np.asarray(a, f32))
    x_prompt, x_sample = A(x_prompt), A(x_sample)
    ncores = 8
    if "nc" not in _PROG:
        _PROG["nc"] = build_program()
    nc = _PROG["nc"]
    dft_s = _dft_mats(2048, 1)
    dft_p = _dft_mats(256, 8)
    cc_s = _core_consts(2048, 1, True)
    cc_p = _core_consts(256, 8, False)
    identb = _bf16(np.eye(128))
    idx = np.arange(128, dtype=f32)
    diff = idx[None, :] - idx[:, None]
    s = f32(128.0 ** -0.5)
    hcw = np.asarray(hy_conv_w, f32)
    deltas = np.abs(np.linspace(math.log(1e-2) / 1.5, math.log(1e-2) / 0.3, 512, dtype=f32)).astype(f32)

    def pcol_for(cv, s0l):
        pc_ = np.zeros((128, NPC), f32)

        def put(name, arr):
            arr = np.asarray(arr, f32)
            pc_[:arr.shape[0], PC[name]:PC[name] + arr.shape[1]] = arr
        put("cvec", _cols(cv))
        put("normg", _cols(norm_g))
        put("ada_b", _cols(ada_b))
        put("final_g", _cols(final_g))
        put("hy_cw", _cols(hcw))
        put("hy_cb", _cols(hy_conv_b))
        put("hy_bias", _cols(hy_bias))
        put("lru_cw", _cols(lru_conv_w))
        put("lru_cb", _cols(lru_conv_b))
        put("lru_gb", _cols(lru_gate_b))
        put("lru_lam", _cols(lru_lambda))
        put("s0lru", _cols(s0l))
        put("hy_b1", np.asarray(hy_ffn_b1, f32).T)
        put("hy_b2", np.asarray(hy_ffn_b2, f32).T)
        put("hy_fr", np.asarray(hy_freq, f32).reshape(4, 64).T)
        put("col127", (127.0 - idx)[:, None])
        put("colp", idx[:, None])
        return pc_

    def ctab_for(cm, nw, bm):
        ct = np.zeros((128, NCT), f32)
        ct[:, CT["pos"]:CT["pos"] + 128] = np.maximum(diff, 0)
        ct[:, CT["neg"]:CT["neg"] + 128] = np.maximum(-diff, 0)
        ct[:, CT["eyes"]:CT["eyes"] + 128] = np.eye(128, dtype=f32) * s
        ct[:, CT["iota1"]:CT["iota1"] + 128] = (idx + 1)[None, :]
        ct[:, CT["rev"]:CT["rev"] + 128] = (128 - idx)[None, :]
        ct[:, CT["delta"]:CT["delta"] + 512] = deltas[None, :]
        ct[:, CT["logit"]:CT["logit"] + 16] = np.asarray(ret_decay_logit, f32).reshape(1, 16)
        ct[:, CT["cmask"]:CT["cmask"] + 32] = cm.reshape(1, 32)
        ct[:, CT["normw"]:CT["normw"] + 128] = nw
        ct[:, CT["nf8"]:CT["nf8"] + 8] = bm[0][None, :]
        ct[:, CT["nl8"]:CT["nl8"] + 8] = bm[1][None, :]
        return ct

    shared = dict(ada_w=A(ada_w), w_in=A(w_in), w_out=A(w_out), hy_ffn_w1=A(hy_ffn_w1), hy_ffn_w2=A(hy_ffn_w2),
                  hy_ffn_w3=A(hy_ffn_w3), lru_gate_w=A(lru_gate_w), identb=identb)
    in_maps = []
    for core in range(ncores):
        if core < 4:
            xs = x_sample[core]
            cv = np.asarray(c, f32)[core]
            s0r = A(state_ret)[core]
            s0l = np.asarray(state_lru, f32)[core]
            cc, dft = cc_s, dft_s
        else:
            pb = (core - 4) % 2
            xs = x_prompt[pb * 8:(pb + 1) * 8].reshape(T, 1024)
            cv = np.asarray(c_ctx, f32)
            s0r = np.zeros((DEPTH, 2, 4, 128, 128), f32)
            s0l = np.zeros((DEPTH, 2, 512), f32)
            cc, dft = cc_p, dft_p
        masks, ropec, ropes, zT, tnorm, cm, nw = cc
        pc_ = pcol_for(cv, s0l)
        pc_[:, PC["tnorm"]:PC["tnorm"] + 16] = tnorm
        m = dict(shared)
        m.update(xT=np.ascontiguousarray(xs.T), pcol=pc_, ctab=ctab_for(cm, nw, masks), ropec=ropec, ropes=ropes,
                 s0ret=np.ascontiguousarray(s0r), dftF=dft[0], dftI=dft[1], zT=zT)
        in_maps.append(m)
    res = run_bass_kernel_spmd(nc, in_maps, core_ids=list(range(ncores)))
    R = res.results
    y_sample = np.stack([np.ascontiguousarray(R[j]["yT"].T) for j in range(4)]).astype(f32)
    y_prompt = np.concatenate([np.ascontiguousarray(R[4 + pb]["yT"].T).reshape(8, 256, 1024) for pb in range(2)]).astype(f32)
    nsr = np.concatenate([R[4 + pb]["st_ret"].transpose(2, 0, 1, 3, 4, 5) for pb in range(2)]).astype(f32)
    nsl = np.concatenate([R[4 + pb]["st_lru"].transpose(4, 0, 1, 2, 3).reshape(8, DEPTH, 2, 512) for pb in range(2)]).astype(f32)
    return (y_prompt, y_sample, np.ascontiguousarray(nsr), np.ascontiguousarray(nsl))
```

```python
import contextlib
import math
import os
import numpy as np
import ml_dtypes
import concourse.bass as bass
import concourse.mybir as mybir
from concourse.bass_utils import run_bass_kernel_spmd
from concourse.ap import AP

F32 = mybir.dt.float32
BF16 = mybir.dt.bfloat16
ALU = mybir.AluOpType
AF = mybir.ActivationFunctionType

EPOCH = 16000
SAME_ENGINE_SYNC = True
EPS = 1e-6
T = 2048
NT = 16
NFS = 4
DEPTH = 2
STAGE = os.environ.get("KSTAGE", "all")


class Res:
    __slots__ = ("name", "w", "r", "dsem", "dcnt", "dkind")

    def __init__(self, name):
        self.name = name
        self.w = None
        self.r = {}
        self.dsem = None
        self.dcnt = 0
        self.dkind = None


class Prog:
    ENG = ("pe", "act", "dve", "pool", "sp")

    def __init__(self, nc, es):
        self.nc = nc
        self.es = es
        self.ops = {e: [] for e in self.ENG}
        self.sem = {}
        self.cnt = {e: 0 for e in self.ENG}
        self.seen = {e: {} for e in self.ENG}
        self.nsem = 0
        self.dres = []
        self.sempool = {"sw": [], "hw": []}
        self.allsems = []
        for e in self.ENG:
            self._new_epoch(e)

    def new_sem(self, name):
        self.nsem += 1
        return self.es.enter_context(self.nc.semaphore("%s_%d" % (name, self.nsem)))

    def _new_epoch(self, e):
        self.sem[e] = self.new_sem("e_" + e)
        self.cnt[e] = 0
        self.allsems.append((e, self.sem[e]))

    def _waits(self, eng, reads, writes):
        need = {}

        def add(ev):
            if ev is None:
                return
            sem, val, e = ev
            if e == eng and (eng == "pe" or not SAME_ENGINE_SYNC):
                return
            k = id(sem)
            if self.seen[eng].get(k, 0) >= val:
                return
            if k not in need or need[k][1] < val:
                need[k] = (sem, val)

        for r in reads:
            add(r.w)
        for w in writes:
            add(w.w)
            for ev in w.r.values():
                add(ev)
        out = list(need.values())
        for sem, val in out:
            self.seen[eng][id(sem)] = val
        return out

    def op(self, eng, fn, reads=(), writes=()):
        waits = self._waits(eng, reads, writes)
        if self.cnt[eng] >= EPOCH:
            self._new_epoch(eng)
        self.cnt[eng] += 1
        ev = (self.sem[eng], self.cnt[eng], eng)
        self.ops[eng].append((waits, fn, (self.sem[eng], 1)))
        for r in reads:
            r.r[id(ev[0])] = ev
        for w in writes:
            w.w = ev
            w.r = {}
        return ev

    def dma(self, eng, fn, reads=(), writes=(), sres=None):
        waits = self._waits(eng, reads, writes)
        kind = "sw" if eng == "pool" else "hw"
        if sres.dsem is None:
            if self.sempool[kind]:
                sres.dsem, sres.dcnt = self.sempool[kind].pop()
            else:
                sres.dsem = self.new_sem("d" + kind)
            sres.dkind = kind
            self.dres.append(sres)
        assert sres.dkind == kind, "semaphore shared between SW and HW DGE: %s" % sres.name
        sres.dcnt += 16
        ev = (sres.dsem, sres.dcnt, "dma")
        self.ops[eng].append((waits, fn, (sres.dsem, 16)))
        for r in reads:
            r.r[id(ev[0])] = ev
        for w in writes:
            w.w = ev
            w.r = {}
        return ev

    def release(self, res_list):
        for r in res_list:
            if r.dsem is not None:
                self.sempool[r.dkind].append((r.dsem, r.dcnt))
                if r in self.dres:
                    self.dres.remove(r)
                r.dsem = None

    def wait_event(self, eng, ev):
        sem, val, e = ev
        if self.seen[eng].get(id(sem), 0) >= val:
            return
        self.seen[eng][id(sem)] = val
        self.ops[eng].append(([(sem, val)], None, None))

    def barrier(self):
        evs = []
        for e in self.ENG:
            if self.cnt[e] > 0:
                evs.append((self.sem[e], self.cnt[e], e))
        for r in self.dres:
            evs.append((r.dsem, r.dcnt, "dma"))
        for e in self.ENG:
            for ev in evs:
                if ev[2] == e:
                    continue
                self.wait_event(e, ev)

    def emit(self):
        nc = self.nc
        with nc.Block() as block:
            def mk(e):
                def body(engh):
                    for waits, fn, inc in self.ops[e]:
                        for sem, val in waits:
                            engh.wait_ge(sem, val)
                        if fn is not None:
                            ins = fn(engh)
                            ins.then_inc(inc[0], inc[1])
                return body
            block.tensor(mk("pe"))
            block.scalar(mk("act"))
            block.vector(mk("dve"))
            block.gpsimd(mk("pool"))
            block.sync(mk("sp"))


def _pcol_layout():
    off = {}
    n = 0

    def add(name, w):
        nonlocal n
        off[name] = n
        n += w
    add("cvec", 8)
    add("normg", 16)
    add("ada_b", 48)
    add("final_g", 8)
    add("hy_cw", 72)
    add("hy_cb", 24)
    add("hy_bias", 16)
    add("lru_cw", 32)
    add("lru_cb", 8)
    add("lru_gb", 32)
    add("lru_lam", 16)
    add("s0lru", 16)
    add("hy_b1", 2)
    add("hy_b2", 2)
    add("hy_fr", 4)
    add("tnorm", 16)
    add("col127", 1)
    add("colp", 1)
    return off, n


PC, NPC = _pcol_layout()
CT = {"pos": 0, "neg": 128, "eyes": 256, "iota1": 384, "rev": 512, "delta": 640, "logit": 1152, "cmask": 1168,
      "normw": 1200, "nf8": 1328, "nl8": 1336}
NCT = 1344


def build_program():
    nc = bass.Bass("TRN2", target_bir_lowering=False)
    dI = lambda name, shape, dt=F32: nc.dram_tensor(name, list(shape), dt, kind="ExternalInput").ap()
    dO = lambda name, shape, dt=F32: nc.dram_tensor(name, list(shape), dt, kind="ExternalOutput").ap()
    xT_d = dI("xT", [1024, T])
    pcol_d = dI("pcol", [128, NPC])
    ctab_d = dI("ctab", [128, NCT])
    ropec_d = dI("ropec", [128, NT * 64])
    ropes_d = dI("ropes", [128, NT * 64])
    identb_d = dI("identb", [128, 128], BF16)
    s0ret_d = dI("s0ret", [DEPTH, 2, 4, 128, 128])
    dftF_d = dI("dftF", [32, 128, NT * 128], BF16)
    dftI_d = dI("dftI", [32, 128, T], BF16)
    zT_d = dI("zT", [33, T])
    ada_w_d = dI("ada_w", [DEPTH, 1024, 3072])
    w_in_d = dI("w_in", [DEPTH, 1024, 5120])
    w_out_d = dI("w_out", [DEPTH, 1536, 1024])
    hw1_d = dI("hy_ffn_w1", [DEPTH, 33, 64])
    hw2_d = dI("hy_ffn_w2", [DEPTH, 64, 64])
    hw3_d = dI("hy_ffn_w3", [DEPTH, 64, 2048])
    lgw_d = dI("lru_gate_w", [DEPTH, 2, 2, 8, 64, 64])
    yT_d = dO("yT", [1024, T])
    stret_d = dO("st_ret", [DEPTH, 2, 8, 4, 128, 128])
    stlru_d = dO("st_lru", [DEPTH, 2, 4, 128, 8])

    with contextlib.ExitStack() as es:
        P = Prog(nc, es)
        cnt = [0]

        def SB(shape, dt, name=None, stack=None):
            cnt[0] += 1
            nm = "%s_%d" % (name or "t", cnt[0])
            t = (stack or es).enter_context(nc.sbuf_tensor(nm, list(shape), dt))
            r = Res(nm)
            if stack is not None:
                if not hasattr(stack, "_res"):
                    stack._res = []
                stack._res.append(r)
            return t, r

        @contextlib.contextmanager
        def scope():
            st = contextlib.ExitStack()
            try:
                yield st
                P.barrier()
                P.release(getattr(st, "_res", []))
            finally:
                st.close()

        def mm(out, lhsT, rhs, start, stop, reads, writes):
            P.op("pe", lambda e, o=out, l=lhsT, r=rhs, s=start, t=stop: e.matmul(o, lhsT=l, rhs=r, start=s, stop=t),
                 reads, writes)

        def tr(out, in_, reads, writes):
            P.op("pe", lambda e, o=out, i=in_: e.transpose(out=o, in_=i, identity=identb[:]), list(reads) + [r_const], writes)

        def act(out, in_, func, reads, writes, scale=1.0, bias=None):
            if bias is None:
                P.op("act", lambda e, o=out, i=in_, f=func, s=scale: e.activation(out=o, in_=i, func=f, scale=s), reads, writes)
            else:
                P.op("act", lambda e, o=out, i=in_, f=func, s=scale, b=bias: e.activation(out=o, in_=i, func=f, scale=s, bias=b), reads, writes)

        def tt(eng, out, in0, in1, op, reads, writes):
            P.op(eng, lambda e, o=out, a=in0, b=in1, p=op: e.tensor_tensor(out=o, in0=a, in1=b, op=p), reads, writes)

        def ts(eng, out, in0, s1, op0, reads, writes, s2=None, op1=None):
            if op1 is None:
                P.op(eng, lambda e, o=out, a=in0, s=s1, p=op0: e.tensor_scalar(out=o, in0=a, scalar1=s, scalar2=None, op0=p), reads, writes)
            else:
                P.op(eng, lambda e, o=out, a=in0, s=s1, p=op0, s_2=s2, p1=op1: e.tensor_scalar(out=o, in0=a, scalar1=s, scalar2=s_2, op0=p, op1=p1), reads, writes)

        def stt(out, in0, scalar, in1, op0, op1, reads, writes):
            P.op("dve", lambda e, o=out, a=in0, s=scalar, b=in1, p0=op0, p1=op1: e.scalar_tensor_tensor(out=o, in0=a, scalar=s, in1=b, op0=p0, op1=p1), reads, writes)

        def cp(eng, out, in_, reads, writes):
            if eng == "act":
                P.op("act", lambda e, o=out, i=in_: e.copy(out=o, in_=i), reads, writes)
            else:
                P.op(eng, lambda e, o=out, i=in_: e.tensor_copy(out=o, in_=i), reads, writes)

        def dma(eng, out, in_, reads, writes, sres):
            P.dma(eng, lambda e, o=out, i=in_: e.dma_start(out=o, in_=i), reads, writes, sres)

        x, _ = SB([128, 8, T], F32, "x")
        xr = [[Res("x%d_%d" % (k, g)) for g in range(4)] for k in range(8)]
        hT, _ = SB([128, 8, T], BF16, "hT")
        hr = [[Res("h%d_%d" % (k, g)) for g in range(4)] for k in range(8)]
        pcol, r_const = SB([128, NPC], F32, "pcol")
        ctab, _ = SB([128, NCT], F32, "ctab")
        identb, _ = SB([128, 128], BF16, "identb")
        ones32, _ = SB([128, 128], F32, "ones32")
        onesb, _ = SB([128, 128], BF16, "onesb")
        normwb, _ = SB([128, 128], BF16, "normwb")
        modt, r_mod = SB([128, 24], F32, "modt")
        Avec, r_A = SB([128, 8], F32, "Avec")
        sc, r_sc = SB([128, 8], F32, "sc")
        LG, r_LG = SB([128, 16], F32, "LG")
        NW = 2
        wslots = [SB([128, 4096], BF16, "w") for _ in range(NW)]
        wnext = [0]
        PSF = []
        for i in range(8):
            t_ = es.enter_context(nc.psum_tensor("psf%d" % i, [128, 512], F32))
            PSF.append((t_, Res("psf%d" % i)))

        class _BV:
            def __init__(self, t_):
                self.v = t_[:].bitcast(BF16)

            def __getitem__(self, k):
                return self.v[k]
        PSB = [(_BV(PSF[6][0]), PSF[6][1]), (_BV(PSF[7][0]), PSF[7][1])]
        bank_i = [0]

        def bank():
            b = PSF[bank_i[0] % 6]
            bank_i[0] += 1
            return b

        def pc(name, i=0, n=1, parts=128):
            o = PC[name] + i
            return pcol[0:parts, o:o + n]

        dma("sp", pcol[:], pcol_d, [], [r_const], r_const)
        dma("sp", ctab[:], ctab_d, [], [r_const], r_const)
        dma("sp", identb[:], identb_d, [], [r_const], r_const)
        P.op("pool", lambda e: e.memset(ones32[:], 1.0), [], [r_const])
        P.op("pool", lambda e: e.memset(onesb[:], 1.0), [], [r_const])
        act(LG[:], ctab[:, CT["logit"]:CT["logit"] + 16], AF.Exp, [r_const], [r_LG], scale=-1.0)
        ts("dve", LG[:], LG[:], 1.0, ALU.add, [r_LG], [r_LG])
        act(LG[:], LG[:], AF.Ln, [r_LG], [r_LG])
        ts("dve", LG[:], LG[:], -1.0, ALU.mult, [r_LG], [r_LG])

        P.barrier()
        cp("dve", normwb[:], ctab[:, CT["normw"]:CT["normw"] + 128], [r_const], [r_const])
        P.barrier()
        for k in range(8):
            dma("sp", x[:, k, :], xT_d[k * 128:(k + 1) * 128, :], [], xr[k], xr[k][0])

        def wslot(view_k, ncols):
            i = wnext[0] % NW
            wnext[0] += 1
            wt, wr = wslots[i]
            v = wt[:, 0:view_k * ncols].rearrange("p (k c) -> p k c", k=view_k)
            return v, wr

        def w_in_src(l, c0, ncols):
            return w_in_d[l].rearrange("(k p) c -> p k c", p=128)[:, :, c0:c0 + ncols]

        def w_in_cols(l, c0, ncols=512):
            v, wr = wslot(8, ncols)
            dma("pool", v, w_in_src(l, c0, ncols), [], [wr], wr)
            return v, wr

        def bcol(t_, off, name):
            return t_.rearrange("p (s t) -> p s t", t=256)[:, :, off]

        def mod_and_norm(l):
            act(sc[:], pc("cvec", 0, 8), AF.Silu, [r_const], [r_sc])
            with scope() as ls:
                scb, r_scb = SB([128, 8], BF16, "scb", ls)
                cp("dve", scb[:], sc[:], [r_sc], [r_scb])
                aws = [SB([128, 8, 512], BF16, "aw", ls) for _ in range(3)]
                pst, psr = bank()
                for pi in range(6):
                    aw, awr = aws[pi % 3]
                    dma("pool", aw[:], ada_w_d[l].rearrange("(k p) c -> p k c", p=128)[:, :, pi * 512:(pi + 1) * 512], [], [awr], awr)
                    for jj in range(4):
                        j = pi * 4 + jj
                        for k in range(8):
                            mm(pst[:, j:j + 1], aw[:, k, jj * 128:(jj + 1) * 128], scb[:, k:k + 1], k == 0, k == 7, [awr, r_scb], [psr])
                tt("dve", modt[:], pst[:, 0:24], pc("ada_b", l * 24, 24), ALU.add, [psr, r_const], [r_mod])
                ts("dve", Avec[:], modt[:, 8:16], 1.0, ALU.add, [r_mod], [r_A])
                tt("dve", Avec[:], Avec[:], pc("normg", l * 8, 8), ALU.mult, [r_A, r_const], [r_A])
                sqs = [SB([128, 512], BF16, "sq", ls) for _ in range(2)]
                rstds = [SB([128, 512], F32, "rstd", ls) for _ in range(2)]
                tmps = [SB([128, 512], F32, "tmp", ls) for _ in range(2)]
                for g in range(4):
                    rstd, r_rstd = rstds[g % 2]
                    pst, psr = bank()
                    for k in range(8):
                        sq, sqr = sqs[k % 2]
                        act(sq[:], x[:, k, g * 512:(g + 1) * 512], AF.Square, [xr[k][g]], [sqr])
                        mm(pst[:], onesb[:], sq[:], k == 0, k == 7, [sqr, r_const], [psr])
                    ts("dve", rstd[:], pst[:], 1.0 / 1024, ALU.mult, [psr], [r_rstd], s2=EPS, op1=ALU.add)
                    act(rstd[:], rstd[:], AF.Ln, [r_rstd], [r_rstd])
                    act(rstd[:], rstd[:], AF.Exp, [r_rstd], [r_rstd], scale=-0.5)
                    for k in range(8):
                        tmp, tmr = tmps[k % 2]
                        tt("dve" if k % 2 == 0 else "pool", tmp[:], x[:, k, g * 512:(g + 1) * 512], rstd[:], ALU.mult, [xr[k][g], r_rstd], [tmr])
                        act(hT[:, k, g * 512:(g + 1) * 512], tmp[:], AF.Identity, [tmr, r_A, r_mod], [hr[k][g]], scale=Avec[:, k:k + 1], bias=modt[:, k:k + 1])
                P.barrier()

        def out_proj(l, row0, nch, ymix, r_y):
            wo, wor = wslot(nch, 1024)
            dma("pool", wo, w_out_d[l, row0:row0 + nch * 128, :].rearrange("(k p) c -> p k c", p=128), [], [wor], wor)
            for kc in range(8):
                for g in range(4):
                    pst, psr = bank()
                    for c in range(nch):
                        mm(pst[:], wo[:, c, kc * 128:(kc + 1) * 128], ymix[:, c, g * 512:(g + 1) * 512], c == 0, c == nch - 1, [wor, r_y], [psr])
                    stt(x[:, kc, g * 512:(g + 1) * 512], pst[:], modt[:, 16 + kc:17 + kc], x[:, kc, g * 512:(g + 1) * 512], ALU.mult, ALU.add,
                        [psr, r_mod, xr[kc][g]], [xr[kc][g]])

        def retention(l):
            s = 128.0 ** -0.5
            with scope() as ls:
                ropec, r_rope = SB([128, NT, 64], F32, "ropec", ls)
                ropes, _ = SB([128, NT, 64], F32, "ropes", ls)
                dma("sp", ropec[:].rearrange("p n f -> p (n f)"), ropec_d, [], [r_rope], r_rope)
                dma("sp", ropes[:].rearrange("p n f -> p (n f)"), ropes_d, [], [r_rope], r_rope)

                def load_qkvg(hp_):
                    WA_, rWA_ = wslot(8, 512)
                    dma("pool", WA_[:, :, 0:256], w_in_src(l, hp_ * 256, 256), [], [rWA_], rWA_)
                    dma("pool", WA_[:, :, 256:512], w_in_src(l, 512 + hp_ * 256, 256), [], [rWA_], rWA_)
                    WB_, rWB_ = wslot(8, 512)
                    dma("pool", WB_[:, :, 0:256], w_in_src(l, 1024 + hp_ * 256, 256), [], [rWB_], rWB_)
                    dma("pool", WB_[:, :, 256:512], w_in_src(l, 1536 + hp_ * 256, 256), [], [rWB_], rWB_)
                    return WA_, rWA_, WB_, rWB_
                preW = load_qkvg(0)
                DT, r_DT = SB([128, 4, 128], F32, "DT", ls)
                XF, r_XF = SB([128, 4, 128], F32, "XF", ls)
                XB, _ = SB([128, 4, 128], F32, "XB", ls)
                ZF, r_Z = SB([128, 512], F32, "ZF", ls)
                ZB, _ = SB([128, 512], F32, "ZB", ls)
                zc, r_zc = SB([128, 8], F32, "zc", ls)
                g128, r_g128 = SB([128, 8], F32, "g128", ls)
                DEC, r_DEC = SB([128, 2, NT, 4], F32, "DEC", ls)
                tA, r_tA = SB([128, 128], F32, "tA", ls)
                for h in range(4):
                    lf = LG[:, l * 8 + h:l * 8 + h + 1]
                    lb = LG[:, l * 8 + 4 + h:l * 8 + 4 + h + 1]
                    ts("dve", tA[:], ctab[:, CT["pos"]:CT["pos"] + 128], lf, ALU.mult, [r_const, r_LG], [r_tA])
                    stt(tA[:], ctab[:, CT["neg"]:CT["neg"] + 128], lb, tA[:], ALU.mult, ALU.add, [r_const, r_LG, r_tA], [r_tA])
                    act(tA[:], tA[:], AF.Exp, [r_tA], [r_tA])
                    stt(DT[:, h, :], tA[:], s, ctab[:, CT["eyes"]:CT["eyes"] + 128], ALU.mult, ALU.add, [r_tA, r_const], [r_DT])
                    act(XF[:, h, :], ctab[:, CT["iota1"]:CT["iota1"] + 128], AF.Exp, [r_const, r_LG], [r_XF], scale=lf)
                    act(XB[:, h, :], ctab[:, CT["rev"]:CT["rev"] + 128], AF.Exp, [r_const, r_LG], [r_XF], scale=lb)
                    act(zc[:, h:h + 1], lf, AF.Exp, [r_LG, r_const], [r_zc], scale=pc("col127"))
                    act(zc[:, 4 + h:5 + h], lb, AF.Exp, [r_LG, r_const], [r_zc], scale=pc("colp"))
                ts("dve", zc[:], zc[:], s, ALU.mult, [r_zc], [r_zc])
                for h in range(4):
                    ts("dve", ZF[:, h * 128:(h + 1) * 128], ones32[:], zc[:, h:h + 1], ALU.mult, [r_const, r_zc], [r_Z])
                    ts("dve", ZB[:, h * 128:(h + 1) * 128], ones32[:], zc[:, 4 + h:5 + h], ALU.mult, [r_const, r_zc], [r_Z])
                act(g128[:], LG[:, l * 8:l * 8 + 8], AF.Exp, [r_LG], [r_g128], scale=128.0)
                for d in range(2):
                    for n in range(NT):
                        cm = ctab[:, CT["cmask"] + d * 16 + n:CT["cmask"] + d * 16 + n + 1]
                        ts("dve", DEC[:, d, n, :], g128[:, d * 4:d * 4 + 4], cm, ALU.mult, [r_g128, r_const], [r_DEC])
                s1, r_s1 = SB([128, 4, 64], F32, "s1", ls)
                s2, r_s2 = SB([128, 4, 64], F32, "s2", ls)
                s3, r_s3 = SB([128, 4, 64], F32, "s3", ls)
                s4, r_s4 = SB([128, 4, 64], F32, "s4", ls)
                P.barrier()
                for hp in range(2):
                    with scope() as hs:
                        qk, _ = SB([128, NT, 512], BF16, "qk", hs)
                        vt, _ = SB([128, NT, 256], BF16, "vt", hs)
                        ymix, r_y = SB([128, 2, T], BF16, "ymix", hs)
                        RB, _ = SB([128, NT, 256], BF16, "RB", hs)
                        qk_r = [Res("qk%d" % n) for n in range(NT)]
                        vn_r = [Res("vn%d" % n) for n in range(NT)]
                        rbn_r = [Res("rb%d" % n) for n in range(NT)]
                        WA, rWA, WB, rWB = preW if hp == 0 else load_qkvg(hp)
                        for n in range(NT):
                            g = n // 4
                            pq, pqr = bank()
                            pv_, pvr = bank()
                            for k in range(8):
                                mm(pq[:], hT[:, k, n * 128:(n + 1) * 128], WA[:, k, :], k == 0, k == 7, [hr[k][g], rWA], [pqr])
                            for k in range(8):
                                mm(pv_[:, 0:256], hT[:, k, n * 128:(n + 1) * 128], WB[:, k, 0:256], k == 0, k == 7, [hr[k][g], rWB], [pvr])
                            pv = pq[:].rearrange("p (h two f) -> p h two f", h=4, two=2)
                            dv = qk[:, n, :].rearrange("p (h two f) -> p h two f", h=4, two=2)
                            cb = ropec[:, n, :].unsqueeze(1).to_broadcast([128, 4, 64])
                            sb_ = ropes[:, n, :].unsqueeze(1).to_broadcast([128, 4, 64])
                            tt("dve", s1[:], pv[:, :, 0, :], cb, ALU.mult, [pqr, r_rope], [r_s1])
                            tt("dve", s2[:], pv[:, :, 1, :], sb_, ALU.mult, [pqr, r_rope], [r_s2])
                            tt("pool", dv[:, :, 0, :], s1[:], s2[:], ALU.subtract, [r_s1, r_s2], [qk_r[n]])
                            tt("dve", s3[:], pv[:, :, 0, :], sb_, ALU.mult, [pqr, r_rope], [r_s3])
                            tt("dve", s4[:], pv[:, :, 1, :], cb, ALU.mult, [pqr, r_rope], [r_s4])
                            tt("pool", dv[:, :, 1, :], s3[:], s4[:], ALU.add, [r_s3, r_s4], [qk_r[n]])
                            cp("act", vt[:, n, :], pv_[:, 0:256], [pvr], [vn_r[n]])
                        Sf, r_Sf = SB([128, 2, 128], F32, "Sf", hs)
                        Sb, r_Sb = SB([128, 2, 128], F32, "Sb", hs)
                        RF, r_RF = SB([128, 256], BF16, "RF", hs)
                        vz, r_vz = SB([128, 256], BF16, "vz", hs)
                        dma("sp", Sf[:], s0ret_d[l, 0, hp * 2:hp * 2 + 2].rearrange("h d e -> d h e"), [], [r_Sf], r_Sf)
                        dma("sp", Sb[:], s0ret_d[l, 1, hp * 2:hp * 2 + 2].rearrange("h d e -> d h e"), [], [r_Sb], r_Sb)
                        H0 = hp * 2
                        for n in range(NT - 1, -1, -1):
                            cmb = ctab[:, CT["cmask"] + 16 + n:CT["cmask"] + 17 + n]
                            ts("dve", RB[:, n, :], Sb[:].rearrange("p h e -> p (h e)"), cmb, ALU.mult, [r_Sb, r_const], [rbn_r[n]])
                            tt("pool", vz[:], vt[:, n, :], ZB[:, H0 * 128:H0 * 128 + 256], ALU.mult, [vn_r[n], r_Z], [r_vz])
                            pst, psr = bank()
                            for hh in range(2):
                                mm(pst[:, hh * 128:(hh + 1) * 128], qk[:, n, 256 + hh * 128:256 + (hh + 1) * 128], vz[:, hh * 128:(hh + 1) * 128], True, True, [qk_r[n], r_vz], [psr])
                            for hh in range(2):
                                stt(Sb[:, hh, :], Sb[:, hh, :], DEC[:, 1, n, H0 + hh:H0 + hh + 1], pst[:, hh * 128:(hh + 1) * 128], ALU.mult, ALU.add, [r_Sb, r_DEC, psr], [r_Sb])
                            if n % 2 == 0:
                                dma("sp", stret_d[l, 1, n // 2, H0:H0 + 2].rearrange("h d e -> d h e"), Sb[:], [r_Sb], [], r_Sb)
                        qT, r_qT = SB([128, 2, 128], BF16, "qT", hs)
                        qxf, r_qxf = SB([128, 2, 128], BF16, "qxf", hs)
                        qxb, r_qxb = SB([128, 2, 128], BF16, "qxb", hs)
                        kT, r_kT = SB([128, 2, 128], BF16, "kT", hs)
                        PT, r_PT = SB([128, 2, 128], BF16, "PT", hs)
                        sqn, r_sqn = SB([128, 512], BF16, "sqn", hs)
                        rn_, r_rn = SB([128, 512], F32, "rn", hs)
                        y1, r_y1 = SB([128, 512], F32, "y1", hs)
                        sgt, r_sgt = SB([128, 512], F32, "sgt", hs)
                        obanks = PSF[0:2]
                        sbank = PSF[4]
                        kbank = PSF[5]
                        for n in range(NT):
                            cmf = ctab[:, CT["cmask"] + n:CT["cmask"] + n + 1]
                            ts("dve", RF[:], Sf[:].rearrange("p h e -> p (h e)"), cmf, ALU.mult, [r_Sf, r_const], [r_RF])
                            tt("pool", vz[:], vt[:, n, :], ZF[:, H0 * 128:H0 * 128 + 256], ALU.mult, [vn_r[n], r_Z], [r_vz])
                            pb, pbr = PSB[n % 2]
                            for i in range(4):
                                tr(pb[:, i * 128:(i + 1) * 128], qk[:, n, i * 128:(i + 1) * 128], [qk_r[n]], [pbr])
                            pq3 = pb[:, 0:256].rearrange("p (h i) -> p h i", h=2)
                            cp("act", qT[:], pq3, [pbr], [r_qT])
                            tt("dve", qxf[:], pq3, XF[:, H0:H0 + 2, :], ALU.mult, [pbr, r_XF], [r_qxf])
                            tt("dve", qxb[:], pq3, XB[:, H0:H0 + 2, :], ALU.mult, [pbr, r_XF], [r_qxb])
                            cp("act", kT[:], pb[:, 256:512].rearrange("p (h i) -> p h i", h=2), [pbr], [r_kT])
                            pss, pssr = sbank
                            for hh in range(2):
                                mm(pss[:, hh * 128:(hh + 1) * 128], kT[:, hh, :], qT[:, hh, :], True, True, [r_kT, r_qT], [pssr])
                            tt("dve", PT[:], pss[:, 0:256].rearrange("p (h i) -> p h i", h=2), DT[:, H0:H0 + 2, :], ALU.mult, [pssr, r_DT], [r_PT])
                            c0 = (n % 4) * 128
                            for hh in range(2):
                                po, por = obanks[hh]
                                mm(po[:, c0:c0 + 128], vt[:, n, hh * 128:(hh + 1) * 128], PT[:, hh, :], True, False, [vn_r[n], r_PT], [por])
                                mm(po[:, c0:c0 + 128], RF[:, hh * 128:(hh + 1) * 128], qxf[:, hh, :], False, False, [r_RF, r_qxf], [por])
                                mm(po[:, c0:c0 + 128], RB[:, n, hh * 128:(hh + 1) * 128], qxb[:, hh, :], False, True, [rbn_r[n], r_qxb], [por])
                            pkv, pkvr = kbank
                            for hh in range(2):
                                mm(pkv[:, hh * 128:(hh + 1) * 128], qk[:, n, 256 + hh * 128:256 + (hh + 1) * 128], vz[:, hh * 128:(hh + 1) * 128], True, True, [qk_r[n], r_vz], [pkvr])
                            for hh in range(2):
                                stt(Sf[:, hh, :], Sf[:, hh, :], DEC[:, 0, n, H0 + hh:H0 + hh + 1], pkv[:, hh * 128:(hh + 1) * 128], ALU.mult, ALU.add, [r_Sf, r_DEC, pkvr], [r_Sf])
                            if n % 2 == 1:
                                dma("sp", stret_d[l, 0, n // 2, H0:H0 + 2].rearrange("h d e -> d h e"), Sf[:], [r_Sf], [], r_Sf)
                            if n % 4 == 3:
                                g = n // 4
                                for hh in range(2):
                                    po, por = obanks[hh]
                                    act(sqn[:], po[:], AF.Square, [por], [r_sqn])
                                    pss, pssr = sbank
                                    mm(pss[:], onesb[:], sqn[:], True, True, [r_sqn, r_const], [pssr])
                                    ts("dve", rn_[:], pss[:], 1.0 / 128, ALU.mult, [pssr], [r_rn], s2=EPS, op1=ALU.add)
                                    act(rn_[:], rn_[:], AF.Ln, [r_rn], [r_rn])
                                    act(rn_[:], rn_[:], AF.Exp, [r_rn], [r_rn], scale=-0.5)
                                    tt("dve", y1[:], po[:], rn_[:], ALU.mult, [por, r_rn], [r_y1])
                                    pg, pgr = PSF[2 + hh]
                                    for k in range(8):
                                        mm(pg[:], WB[:, k, 256 + hh * 128:256 + (hh + 1) * 128], hT[:, k, g * 512:(g + 1) * 512], k == 0, k == 7, [rWB, hr[k][g]], [pgr])
                                    act(sgt[:], pg[:], AF.Silu, [pgr], [r_sgt])
                                    tt("pool", ymix[:, hh, g * 512:(g + 1) * 512], y1[:], sgt[:], ALU.mult, [r_y1, r_sgt], [r_y])
                        out_proj(l, hp * 256, 2, ymix, r_y)
                        P.barrier()

        sv8, r_sv8 = SB([128, 16], F32, "sv8")

        def proj_conv(W, rW, wc, taps, b_ap, dst, dst_r, zfull, r_z):
            for g in range(4):
                pst, psr = bank()
                for k in range(8):
                    mm(pst[:], W[:, k, wc * 128:(wc + 1) * 128], hT[:, k, g * 512:(g + 1) * 512], k == 0, k == 7, [rW, hr[k][g]], [psr])
                cp("act", zfull[:, g * 512:(g + 1) * 512], pst[:], [psr], [r_z])
            for off, wap in taps:
                if off == 0:
                    act(dst, zfull[:], AF.Identity, [r_z, r_const], [dst_r], scale=wap, bias=b_ap)
            nf8 = ctab[:, CT["nf8"]:CT["nf8"] + 8]
            nl8 = ctab[:, CT["nl8"]:CT["nl8"] + 8]
            for off, wap in taps:
                if off == 0:
                    continue
                if off < 0:
                    o = -off
                    cols = [bcol(zfull[:], 255 - i, "z") for i in range(o)]
                    mk = nl8
                else:
                    cols = [bcol(zfull[:], 0, "z")]
                    mk = nf8
                for i, cl in enumerate(cols):
                    cp("dve", sv8[:, i * 8:(i + 1) * 8], cl, [r_z], [r_sv8])
                    tt("dve", cl, sv8[:, i * 8:(i + 1) * 8], mk, ALU.mult, [r_sv8, r_const], [r_z])
                if off < 0:
                    stt(dst[:, o:T], zfull[:, 0:T - o], wap, dst[:, o:T], ALU.mult, ALU.add, [r_z, r_const, dst_r], [dst_r])
                else:
                    stt(dst[:, 0:T - off], zfull[:, off:T], wap, dst[:, 0:T - off], ALU.mult, ALU.add, [r_z, r_const, dst_r], [dst_r])
                for i, cl in enumerate(cols):
                    cp("dve", cl, sv8[:, i * 8:(i + 1) * 8], [r_sv8, dst_r], [r_z])

        def rglru(l):
            with scope() as ls:
                ymix, r_y = SB([128, 4, T], BF16, "ymixl", ls)
                S = [SB([128, T], F32, "S%d" % i, ls) for i in range(6)]
                ub, r_ub = SB([128, T], BF16, "ub", ls)
                gwt = [SB([128, 128], BF16, "gw", ls) for _ in range(16)]
                scv, r_scv = SB([128, 16], F32, "scv", ls)
                stc, r_stc = SB([128, 8], F32, "stc", ls)
                sgl, r_sgl = SB([128, 512], F32, "sgl", ls)
                act(scv[:, 0:8], pc("lru_lam", l * 8, 8), AF.Exp, [r_const], [r_scv], scale=-1.0)
                ts("dve", scv[:, 0:8], scv[:, 0:8], 1.0, ALU.add, [r_scv], [r_scv])
                act(scv[:, 0:8], scv[:, 0:8], AF.Ln, [r_scv], [r_scv])
                ts("dve", scv[:, 8:16], scv[:, 0:8], -16.0, ALU.mult, [r_scv], [r_scv])
                ts("dve", scv[:, 0:8], scv[:, 0:8], -8.0, ALU.mult, [r_scv], [r_scv])
                Wx, rWx = w_in_cols(l, 4096)
                Wgl, rWgl = w_in_cols(l, 4608)
                for gt, gr in gwt:
                    P.op("pool", lambda e, o=gt: e.memset(o[:], 0.0), [], [gr])
                for c_ in range(4):
                    for d_ in range(2):
                        for gi_ in range(2):
                            gt, gr = gwt[(c_ * 2 + d_) * 2 + gi_]
                            for bb in range(2):
                                dma("pool", gt[bb * 64:(bb + 1) * 64, bb * 64:(bb + 1) * 64], lgw_d[l, d_, gi_, c_ * 2 + bb], [], [gr], gr)
                nf8 = ctab[:, CT["nf8"]:CT["nf8"] + 8]
                nl8 = ctab[:, CT["nl8"]:CT["nl8"] + 8]
                for c in range(4):
                    (zfull, r_z), (u, r_u), (scr, r_scr), (h1, r_h1), (h2, r_h2), (rg1, r_rg1) = S
                    taps = [(j - 2, pc("lru_cw", (l * 4 + j) * 4 + c)) for j in range(4)]
                    proj_conv(Wx, rWx, c, taps, pc("lru_cb", l * 4 + c), u[:], r_u, zfull, r_z)
                    cp("pool", ub[:], u[:], [r_u], [r_ub])
                    for d in range(2):
                        gws = []
                        for gi in range(2):
                            gws.append(gwt[(c * 2 + d) * 2 + gi])
                        rg, r_rg = (zfull, r_z) if d == 0 else (rg1, r_rg1)
                        ig, r_ig = (h2, r_h2)
                        for gi, (dstt, dstr) in enumerate(((rg, r_rg), (ig, r_ig))):
                            for g in range(4):
                                pst, psr = bank()
                                mm(pst[:], gws[gi][0][:], ub[:, g * 512:(g + 1) * 512], True, True, [gws[gi][1], r_ub], [psr])
                                act(dstt[:, g * 512:(g + 1) * 512], pst[:], AF.Sigmoid, [psr, r_const], [dstr],
                                    bias=pc("lru_gb", ((l * 2 + d) * 2 + gi) * 4 + c))
                        act(scr[:], rg[:], AF.Exp, [r_rg, r_scv], [r_scr], scale=scv[:, 8 + d * 4 + c:9 + d * 4 + c])
                        act(rg[:], rg[:], AF.Exp, [r_rg, r_scv], [r_rg], scale=scv[:, d * 4 + c:d * 4 + c + 1])
                        ts("dve", scr[:], scr[:], 1.0, ALU.min, [r_scr], [r_scr], s2=-1.0, op1=ALU.mult)
                        act(scr[:], scr[:], AF.Sqrt, [r_scr, r_const], [r_scr], bias=ones32[:, 0:1])
                        tt("dve", scr[:], scr[:], ig[:], ALU.mult, [r_scr, r_ig], [r_scr])
                        tt("pool", scr[:], scr[:], u[:], ALU.mult, [r_scr, r_u], [r_scr])
                        h0 = pc("s0lru", (l * 2 + d) * 4 + c)
                        if d == 0:
                            cl = bcol(rg[:], 0, "a")
                            tt("dve", cl, cl, nf8, ALU.mult, [r_rg, r_const], [r_rg])
                            P.op("dve", lambda e, o=h1, a=rg, b=scr, i0=h0: e.tensor_tensor_scan(out=o[:], data0=a[:], data1=b[:], initial=i0, op0=ALU.mult, op1=ALU.add),
                                 [r_rg, r_scr, r_const], [r_h1])
                            cp("dve", stc[:], bcol(h1[:], 255, "h"), [r_h1], [r_stc])
                        else:
                            cl = bcol(rg[:], 255, "a")
                            tt("dve", cl, cl, nl8, ALU.mult, [r_rg, r_const], [r_rg])

                            def rev(t_):
                                xx = t_[:, :]
                                return AP(xx.tensor, xx.offset + T - 1, [list(xx.ap[0]), [-1, T]])
                            P.op("dve", lambda e, o=rev(h2), a=rev(rg), b=rev(scr), i0=h0: e.tensor_tensor_scan(out=o, data0=a, data1=b, initial=i0, op0=ALU.mult, op1=ALU.add),
                                 [r_rg, r_scr, r_const, r_ig], [r_h2])
                            cp("dve", stc[:], bcol(h2[:], 0, "h"), [r_h2], [r_stc])
                            tt("pool", h1[:], h1[:], h2[:], ALU.add, [r_h1, r_h2], [r_h1])
                        dma("sp", stlru_d[l, d, c], stc[:], [r_stc], [], r_stc)
                    for g in range(4):
                        pst, psr = bank()
                        for k in range(8):
                            mm(pst[:], Wgl[:, k, c * 128:(c + 1) * 128], hT[:, k, g * 512:(g + 1) * 512], k == 0, k == 7, [rWgl, hr[k][g]], [psr])
                        act(sgl[:], pst[:], AF.Silu, [psr], [r_sgl])
                        tt("dve", ymix[:, c, g * 512:(g + 1) * 512], h1[:, g * 512:(g + 1) * 512], sgl[:], ALU.mult, [r_h1, r_sgl], [r_y])
                out_proj(l, 1024, 4, ymix, r_y)
                P.barrier()

        def hyena(l):
            with scope() as ls:
                hidb, r_hid = SB([64, T], BF16, "hidb", ls)
                w3, r_w3 = SB([64, 2048], BF16, "w3", ls)
                dma("pool", w3[:], hw3_d[l], [], [r_w3], r_w3)
                Wv_pre = w_in_cols(l, 2048)
                with scope() as fs_:
                    zTt, r_zT = SB([33, T], F32, "zT", fs_)
                    w1, r_w1 = SB([33, 64], F32, "w1", fs_)
                    w2, r_w2 = SB([64, 64], F32, "w2", fs_)
                    fb, r_fb = SB([64, 4], F32, "fb", fs_)
                    frh, r_frh = SB([64, 4], F32, "frh", fs_)
                    sa, r_sa = SB([64, 512], F32, "sa", fs_)
                    sb4, r_sb4 = SB([64, 512], F32, "sb4", fs_)
                    hid1, r_hid1 = SB([64, T], F32, "hid1", fs_)
                    dma("sp", zTt[:], zT_d, [], [r_zT], r_zT)
                    dma("sp", w1[:], hw1_d[l], [], [r_w1], r_w1)
                    dma("sp", w2[:], hw2_d[l], [], [r_w2], r_w2)
                    for i, bn in enumerate(("hy_b1", "hy_b2")):
                        tt("dve", fb[:, i:i + 1], pc(bn, l, 1, 64), pc("hy_fr", l * 2 + i, 1, 64), ALU.mult, [r_const], [r_fb])
                        ts("dve", fb[:, 2 + i:3 + i], fb[:, i:i + 1], 0.25, ALU.mult, [r_fb], [r_fb])
                        ts("dve", fb[:, i:i + 1], fb[:, i:i + 1], 0.5, ALU.mult, [r_fb], [r_fb])
                        ts("dve", frh[:, i:i + 1], pc("hy_fr", l * 2 + i, 1, 64), 0.5, ALU.mult, [r_const], [r_frh])
                        ts("dve", frh[:, 2 + i:3 + i], pc("hy_fr", l * 2 + i, 1, 64), 0.25, ALU.mult, [r_const], [r_frh])

                    def sin_layer(i, lhsT, lr, rhs_t, rr, K, out_t, out_r):
                        for g in range(4):
                            pst, psr = bank()
                            mm(pst[0:64, :], lhsT, rhs_t[0:K, g * 512:(g + 1) * 512], True, True, [lr, rr], [psr])
                            act(sa[:], pst[0:64, :], AF.Sin, [psr, r_frh, r_fb], [r_sa], scale=frh[:, i:i + 1], bias=fb[:, i:i + 1])
                            act(sb4[:], pst[0:64, :], AF.Sin, [psr, r_frh, r_fb], [r_sb4], scale=frh[:, 2 + i:3 + i], bias=fb[:, 2 + i:3 + i])
                            tt("dve", sb4[:], sb4[:], sb4[:], ALU.mult, [r_sb4], [r_sb4])
                            ts("dve", sb4[:], sb4[:], -4.0, ALU.mult, [r_sb4], [r_sb4], s2=2.0, op1=ALU.add)
                            tt("dve", out_t[:, g * 512:(g + 1) * 512], sa[:], sb4[:], ALU.mult, [r_sa, r_sb4], [out_r])
                    sin_layer(0, w1[:], r_w1, zTt, r_zT, 33, hid1, r_hid1)
                    sin_layer(1, w2[:], r_w2, hid1, r_hid1, 64, hidb, r_hid)
                    P.barrier()

                Wv_, Wx1, Wx2, Wg_ = 2048, 2560, 3072, 3584

                for hh in range(2):
                    with scope() as hs:
                        Y, r_Y = SB([128, 32, 256], BF16, "Y", hs)
                        ufm, r_ufm = SB([128, 2, T], BF16, "ufm", hs)
                        ntn, r_ntn = SB([128, NT], F32, "ntn", hs)
                        ts("dve", ntn[:], pc("tnorm", 0, NT), -1.0, ALU.mult, [r_const], [r_ntn])

                        def conv_one(W, rW, cc, tapbase, cacc, r_cacc, zfull, r_z):
                            c = hh * 2 + cc
                            ch12 = tapbase * 4 + c
                            taps = [(j - 1, pc("hy_cw", (l * 3 + j) * 12 + ch12)) for j in range(3)]
                            proj_conv(W, rW, c, taps, pc("hy_cb", l * 12 + ch12), cacc[:], r_cacc, zfull, r_z)

                        def spectral(o):
                            with scope() as s2:
                                FU, r_FU = SB([128, NT, 768], BF16, "FU", s2)
                                fpre = [SB([128, NT, 128], BF16, "fstr", s2) for _ in range(2)]
                                dma("sp", fpre[0][0][:].rearrange("p n r -> p (n r)"), dftF_d[0], [], [fpre[0][1]], fpre[0][1])
                                dma("sp", fpre[1][0][:].rearrange("p n r -> p (n r)"), dftF_d[16], [], [fpre[1][1]], fpre[1][1])
                                for n in range(NT):
                                    pb, pbr = PSB[n % 2]
                                    for cc in range(2):
                                        tr(pb[:, cc * 128:(cc + 1) * 128], ufm[:, cc, n * 128:(n + 1) * 128], [r_ufm], [pbr])
                                    cp("act", FU[:, n, 512:768], pb[:, 0:256], [pbr], [r_FU])
                                with scope() as s3:
                                    decs = [SB([128, 256], F32, "dec", s3) for _ in range(2)]
                                    abs_ = [SB([128, 512], BF16, "ab", s3) for _ in range(3)]
                                    tas = [SB([128, 256], F32, "ta", s3) for _ in range(2)]
                                    tbs = [SB([128, 256], F32, "tb", s3) for _ in range(2)]
                                    rn_, r_rn = SB([128, 512], F32, "rnh", s3)
                                    fun_r = [Res("fun%d" % n) for n in range(NT)]
                                    cf = o * 1024 + hh * 256
                                    cb_ = o * 1024 + 512 + hh * 256
                                    pn, pnr = PSF[5]
                                    for n in range(NT + 2):
                                        if n < NT:
                                            pst, psr = PSF[n % 4]
                                            dec, r_dec = decs[n % 2]
                                            ab, r_ab = abs_[n % 3]
                                            mm(pst[:, 0:256], hidb[0:64, n * 128:(n + 1) * 128], w3[0:64, cf:cf + 256], True, True, [r_hid, r_w3], [psr])
                                            mm(pst[:, 256:512], hidb[0:64, n * 128:(n + 1) * 128], w3[0:64, cb_:cb_ + 256], True, True, [r_hid, r_w3], [psr])
                                            act(dec[:], ctab[:, CT["delta"] + hh * 256:CT["delta"] + hh * 256 + 256], AF.Exp, [r_const, r_ntn], [r_dec], scale=ntn[:, n:n + 1])
                                            tt("dve", FU[:, n, 0:512].rearrange("p (d c) -> p d c", d=2), pst[:].rearrange("p (d c) -> p d c", d=2),
                                               dec[:].unsqueeze(1).to_broadcast([128, 2, 256]), ALU.mult, [psr, r_dec], [fun_r[n], r_FU])
                                            act(ab[:], FU[:, n, 0:512], AF.Abs, [fun_r[n]], [r_ab])
                                        m = n - 2
                                        if m >= 0:
                                            ab, r_ab = abs_[m % 3]
                                            mm(pn[:], normwb[:], ab[:], m == 0, m == NT - 1, [r_const, r_ab], [pnr])
                                    ts("dve", rn_[:], pn[:], EPS, ALU.add, [pnr], [r_rn])
                                    P.op("dve", lambda e, r=rn_: e.reciprocal(out=r[:], in_=r[:]), [r_rn], [r_rn])
                                    for n in range(NT):
                                        ta, r_ta = tas[n % 2]
                                        tb, r_tb = tbs[n % 2]
                                        tt("dve", ta[:], FU[:, n, 0:256], rn_[:, 0:256], ALU.mult, [fun_r[n], r_rn], [r_ta])
                                        tt("dve", tb[:], FU[:, n, 256:512], rn_[:, 256:512], ALU.mult, [fun_r[n], r_rn], [r_tb])
                                        tt("dve", FU[:, n, 0:256], ta[:], tb[:], ALU.add, [r_ta, r_tb], [fun_r[n], r_FU])
                                        tt("pool", FU[:, n, 256:512], ta[:], tb[:], ALU.subtract, [r_ta, r_tb], [fun_r[n], r_FU])
                                with scope() as s4:
                                    fstr = fpre + [SB([128, NT, 128], BF16, "fstr", s4) for _ in range(NFS - 2)]
                                    U0, r_U0 = SB([128, 512], F32, "U0", s4)
                                    U1, r_U1 = SB([128, 512], F32, "U1", s4)
                                    t1, r_t1 = SB([128, 256], F32, "t1", s4)
                                    t2, r_t2 = SB([128, 256], F32, "t2", s4)
                                    for j in range(16):
                                        fr_, frr = fstr[(2 * j) % NFS]
                                        fi_, fir = fstr[(2 * j + 1) % NFS]
                                        if j > 0:
                                            dma("sp", fr_[:].rearrange("p n r -> p (n r)"), dftF_d[j], [], [frr], frr)
                                            dma("sp", fi_[:].rearrange("p n r -> p (n r)"), dftF_d[16 + j], [], [fir], fir)
                                        pre, prer = PSF[(2 * j) % 6]
                                        pim, pimr = PSF[(2 * j + 1) % 6]
                                        for n in range(NT):
                                            mm(pre[:, 0:256], fr_[:, n, :], FU[:, n, 0:256], n == 0, n == NT - 1, [frr, r_FU], [prer])
                                        for n in range(NT):
                                            mm(pre[:, 256:512], fr_[:, n, :], FU[:, n, 512:768], n == 0, n == NT - 1, [frr, r_FU], [prer])
                                        for n in range(NT):
                                            mm(pim[:], fi_[:, n, :], FU[:, n, 256:768], n == 0, n == NT - 1, [fir, r_FU], [pimr])
                                        cp("act", U0[:], pre[:], [prer], [r_U0])
                                        cp("act", U1[:], pim[:], [pimr], [r_U1])
                                        tt("dve", t1[:], U0[:, 256:512], U0[:, 0:256], ALU.mult, [r_U0], [r_t1])
                                        tt("pool", t2[:], U1[:, 256:512], U1[:, 0:256], ALU.mult, [r_U1], [r_t2])
                                        tt("dve", Y[:, j, :], t1[:], t2[:], ALU.subtract, [r_t1, r_t2], [r_Y])
                                        tt("dve", t1[:], U0[:, 256:512], U1[:, 0:256], ALU.mult, [r_U0, r_U1], [r_t1])
                                        tt("pool", t2[:], U1[:, 256:512], U0[:, 0:256], ALU.mult, [r_U0, r_U1], [r_t2])
                                        tt("dve", Y[:, 16 + j, :], t1[:], t2[:], ALU.add, [r_t1, r_t2], [r_Y])

                        def inverse(Wpre, tapbase, o, gate):
                            with scope() as s5:
                                istr = [SB([128, T], BF16, "istr", s5) for _ in range(4)]
                                zfull, r_z = SB([128, T], F32, "zf", s5)
                                cacc, r_cacc = SB([128, T], F32, "cacc", s5)
                                xc, r_xc = SB([128, 2, T], BF16, "xc", s5)
                                ev1, r_ev1 = SB([128, 512], F32, "ev1", s5)
                                sgh, r_sgh = SB([128, 512], F32, "sgh", s5)
                                (W, rW), gpre = Wpre
                                if gate:
                                    Wg, rWg = gpre
                                for j in range(4):
                                    it, itr = istr[j % 4]
                                    dma("sp", it[:], dftI_d[j], [], [itr], itr)
                                for cc in range(2):
                                    c = hh * 2 + cc
                                    conv_one(W, rW, cc, tapbase, cacc, r_cacc, zfull, r_z)
                                    if gate:
                                        for g in range(4):
                                            sl = slice(g * 512, (g + 1) * 512)
                                            pg, pgr = bank()
                                            for k in range(8):
                                                mm(pg[:], Wg[:, k, c * 128:(c + 1) * 128], hT[:, k, sl], k == 0, k == 7, [rWg, hr[k][g]], [pgr])
                                            act(sgh[:], pg[:], AF.Silu, [pgr], [r_sgh])
                                            tt("dve", xc[:, cc, sl], cacc[:, sl], sgh[:], ALU.mult, [r_cacc, r_sgh], [r_xc])
                                    else:
                                        cp("pool", xc[:, cc, :], cacc[:], [r_cacc], [r_xc])
                                for j in range(32):
                                    it, itr = istr[j % 4]
                                    if j >= 4:
                                        dma("sp", it[:], dftI_d[j], [], [itr], itr)
                                    for cc in range(2):
                                        for g in range(4):
                                            pst, psr = PSF[cc * 4 + g]
                                            mm(pst[:], Y[:, j, cc * 128:(cc + 1) * 128], it[:, g * 512:(g + 1) * 512], j == 0, j == 31, [r_Y, itr], [psr])
                                for cc in range(2):
                                    for g in range(4):
                                        sl = slice(g * 512, (g + 1) * 512)
                                        pst, psr = PSF[cc * 4 + g]
                                        stt(ev1[:], ufm[:, cc, sl], pc("hy_bias", (l * 2 + o) * 4 + hh * 2 + cc), pst[:], ALU.mult, ALU.add, [r_ufm, r_const, psr], [r_ev1])
                                        tt("pool", ufm[:, cc, sl], ev1[:], xc[:, cc, sl], ALU.mult, [r_ev1, r_xc], [r_ufm])

                        with scope() as s1:
                            zfull, r_z = SB([128, T], F32, "zf", s1)
                            cacc, r_cacc = SB([128, T], F32, "cacc", s1)
                            W, rW = Wv_pre if hh == 0 else w_in_cols(l, Wv_)
                            for cc in range(2):
                                conv_one(W, rW, cc, 0, cacc, r_cacc, zfull, r_z)
                                cp("pool", ufm[:, cc, :], cacc[:], [r_cacc], [r_ufm])
                        pre1 = (w_in_cols(l, Wx1), None)
                        spectral(0)
                        inverse(pre1, 1, 0, False)
                        pre2 = (w_in_cols(l, Wx2), w_in_cols(l, Wg_))
                        spectral(1)
                        inverse(pre2, 2, 1, True)
                        out_proj(l, 512 + hh * 256, 2, ufm, r_ufm)

        for l in range(DEPTH):
            mod_and_norm(l)
            if STAGE in ("all", "ret"):
                retention(l)
            if STAGE in ("all", "lru"):
                rglru(l)
            if STAGE in ("all", "hy"):
                hyena(l)
        with scope() as ls:
            outs = [SB([128, 512], F32, "ob", ls) for _ in range(4)]
            oi = [0]
            sqs = [SB([128, 512], BF16, "sq", ls) for _ in range(2)]
            rstds = [SB([128, 512], F32, "rstd", ls) for _ in range(2)]
            for g in range(4):
                rstd, r_rstd = rstds[g % 2]
                pst, psr = bank()
                for k in range(8):
                    sq, sqr = sqs[k % 2]
                    act(sq[:], x[:, k, g * 512:(g + 1) * 512], AF.Square, [xr[k][g]], [sqr])
                    mm(pst[:], onesb[:], sq[:], k == 0, k == 7, [sqr, r_const], [psr])
                ts("dve", rstd[:], pst[:], 1.0 / 1024, ALU.mult, [psr], [r_rstd], s2=EPS, op1=ALU.add)
                act(rstd[:], rstd[:], AF.Ln, [r_rstd], [r_rstd])
                act(rstd[:], rstd[:], AF.Exp, [r_rstd], [r_rstd], scale=-0.5)
                for k in range(8):
                    ob, obr = outs[oi[0] % 4]
                    oi[0] += 1
                    stt(ob[:], x[:, k, g * 512:(g + 1) * 512], pc("final_g", k), rstd[:], ALU.mult, ALU.mult, [xr[k][g], r_const, r_rstd], [obr])
                    dma("sp", yT_d[k * 128:(k + 1) * 128, g * 512:(g + 1) * 512], ob[:], [obr], [], obr)
            P.barrier()
        P.emit()
    return nc


def _bf16(a):
    return np.asarray(a, dtype=np.float32).astype(ml_dtypes.bfloat16)


def _dft_mats(L, nseq):
    N = 2 * L
    t = np.arange(L, dtype=np.int64)
    k = np.arange(L, dtype=np.int64)
    ph = ((2 * k[None, :] + 1) * t[:, None]) % (2 * N)
    ang = ph.astype(np.float64) * (math.pi / N)
    C = np.cos(ang)
    S = -np.sin(ang)
    Fwd = np.zeros((T, 2 * T), np.float32)
    for b in range(nseq):
        Fwd[b * L:(b + 1) * L, b * L:(b + 1) * L] = C
        Fwd[b * L:(b + 1) * L, T + b * L:T + (b + 1) * L] = S
    dftF = np.ascontiguousarray(Fwd.reshape(NT, 128, 32, 128).transpose(2, 1, 0, 3)).reshape(32, 128, NT * 128)
    Inv = (Fwd.T * (2.0 / N)).astype(np.float32)
    dftI = np.ascontiguousarray(Inv.reshape(32, 128, T))
    return _bf16(dftF), _bf16(dftI)


def _core_consts(L, nseq, rope_on):
    f32 = np.float32
    pos_in = np.arange(T) % L
    m_int = 1.0 if L == T else 0.0
    nf8 = np.array([1.0] + [m_int] * 7, f32)
    nl8 = np.array([m_int] * 7 + [1.0], f32)
    masks = (nf8, nl8)
    if rope_on:
        rows = T // 64
        row = np.repeat(np.arange(rows, dtype=f32), 64)
        col = np.tile(np.arange(64, dtype=f32), rows)
        inv = (f32(10000.0) ** (-np.arange(32, dtype=f32) / f32(32))).astype(f32)
        ang = np.concatenate([row[:, None] * inv[None], col[:, None] * inv[None]], axis=-1).astype(f32)
        c, s = np.cos(ang).astype(f32), np.sin(ang).astype(f32)
    else:
        c, s = np.ones((T, 64), f32), np.zeros((T, 64), f32)
    ropec = np.ascontiguousarray(c.reshape(NT, 128, 64).transpose(1, 0, 2)).reshape(128, NT * 64)
    ropes = np.ascontiguousarray(s.reshape(NT, 128, 64).transpose(1, 0, 2)).reshape(128, NT * 64)
    tl = np.linspace(0.0, 1.0, L, dtype=f32)
    f = np.linspace(1e-4, 15.0, 16, dtype=f32)
    ang = (f32(2.0 * math.pi / L) * np.arange(L, dtype=f32)[:, None] * f[None, :]).astype(f32)
    z = np.concatenate([tl[:, None], np.cos(ang), -np.sin(ang)], axis=-1).astype(f32)
    zT = np.ascontiguousarray(np.tile(z, (nseq, 1)).T)
    tnorm = np.tile(tl, nseq).reshape(NT, 128).T
    cm = np.ones((2, NT), f32)
    cpl = L // 128
    for n in range(NT):
        if n % cpl == 0 and n != 0:
            cm[0, n] = 0.0
        if n % cpl == cpl - 1 and n != NT - 1:
            cm[1, n] = 0.0
    return masks, ropec, ropes, zT, np.ascontiguousarray(tnorm), cm, 1.0 / nseq


def _cols(a):
    a = np.asarray(a, np.float32)
    lead = int(np.prod(a.shape[:-1])) if a.ndim > 1 else 1
    n = a.shape[-1] // 128
    return np.ascontiguousarray(a.reshape(lead, n, 128).transpose(2, 0, 1)).reshape(128, lead * n)


_PROG = {}


def kernel(x_prompt, x_sample, state_ret, state_lru, c, c_ctx, norm_g, ada_w, ada_b, w_in,
           ret_decay_logit, hy_conv_w, hy_conv_b, hy_ffn_w1, hy_ffn_b1, hy_ffn_w2, hy_ffn_b2,
           hy_ffn_w3, hy_freq, hy_bias, lru_conv_w, lru_conv_b, lru_gate_w, lru_gate_b,
           lru_lambda, w_out, final_g):
    f32 = np.float32
    A = lambda a: np.ascontiguousarray(np.asarray(a, f32))
    x_prompt, x_sample = A(x_prompt), A(x_sample)
    ncores = 8
    if "nc" not in _PROG:
        _PROG["nc"] = build_program()
    nc = _PROG["nc"]
    dft_s = _dft_mats(2048, 1)
    dft_p = _dft_mats(256, 8)
    cc_s = _core_consts(2048, 1, True)
    cc_p = _core_consts(256, 8, False)
    identb = _bf16(np.eye(128))
    idx = np.arange(128, dtype=f32)
    diff = idx[None, :] - idx[:, None]
    s = f32(128.0 ** -0.5)
    hcw = np.asarray(hy_conv_w, f32)
    deltas = np.abs(np.linspace(math.log(1e-2) / 1.5, math.log(1e-2) / 0.3, 512, dtype=f32)).astype(f32)

    def pcol_for(cv, s0l):
        pc_ = np.zeros((128, NPC), f32)

        def put(name, arr):
            arr = np.asarray(arr, f32)
            pc_[:arr.shape[0], PC[name]:PC[name] + arr.shape[1]] = arr
        put("cvec", _cols(cv))
        put("normg", _cols(norm_g))
        put("ada_b", _cols(ada_b))
        put("final_g", _cols(final_g))
        put("hy_cw", _cols(hcw))
        put("hy_cb", _cols(hy_conv_b))
        put("hy_bias", _cols(hy_bias))
        put("lru_cw", _cols(lru_conv_w))
        put("lru_cb", _cols(lru_conv_b))
        put("lru_gb", _cols(lru_gate_b))
        put("lru_lam", _cols(lru_lambda))
        put("s0lru", _cols(s0l))
        put("hy_b1", np.asarray(hy_ffn_b1, f32).T)
        put("hy_b2", np.asarray(hy_ffn_b2, f32).T)
        put("hy_fr", np.asarray(hy_freq, f32).reshape(4, 64).T)
        put("col127", (127.0 - idx)[:, None])
        put("colp", idx[:, None])
        return pc_

    def ctab_for(cm, nw, bm):
        ct = np.zeros((128, NCT), f32)
        ct[:, CT["pos"]:CT["pos"] + 128] = np.maximum(diff, 0)
        ct[:, CT["neg"]:CT["neg"] + 128] = np.maximum(-diff, 0)
        ct[:, CT["eyes"]:CT["eyes"] + 128] = np.eye(128, dtype=f32) * s
        ct[:, CT["iota1"]:CT["iota1"] + 128] = (idx + 1)[None, :]
        ct[:, CT["rev"]:CT["rev"] + 128] = (128 - idx)[None, :]
        ct[:, CT["delta"]:CT["delta"] + 512] = deltas[None, :]
        ct[:, CT["logit"]:CT["logit"] + 16] = np.asarray(ret_decay_logit, f32).reshape(1, 16)
        ct[:, CT["cmask"]:CT["cmask"] + 32] = cm.reshape(1, 32)
        ct[:, CT["normw"]:CT["normw"] + 128] = nw
        ct[:, CT["nf8"]:CT["nf8"] + 8] = bm[0][None, :]
        ct[:, CT["nl8"]:CT["nl8"] + 8] = bm[1][None, :]
        return ct

    shared = dict(ada_w=A(ada_w), w_in=A(w_in), w_out=A(w_out), hy_ffn_w1=A(hy_ffn_w1), hy_ffn_w2=A(hy_ffn_w2),
                  hy_ffn_w3=A(hy_ffn_w3), lru_gate_w=A(lru_gate_w), identb=identb)
    in_maps = []
    for core in range(ncores):
        if core < 4:
            xs = x_sample[core]
            cv = np.asarray(c, f32)[core]
            s0r = A(state_ret)[core]
            s0l = np.asarray(state_lru, f32)[core]
            cc, dft = cc_s, dft_s
        else:
            pb = (core - 4) % 2
            xs = x_prompt[pb * 8:(pb + 1) * 8].reshape(T, 1024)
            cv = np.asarray(c_ctx, f32)
            s0r = np.zeros((DEPTH, 2, 4, 128, 128), f32)
            s0l = np.zeros((DEPTH, 2, 512), f32)
            cc, dft = cc_p, dft_p
        masks, ropec, ropes, zT, tnorm, cm, nw = cc
        pc_ = pcol_for(cv, s0l)
        pc_[:, PC["tnorm"]:PC["tnorm"] + 16] = tnorm
        m = dict(shared)
        m.update(xT=np.ascontiguousarray(xs.T), pcol=pc_, ctab=ctab_for(cm, nw, masks), ropec=ropec, ropes=ropes,
                 s0ret=np.ascontiguousarray(s0r), dftF=dft[0], dftI=dft[1], zT=zT)
        in_maps.append(m)
    res = run_bass_kernel_spmd(nc, in_maps, core_ids=list(range(ncores)))
    R = res.results
    y_sample = np.stack([np.ascontiguousarray(R[j]["yT"].T) for j in range(4)]).astype(f32)
    y_prompt = np.concatenate([np.ascontiguousarray(R[4 + pb]["yT"].T).reshape(8, 256, 1024) for pb in range(2)]).astype(f32)
    nsr = np.concatenate([R[4 + pb]["st_ret"].transpose(2, 0, 1, 3, 4, 5) for pb in range(2)]).astype(f32)
    nsl = np.concatenate([R[4 + pb]["st_lru"].transpose(4, 0, 1, 2, 3).reshape(8, DEPTH, 2, 512) for pb in range(2)]).astype(f32)
    return (y_prompt, y_sample, np.ascontiguousarray(nsr), np.ascontiguousarray(nsl))
```
